# Optimizing a Trainium2 kernel written in Bass

```python
import jax, jax.numpy as jnp
from jax import lax
import numpy as np

D_MODEL = 1024
BATCH = 8
SEQ = 2048
DEPTH = 2

N_MIXERS = 2
N_A_LAYERS = (DEPTH + 1) // 2
N_B_LAYERS = DEPTH // 2
EPS = 1e-6

HG_HEADS = 6
HG_KDIM = 128
HG_VDIM = 128
HG_CHUNK = 64
HG_F = HG_HEADS * HG_KDIM
HG_V = HG_HEADS * HG_VDIM

MLA_HEADS = 6
MLA_Q_LORA = 384
MLA_KV_LORA = 256
MLA_NOPE = 128
MLA_ROPE = 64
MLA_QK = MLA_NOPE + MLA_ROPE
MLA_VDIM = 128
ROPE_THETA = 10000.0
Q_BLOCK = 128

MEM_LEN = 256
MEM_HEADS = 4
MEM_HDIM = 64
MEM_W = MEM_HEADS * MEM_HDIM

MIX_W = 768
OUT_W = MIX_W + MEM_W
HG_IN_W = 2 * HG_F + 2 * HG_V + MEM_W
MLA_IN_W = MLA_Q_LORA + MLA_KV_LORA + MLA_ROPE + MEM_W

D_FF = 2816
CONV_W = 3

kernel_name = "hybrid_hgrn2_mla_memxattn_convffn"


def rms_norm(x, g):
    xf = x.astype(jnp.float32)
    y = xf * lax.rsqrt(jnp.mean(xf * xf, axis=-1, keepdims=True) + EPS)
    return (y * g.astype(jnp.float32)).astype(x.dtype)


def rope_tables(positions):
    half = MLA_ROPE // 2
    inv = ROPE_THETA ** (-jnp.arange(half, dtype=jnp.float32) / half)
    ang = positions.astype(jnp.float32)[..., None] * inv
    return jnp.cos(ang)[:, :, None, :], jnp.sin(ang)[:, :, None, :]


def apply_rope(x, cos, sin):
    half = MLA_ROPE // 2
    xf = x.astype(jnp.float32)
    x1, x2 = xf[..., :half], xf[..., half:]
    return jnp.concatenate([x1 * cos - x2 * sin, x1 * sin + x2 * cos], axis=-1).astype(x.dtype)


def hgrn2_mixer(z, lb, onorm_g):
    B, S, _ = z.shape
    C = HG_CHUNK
    N = S // C
    q, f, i, g = jnp.split(z, [HG_F, 2 * HG_F, 2 * HG_F + HG_V], axis=-1)
    q = jax.nn.silu(q.astype(jnp.float32)) * (HG_KDIM ** -0.5)
    fg = lb + (1.0 - lb) * jax.nn.sigmoid(f.astype(jnp.float32))
    k = 1.0 - fg
    logf = jnp.log(fg)
    v = i.astype(jnp.float32)

    def to_chunks(t, d):
        return t.reshape(B, N, C, HG_HEADS, d).transpose(1, 0, 3, 2, 4)

    qc, kc, lc = to_chunks(q, HG_KDIM), to_chunks(k, HG_KDIM), to_chunks(logf, HG_KDIM)
    vc = to_chunks(v, HG_VDIM)
    causal = jnp.tril(jnp.ones((C, C), dtype=bool))[:, :, None]

    def step(state, xs):
        q_, k_, v_, lf = xs
        b = jnp.cumsum(lf, axis=2)
        o_inter = jnp.einsum('bhtk,bhkv->bhtv', q_ * jnp.exp(b), state)
        diff = b[:, :, :, None, :] - b[:, :, None, :, :]
        decay = jnp.exp(jnp.where(causal, diff, -jnp.inf))
        attn = jnp.einsum('bhtk,bhtsk,bhsk->bhts', q_, decay, k_)
        o = o_inter + jnp.einsum('bhts,bhsv->bhtv', attn, v_)
        b_last = b[:, :, -1, :]
        state = state * jnp.exp(b_last)[..., None] + jnp.einsum(
            'bhsk,bhsv->bhkv', k_ * jnp.exp(b_last[:, :, None, :] - b), v_)
        return state, o

    s0 = jnp.zeros((B, HG_HEADS, HG_KDIM, HG_VDIM), jnp.float32)
    _, o = lax.scan(step, s0, (qc, kc, vc, lc))
    o = o.transpose(1, 0, 3, 2, 4).reshape(B, S, HG_HEADS, HG_VDIM)
    gate = g.astype(jnp.float32).reshape(B, S, HG_HEADS, HG_VDIM)
    o = rms_norm(o, onorm_g) * jax.nn.silu(gate)
    return o.reshape(B, S, HG_V).astype(z.dtype)


def causal_block_attention(q, k, v):
    B, S, H, Dq = q.shape
    nb = S // Q_BLOCK
    scale = Dq ** -0.5
    qb = q.reshape(B, nb, Q_BLOCK, H, Dq).transpose(1, 0, 2, 3, 4)
    kpos = jnp.arange(S)

    def one_block(args):
        qi, idx = args
        s = jnp.einsum('bqhd,bkhd->bhqk', qi, k).astype(jnp.float32) * scale
        qpos = idx * Q_BLOCK + jnp.arange(Q_BLOCK)
        s = jnp.where(kpos[None, :] <= qpos[:, None], s, -jnp.inf)
        p = jax.nn.softmax(s, axis=-1)
        return jnp.einsum('bhqk,bkhd->bqhd', p.astype(v.dtype), v)

    out = lax.map(one_block, (qb, jnp.arange(nb)))
    return out.transpose(1, 0, 2, 3, 4).reshape(B, S, H, v.shape[-1])


def mla_mixer(z, cos, sin, qa_g, kva_g, w_uq, w_ukv, qn_g, kn_g):
    B, S, _ = z.shape
    cq, ckv, kpe = jnp.split(z, [MLA_Q_LORA, MLA_Q_LORA + MLA_KV_LORA], axis=-1)
    q = (rms_norm(cq, qa_g) @ w_uq).reshape(B, S, MLA_HEADS, MLA_QK)
    kv = (rms_norm(ckv, kva_g) @ w_ukv).reshape(B, S, MLA_HEADS, MLA_NOPE + MLA_VDIM)
    k_nope, v = kv[..., :MLA_NOPE], kv[..., MLA_NOPE:]
    k_rope = jnp.broadcast_to(kpe[:, :, None, :], (B, S, MLA_HEADS, MLA_ROPE))
    k = jnp.concatenate([k_nope, k_rope], axis=-1)
    q = rms_norm(q, qn_g)
    k = rms_norm(k, kn_g)
    q = jnp.concatenate([q[..., :MLA_NOPE], apply_rope(q[..., MLA_NOPE:], cos, sin)], axis=-1)
    k = jnp.concatenate([k[..., :MLA_NOPE], apply_rope(k[..., MLA_NOPE:], cos, sin)], axis=-1)
    o = causal_block_attention(q, k, v)
    return o.reshape(B, S, MLA_HEADS * MLA_VDIM)


def memory_attention(qm, mem_n, w_mem_kv, qn_g, kn_g):
    B, S, _ = qm.shape
    M = mem_n.shape[1]
    kv = mem_n @ w_mem_kv
    q = rms_norm(qm.reshape(B, S, MEM_HEADS, MEM_HDIM), qn_g)
    k = rms_norm(kv[..., :MEM_W].reshape(B, M, MEM_HEADS, MEM_HDIM), kn_g)
    v = kv[..., MEM_W:].reshape(B, M, MEM_HEADS, MEM_HDIM)
    s = jnp.einsum('bshd,bmhd->bhsm', q, k).astype(jnp.float32) * (MEM_HDIM ** -0.5)
    p = jax.nn.softmax(s, axis=-1)
    o = jnp.einsum('bhsm,bmhd->bshd', p.astype(v.dtype), v)
    return o.reshape(B, S, MEM_W)


def conv_ffn(h, w_up, conv_w, conv_b, w_down):
    u = h @ w_up
    u = lax.conv_general_dilated(
        u, conv_w[:, None, :].astype(u.dtype), window_strides=(1,),
        padding=[(CONV_W - 1, 0)], dimension_numbers=('NWC', 'WIO', 'NWC'),
        feature_group_count=u.shape[-1]) + conv_b
    a, b = u[..., :D_FF], u[..., D_FF:]
    return (jax.nn.silu(a) * b) @ w_down


def setup_inputs(seed: int = 0) -> dict:
    key = jax.random.key(seed)
    ks = jax.random.split(key, 32)
    nrm = lambda k, shape, scale: jax.random.normal(k, shape, jnp.float32) * scale
    gain = lambda k, shape: 1.0 + 0.02 * jax.random.normal(k, shape, jnp.float32)
    offs = jax.random.randint(ks[2], (BATCH, 1), 0, 1024, dtype=jnp.int32)
    positions = offs + jnp.arange(SEQ, dtype=jnp.int32)[None, :]
    return {
        "x": nrm(ks[0], (BATCH, SEQ, D_MODEL), 1.0),
        "mem": nrm(ks[1], (BATCH, MEM_LEN, D_MODEL), 1.0),
        "positions": positions,
        "mix_norm_g": gain(ks[3], (DEPTH, D_MODEL)),
        "ffn_norm_g": gain(ks[4], (DEPTH, D_MODEL)),
        "mem_norm_g": gain(ks[5], (DEPTH, D_MODEL)),
        "w_mem_kv": nrm(ks[6], (DEPTH, D_MODEL, 2 * MEM_W), D_MODEL ** -0.5),
        "mem_q_norm_g": gain(ks[7], (DEPTH, MEM_HDIM)),
        "mem_k_norm_g": gain(ks[8], (DEPTH, MEM_HDIM)),
        "w_out": nrm(ks[9], (DEPTH, OUT_W, D_MODEL), OUT_W ** -0.5),
        "w_up": nrm(ks[10], (DEPTH, D_MODEL, 2 * D_FF), D_MODEL ** -0.5),
        "conv_w": nrm(ks[11], (DEPTH, CONV_W, 2 * D_FF), CONV_W ** -0.5),
        "conv_b": nrm(ks[12], (DEPTH, 2 * D_FF), 0.02),
        "w_down": nrm(ks[13], (DEPTH, D_FF, D_MODEL), D_FF ** -0.5),
        "hg_w_in": nrm(ks[14], (N_A_LAYERS, D_MODEL, HG_IN_W), D_MODEL ** -0.5),
        "hg_lb_logits": nrm(ks[15], (DEPTH + 1, HG_F), 0.5),
        "hg_out_norm_g": gain(ks[16], (N_A_LAYERS, HG_VDIM)),
        "mla_w_in": nrm(ks[17], (N_B_LAYERS, D_MODEL, MLA_IN_W), D_MODEL ** -0.5),
        "mla_qa_norm_g": gain(ks[18], (N_B_LAYERS, MLA_Q_LORA)),
        "mla_kva_norm_g": gain(ks[19], (N_B_LAYERS, MLA_KV_LORA)),
        "mla_w_uq": nrm(ks[20], (N_B_LAYERS, MLA_Q_LORA, MLA_HEADS * MLA_QK), MLA_Q_LORA ** -0.5),
        "mla_w_ukv": nrm(ks[21], (N_B_LAYERS, MLA_KV_LORA, MLA_HEADS * (MLA_NOPE + MLA_VDIM)), MLA_KV_LORA ** -0.5),
        "mla_q_norm_g": gain(ks[22], (N_B_LAYERS, MLA_QK)),
        "mla_k_norm_g": gain(ks[23], (N_B_LAYERS, MLA_QK)),
    }


def reference(x, mem, positions, mix_norm_g, ffn_norm_g, mem_norm_g, w_mem_kv,
              mem_q_norm_g, mem_k_norm_g, w_out, w_up, conv_w, conv_b, w_down,
              hg_w_in, hg_lb_logits, hg_out_norm_g, mla_w_in, mla_qa_norm_g,
              mla_kva_norm_g, mla_w_uq, mla_w_ukv, mla_q_norm_g, mla_k_norm_g):
    cos, sin = rope_tables(positions)
    lower_bounds = jnp.cumsum(jax.nn.softmax(hg_lb_logits.astype(jnp.float32), axis=0), axis=0)
    for layer in range(DEPTH):
        h = rms_norm(x, mix_norm_g[layer])
        j = layer // N_MIXERS
        if layer % N_MIXERS == 0:
            z = h @ hg_w_in[j]
            y = hgrn2_mixer(z[..., :-MEM_W], lower_bounds[layer], hg_out_norm_g[j])
        else:
            z = h @ mla_w_in[j]
            y = mla_mixer(z[..., :-MEM_W], cos, sin, mla_qa_norm_g[j], mla_kva_norm_g[j],
                          mla_w_uq[j], mla_w_ukv[j], mla_q_norm_g[j], mla_k_norm_g[j])
        m = memory_attention(z[..., -MEM_W:], rms_norm(mem, mem_norm_g[layer]), w_mem_kv[layer],
                             mem_q_norm_g[layer], mem_k_norm_g[layer])
        x = x + jnp.concatenate([y, m], axis=-1) @ w_out[layer]
        x = x + conv_ffn(rms_norm(x, ffn_norm_g[layer]), w_up[layer], conv_w[layer],
                         conv_b[layer], w_down[layer])
    return x
```

```python
import contextlib
import numpy as np
import concourse.bass as bass
import concourse.mybir as mybir
from concourse.bass_utils import run_bass_kernel_spmd

F32 = mybir.dt.float32
BF16 = mybir.dt.bfloat16
I32 = mybir.dt.int32
AF = mybir.ActivationFunctionType
ALU = mybir.AluOpType

D = 1024
KC = 8
MEM = 256
DFF = 2816
NPAIR = 22
EPS = 1e-6
ENGS = ("pe", "act", "dve", "pool", "sp")
NDSEM = 24


class Op:
    __slots__ = ("eng", "idx", "fn", "waits", "dwaits", "need_inc", "incval", "is_dma", "dma_id", "known", "dknown")

    def __init__(self, eng, idx, fn, is_dma):
        self.eng = eng
        self.idx = idx
        self.fn = fn
        self.waits = {}
        self.dwaits = []
        self.need_inc = False
        self.incval = 0
        self.is_dma = is_dma
        self.dma_id = -1
        self.known = None
        self.dknown = None


class Sched:
    def __init__(self):
        self.ops = {e: [] for e in ENGS}
        self.known = {e: {f: -1 for f in ENGS} for e in ENGS}
        self.dknown = {e: set() for e in ENGS}
        self.regs = {}
        self.ndma = 0
        self.dma_ops = []
        self.pending = {e: [] for e in ENGS}

    @staticmethod
    def _conf(k1, k2):
        n = min(len(k1), len(k2))
        return k1[:n] == k2[:n]

    def add(self, eng, fn, reads=(), writes=(), dma=False):
        op = Op(eng, len(self.ops[eng]), fn, dma)
        reads = [k if isinstance(k, tuple) else (k,) for k in reads]
        writes = [k if isinstance(k, tuple) else (k,) for k in writes]
        deps = []
        odeps = []
        for k in reads:
            root = self.regs.setdefault(k[0], {})
            for kk, ent in root.items():
                if self._conf(kk, k):
                    deps.extend(ent[0])
        for k in writes:
            root = self.regs.setdefault(k[0], {})
            for kk, ent in root.items():
                if self._conf(kk, k):
                    odeps.extend(ent[0])
                    odeps.extend(ent[1])
        deps.extend(odeps)
        pend = self.pending[eng]
        if pend:
            deps.extend(pend)
            self.pending[eng] = []
        for k in reads:
            ent = self.regs[k[0]].setdefault(k, [[], []])
            ent[1].append(op)
        for k in writes:
            root = self.regs[k[0]]
            for kk in list(root.keys()):
                if kk != k and self._conf(kk, k):
                    if len(kk) >= len(k):
                        del root[kk]
                    else:
                        root[kk][0].append(op)
            root[k] = [[op], []]
        kn = self.known[eng]
        dk = self.dknown[eng]
        for d in deps:
            if d is op:
                continue
            if d.is_dma:
                if d.dma_id in dk:
                    continue
                if d not in op.dwaits:
                    op.dwaits.append(d)
            else:
                if d.eng == eng and eng in ("pe", "sp"):
                    continue
                if kn[d.eng] >= d.idx:
                    continue
                cur = op.waits.get(d.eng)
                if cur is None or cur.idx < d.idx:
                    op.waits[d.eng] = d
        if dma:
            op.dma_id = self.ndma
            self.ndma += 1
            self.dma_ops.append(op)
            if op.dma_id >= NDSEM:
                prev = self.dma_ops[op.dma_id - NDSEM]
                if prev.dma_id not in dk and prev not in op.dwaits:
                    op.dwaits.append(prev)
            op.need_inc = True
        for d in op.waits.values():
            d.need_inc = True
            if kn[d.eng] < d.idx:
                kn[d.eng] = d.idx
            for f, v in d.known.items():
                if f != eng and kn[f] < v:
                    kn[f] = v
            dk |= d.dknown
        for d in op.dwaits:
            dk.add(d.dma_id)
            for f, v in d.known.items():
                if f != eng and kn[f] < v:
                    kn[f] = v
            dk |= d.dknown
        op.known = dict(kn)
        op.dknown = set(dk)
        self.ops[eng].append(op)
        return op

    def barrier(self):
        lasts = []
        for f in ENGS:
            if f == "sp":
                continue
            if self.ops[f]:
                lasts.append(self.ops[f][-1])
        for e in ENGS:
            self.pending[e] = [o for o in lasts if o.eng != e] + list(self.dma_ops)

    def emit(self, nc, final_wait_ops=()):
        for e in ENGS:
            c = 0
            for op in self.ops[e]:
                if op.is_dma:
                    op.incval = 16 * (op.dma_id // NDSEM + 1)
                elif op.need_inc:
                    c += 1
                    op.incval = c
        with contextlib.ExitStack() as st:
            sems = {e: st.enter_context(nc.semaphore("s_" + e)) for e in ENGS}
            dsems = [st.enter_context(nc.semaphore("d%d" % i)) for i in range(NDSEM)]
            block = st.enter_context(nc.Block())

            def run(e, h):
                for op in self.ops[e]:
                    for d in op.waits.values():
                        h.wait_ge(sems[d.eng], d.incval)
                    for d in op.dwaits:
                        h.wait_ge(dsems[d.dma_id % NDSEM], d.incval)
                    ins = op.fn(h)
                    if op.is_dma:
                        ins.then_inc(dsems[op.dma_id % NDSEM], 16)
                    elif op.need_inc:
                        ins.then_inc(sems[e], 1)
                if e == "sp":
                    for d in final_wait_ops:
                        h.wait_ge(dsems[d.dma_id % NDSEM], d.incval)

            @block.tensor
            def _(h):
                run("pe", h)

            @block.scalar
            def _(h):
                run("act", h)

            @block.vector
            def _(h):
                run("dve", h)

            @block.gpsimd
            def _(h):
                run("pool", h)

            @block.sync
            def _(h):
                run("sp", h)


def _cols(vec, n=128):
    v = np.asarray(vec, np.float32).reshape(-1, n)
    return np.ascontiguousarray(v.T)


def _wpack(w, kc):
    n = w.shape[1]
    return np.ascontiguousarray(w.reshape(kc, 128, n).transpose(1, 0, 2).reshape(128, kc * n))


class PCol:
    def __init__(self):
        self.cols = []
        self.idx = {}

    def put(self, name, arr):
        arr = np.asarray(arr, np.float32)
        if arr.ndim == 1:
            arr = arr[:, None]
        assert arr.shape[0] == 128, (name, arr.shape)
        self.idx[name] = (sum(c.shape[1] for c in self.cols), arr.shape[1])
        self.cols.append(arr)

    def array(self):
        return np.ascontiguousarray(np.concatenate(self.cols, axis=1))


def pack_shared(inp):
    pc = PCol()
    f = lambda a: np.asarray(a, np.float32)
    for l in range(2):
        pc.put("mixg%d" % l, _cols(f(inp["mix_norm_g"])[l]))
        pc.put("ffng%d" % l, _cols(f(inp["ffn_norm_g"])[l]))
        pc.put("memg%d" % l, _cols(f(inp["mem_norm_g"])[l]))
        pc.put("mqg%d" % l, np.tile(f(inp["mem_q_norm_g"])[l], 2))
        pc.put("mkg%d" % l, np.tile(f(inp["mem_k_norm_g"])[l], 2))
        cw = f(inp["conv_w"])[l]
        for j in range(3):
            pc.put("cw%d_%d" % (j, l), _cols(cw[j]))
        pc.put("cb%d" % l, _cols(f(inp["conv_b"])[l]))
    lbl = f(inp["hg_lb_logits"])
    for j in range(3):
        pc.put("lbl%d" % j, _cols(lbl[j]))
    pc.put("hgo", f(inp["hg_out_norm_g"])[0])
    pc.put("qag", _cols(f(inp["mla_qa_norm_g"])[0]))
    pc.put("kvag", _cols(f(inp["mla_kva_norm_g"])[0]))
    qg = f(inp["mla_q_norm_g"])[0]
    kg = f(inp["mla_k_norm_g"])[0]
    sw = np.concatenate([np.arange(160, 192), np.arange(128, 160)])
    pc.put("qg_n", qg[:128])
    pc.put("qg_r", np.tile(qg[128:192], 2))
    pc.put("qg_rs", np.tile(qg[sw], 2))
    pc.put("kg_n", kg[:128])
    pc.put("kg_r", np.tile(kg[128:192], 2))
    pc.put("kg_rs", np.tile(kg[sw], 2))
    inv = (10000.0 ** (-np.arange(32, dtype=np.float32) / 32)).astype(np.float32)
    pc.put("invf", np.tile(inv, 4))
    pc.put("sgn", np.tile(np.concatenate([-np.ones(32), np.ones(32)]), 2))
    pc.put("eps", np.full(128, EPS))
    pc.put("halfpi", np.full(128, np.pi / 2))

    out = {"pcol": pc.array()}
    w = f(inp["hg_w_in"])[0]
    units = []
    for h in range(6):
        cs = np.concatenate([np.arange(h * 128, h * 128 + 128), 768 + np.arange(h * 128, h * 128 + 128),
                             2304 + np.arange(h * 128, h * 128 + 128), 1536 + np.arange(h * 128, h * 128 + 128)])
        units.append(_wpack(w[:, cs], 8))
    out["w0"] = np.stack(units)
    out["w0m"] = _wpack(w[:, 3072:3328], 8)
    out["wmem"] = np.stack([_wpack(f(inp["w_mem_kv"])[l], 8) for l in range(2)])
    out["wout"] = np.stack([_wpack(f(inp["w_out"])[l], 8) for l in range(2)])
    wup = f(inp["w_up"])
    wdn = f(inp["w_down"])
    ups, dns = [], []
    for l in range(2):
        for i in range(NPAIR):
            cs = np.concatenate([np.arange(i * 128, i * 128 + 128), DFF + np.arange(i * 128, i * 128 + 128)])
            ups.append(_wpack(wup[l][:, cs], 8))
            dns.append(np.ascontiguousarray(wdn[l][i * 128:(i + 1) * 128, :]))
    out["wup"] = np.stack(ups).reshape(2, NPAIR, 128, 2048)
    out["wdn"] = np.stack(dns).reshape(2, NPAIR, 128, 1024)
    w1 = f(inp["mla_w_in"])[0]
    swk = np.concatenate([np.arange(32, 64), np.arange(0, 32)])
    tiles = [w1[:, 0:128], w1[:, 128:256], w1[:, 256:384], w1[:, 384:512], w1[:, 512:640],
             np.concatenate([w1[:, 640:704], w1[:, 640:704]], axis=1),
             np.concatenate([w1[:, 640 + swk], w1[:, 640 + swk]], axis=1),
             w1[:, 704:832], w1[:, 832:960]]
    out["w1in"] = np.stack([_wpack(t, 8) for t in tiles])
    wuq = f(inp["mla_w_uq"])[0]
    wukv = f(inp["mla_w_ukv"])[0]
    uqs, ukvs = [], []
    for h in range(6):
        b = h * 192
        cs = np.concatenate([np.arange(b, b + 128), np.arange(b + 128, b + 192), b + 128 + swk])
        uqs.append(_wpack(wuq[:, cs], 3))
        ukvs.append(_wpack(wukv[:, h * 256:(h + 1) * 256], 2))
    out["wuq"] = np.stack(uqs)
    out["wukv"] = np.stack(ukvs)
    return out, pc.idx


def build(S, pidx, ncol, nlayers=2, dbg=()):
    nc = bass.Bass("TRN2", target_bir_lowering=False)
    TT = 512
    NT = S // TT
    NB = S // 128
    NCH = S // 64
    sch = Sched()
    A = sch.add

    def dram(name, shape, dt=F32, kind="ExternalInput"):
        return nc.dram_tensor(name, list(shape), dt, kind=kind).ap()

    xT_d = dram("xT", [D, S])
    memT_d = dram("memT", [D, MEM])
    pos_d = dram("pos", [1, S], I32)
    pcol_d = dram("pcol", [128, ncol])
    w0_d = dram("w0", [6, 128, 4096])
    w0m_d = dram("w0m", [128, 2048])
    wmem_d = dram("wmem", [2, 128, 4096])
    wout_d = dram("wout", [2, 128, 8192])
    wup_d = dram("wup", [2, NPAIR, 128, 2048])
    wdn_d = dram("wdn", [2, NPAIR, 128, 1024])
    w1in_d = dram("w1in", [9, 128, 1024])
    wuq_d = dram("wuq", [6, 128, 768])
    wukv_d = dram("wukv", [6, 128, 512])
    out_d = dram("outT", [D, S], F32, "ExternalOutput")
    DBG = {}

    BASE = 16640
    LIMIT = 229376
    cur = [BASE]

    def sb(name, shape, dt, off=None):
        nbytes = int(np.prod(shape[1:])) * (4 if dt in (F32, I32) else 2)
        if off is None:
            off = cur[0]
            cur[0] += (nbytes + 63) // 64 * 64
            assert cur[0] <= LIMIT, (name, cur[0])
        assert off + nbytes <= LIMIT, (name, off, nbytes)
        return nc.alloc_sbuf_tensor_at(name, list(shape), dt, offset=off)

    xT = sb("xTs", [128, KC, S], F32)
    hT = sb("hTs", [128, KC, S], BF16)
    yT = sb("yTs", [128, KC, S], BF16)
    pcol = sb("pcols", [128, ncol], F32)
    ones_bf = sb("ones", [128, 128], BF16)
    ident_bf = sb("ident", [128, 128], BF16)
    tri_bf = sb("tri", [128, 128], BF16)
    tri64 = sb("tri64", [128, 64], BF16)
    blk_bf = sb("blk", [128, 128], BF16)
    kn_m = sb("kn_m", [128, 2, MEM], BF16)
    vm_m = sb("vm_m", [128, 2, 256], BF16)
    lbc = sb("lbc", [128, 12], F32)
    F0 = cur[0]
    FSZ = LIMIT - F0

    def fsb(name, shape, dt, off):
        return sb(name, shape, dt, F0 + off)

    ps = [nc.alloc_psum_tensor("ps%d" % i, [128, 512], F32) for i in range(8)]
    rr = [0]

    def bank():
        b = rr[0]
        rr[0] = (rr[0] + 1) % 4
        return b

    rr8 = [0]

    def bank8():
        b = rr8[0]
        rr8[0] = (rr8[0] + 1) % 8
        return b

    def pc(name, j=0, rows=slice(0, 128)):
        o, n = pidx[name]
        return pcol[rows, o + j:o + j + 1]

    def _ap(x):
        if x is None or isinstance(x, (int, float)) or isinstance(x, bass.AP):
            return x
        return x[:]

    def MM(out, lhsT, rhs, start, stop, r, w):
        out, lhsT, rhs = _ap(out), _ap(lhsT), _ap(rhs)
        return A("pe", lambda h: h.matmul(out, lhsT=lhsT, rhs=rhs, start=start, stop=stop), r, w)

    def ACT(out, in_, func, r, w, bias=None, scale=None):
        kw = {}
        out, in_ = _ap(out), _ap(in_)
        if bias is not None:
            kw["bias"] = _ap(bias)
        if scale is not None:
            kw["scale"] = _ap(scale)
        return A("act", lambda h: h.activation(out=out, in_=in_, func=func, **kw), r, w)

    def TTOP(eng, out, in0, in1, op, r, w):
        out, in0, in1 = _ap(out), _ap(in0), _ap(in1)
        return A(eng, lambda h: h.tensor_tensor(out=out, in0=in0, in1=in1, op=op), r, w)

    def TS(eng, out, in0, s1, s2, op0, op1, r, w):
        out, in0, s1, s2 = _ap(out), _ap(in0), _ap(s1), _ap(s2)
        return A(eng, lambda h: h.tensor_scalar(out=out, in0=in0, scalar1=s1, scalar2=s2, op0=op0, op1=op1), r, w)

    def STT(out, in0, scalar, in1, op0, op1, r, w):
        out, in0, scalar, in1 = _ap(out), _ap(in0), _ap(scalar), _ap(in1)
        return A("dve", lambda h: h.scalar_tensor_tensor(out=out, in0=in0, scalar=scalar, in1=in1, op0=op0, op1=op1), r, w)

    def COPY(eng, out, in_, r, w):
        out, in_ = _ap(out), _ap(in_)
        if eng == "act":
            return A("act", lambda h: h.copy(out=out, in_=in_), r, w)
        return A(eng, lambda h: h.tensor_copy(out=out, in_=in_), r, w)

    def DMA(out, in_, r, w):
        out, in_ = _ap(out), _ap(in_)
        return A("sp", lambda h: h.dma_start(out=out, in_=in_), r, w, dma=True)

    def MEMSET(ap, val, w):
        return A("pool", lambda h: h.memset(ap, val), (), w)

    NSTG = 2
    stg_i = [0]

    def load_w(dram_ap, n, dst_flat, wkey, stg):
        p = 0
        while p < n:
            m = min(2048, n - p)
            s = stg_i[0] % NSTG
            stg_i[0] += 1
            DMA(stg[s][:, 0:m], dram_ap[:, p:p + m], (), [("stg", s)])
            COPY("pool", dst_flat[:, p:p + m], stg[s][:, 0:m], [("stg", s)], [wkey])
            p += m

    DMA(pcol[:, :], pcol_d[:, :], (), ["pcol"])
    MEMSET(ones_bf[:, :], 1.0, ["ones"])
    A("pool", lambda h: h.affine_select(out=ident_bf[:, :], in_=ones_bf[:, :], pattern=[[-1, 128]], compare_op=ALU.is_equal,
                                        fill=0.0, base=0, channel_multiplier=1), ["ones"], ["ident"])
    A("pool", lambda h: h.affine_select(out=tri_bf[:, :], in_=ones_bf[:, :], pattern=[[1, 128]], compare_op=ALU.is_ge,
                                        fill=0.0, base=0, channel_multiplier=-1), ["ones"], ["tri"])
    for hb in (0, 64):
        A("pool", lambda h, hb=hb: h.affine_select(out=tri64[hb:hb + 64, :], in_=ones_bf[hb:hb + 64, 0:64], pattern=[[1, 64]],
                                                   compare_op=ALU.is_ge, fill=0.0, base=0, channel_multiplier=-1), ["ones"], [("tri64", hb)])
    MEMSET(blk_bf[:, :], 0.0, ["blk"])
    MEMSET(blk_bf[0:64, 0:64], 1.0, ["blk"])
    MEMSET(blk_bf[64:128, 64:128], 1.0, ["blk"])

    def dump(name, ap, key, shape, dt=F32):
        d = dram("dbg_" + name, shape, dt, "ExternalOutput")
        return DMA(d, ap, [key], [("dbgout", name)])

    def rstd_from_ss(ssbank, width, inv_n, tmp, rstd, keyt, keyr, rows=slice(0, 128)):
        ACT(tmp, ps[ssbank][rows, 0:width], AF.Ln, [("ps", ssbank), "pcol"], [keyt], bias=pc("eps", 0, rows), scale=inv_n)
        ACT(rstd, tmp, AF.Exp, [keyt], [keyr], scale=-0.5)

    def norm_tile(tt, src, skey, gname, dst, dkey, sq, tmp, rstd):
        c = slice(tt * TT, (tt + 1) * TT)
        b = bank()
        for kc in range(KC):
            s_ = kc % 2
            ACT(sq[s_], src[:, kc, c], AF.Square, [(skey, kc, tt)], [("sq", id(sq), s_)])
            MM(ps[b][:, :], ones_bf[:, :], sq[s_], kc == 0, kc == KC - 1, [("sq", id(sq), s_), "ones"], [("ps", b)])
        rstd_from_ss(b, TT, 1.0 / D, tmp, rstd, ("ntmp", id(tmp)), ("nrstd", id(rstd)))
        for kc in range(KC):
            STT(dst[:, kc, c], src[:, kc, c], pc(gname, kc), rstd, ALU.mult, ALU.mult,
                [(skey, kc, tt), ("nrstd", id(rstd)), "pcol"], [(dkey, kc, tt)])

    def norm_full(src, skey, gname, dst, dkey, sq, tmp, rstd):
        for tt in range(NT):
            norm_tile(tt, src, skey, gname, dst, dkey, sq, tmp, rstd)

    def mem_prep(l):
        o = 0
        memT = fsb("memT", [128, KC, MEM], F32, o); o += KC * MEM * 4
        memn = fsb("memn", [128, KC, MEM], BF16, o); o += KC * MEM * 2
        wm = fsb("wm", [128, KC, 512], BF16, o); o += KC * 512 * 2
        stg = [fsb("mstg%d" % i, [128, 2048], F32, o + i * 8192) for i in range(NSTG)]; o += NSTG * 8192
        sq = [fsb("msq%d" % i, [128, MEM], BF16, o + i * 512) for i in range(2)]; o += 1024
        tmp = fsb("mtmp", [128, MEM], F32, o); o += 1024
        rstd = fsb("mrstd", [128, MEM], F32, o); o += 1024
        kraw = fsb("mkraw", [128, MEM], F32, o); o += 1024
        for kc in range(KC):
            DMA(memT[:, kc, :], memT_d[kc * 128:(kc + 1) * 128, :], (), [("memT", kc)])
        load_w(wmem_d[l], 4096, wm[:, :, :].rearrange("p k n -> p (k n)"), "wm", stg)
        b = bank()
        for kc in range(KC):
            s = kc % 2
            ACT(sq[s], memT[:, kc, :], AF.Square, [("memT", kc)], [("msq", s)])
            MM(ps[b][:, 0:MEM], ones_bf[:, :], sq[s], kc == 0, kc == KC - 1, [("msq", s), "ones"], [("ps", b)])
        rstd_from_ss(b, MEM, 1.0 / D, tmp, rstd, "mtmp", "mrstd")
        for kc in range(KC):
            STT(memn[:, kc, :], memT[:, kc, :], pc("memg%d" % l, kc), rstd, ALU.mult, ALU.mult,
                [("memT", kc), "mrstd", "pcol"], [("memn", kc)])
        for j in range(2):
            b = bank()
            for kc in range(KC):
                MM(ps[b][:, 0:MEM], wm[:, kc, j * 128:(j + 1) * 128], memn[:, kc, :], kc == 0, kc == KC - 1,
                   ["wm", ("memn", kc)], [("ps", b)])
            COPY("act", kraw, ps[b][:, 0:MEM], [("ps", b)], ["mkraw"])
            ACT(sq[0], ps[b][:, 0:MEM], AF.Square, [("ps", b)], [("msq", 0)])
            b2 = bank()
            MM(ps[b2][:, 0:MEM], blk_bf[:, :], sq[0], True, True, [("msq", 0), "blk"], [("ps", b2)])
            rstd_from_ss(b2, MEM, 1.0 / 64, tmp, rstd, "mtmp", "mrstd")
            STT(kn_m[:, j, :], kraw, pc("mkg%d" % l), rstd, ALU.mult, ALU.mult, ["mkraw", "mrstd", "pcol"], [("kn_m", j)])
        for mt in range(2):
            b = bank()
            for kc in range(KC):
                MM(ps[b][:, 0:256], memn[:, kc, mt * 128:(mt + 1) * 128], wm[:, kc, 256:512], kc == 0, kc == KC - 1,
                   ["wm", ("memn", kc)], [("ps", b)])
            COPY("act", vm_m[:, mt, :], ps[b][:, 0:256], [("ps", b)], [("vm_m", mt)])

    def mem_attn(l, qw, qwkey, o):
        qraw = fsb("qraw%d" % l, [128, TT], F32, o); o += TT * 4
        sq = fsb("qsq%d" % l, [128, TT], BF16, o); o += TT * 2
        tmp = fsb("qtmp%d" % l, [128, TT], F32, o); o += TT * 4
        rstd = fsb("qrstd%d" % l, [128, TT], F32, o); o += TT * 4
        qn = [fsb("qn%d_%d" % (l, i), [128, TT], BF16, o + i * TT * 2) for i in range(2)]; o += 2 * TT * 2
        pt = [fsb("pt%d_%d" % (l, i), [128, TT], BF16, o + i * TT * 2) for i in range(4)]; o += 4 * TT * 2
        rec2 = [fsb("rec%d_%d" % (l, i), [128, TT], F32, o + i * TT * 4) for i in range(2)]; o += 2 * TT * 4
        pti = [0]
        pend_pv = []

        def pv_unit(tt, j, e, pts):
            c = slice(tt * TT, (tt + 1) * TT)
            hh = 2 * j + e
            pb = 64 * e
            bo = 4 + 2 * ((tt * 2 + j) % 2)
            bd = bo + 1
            for mt in range(2):
                MM(ps[bo][pb:pb + 64, :], vm_m[:, mt, hh * 64:(hh + 1) * 64], pt[pts[mt]], mt == 0, mt == 1,
                   [("vm_m", mt), ("pt", pts[mt])], [("ps", bo, e)])
            for mt in range(2):
                MM(ps[bd][pb:pb + 64, :], ones_bf[:, 0:64], pt[pts[mt]], mt == 0, mt == 1,
                   ["ones", ("pt", pts[mt])], [("ps", bd, e)])
            rk = ("rec", (tt * 2 + j) % 2, e)
            rc = rec2[(tt * 2 + j) % 2]
            A("dve", lambda h, bd=bd, pb=pb, rc=rc: h.reciprocal(out=rc[pb:pb + 64, :], in_=ps[bd][pb:pb + 64, :]),
              [("ps", bd, e)], [rk])
            TTOP("dve", yT[pb:pb + 64, 6 + j, c], ps[bo][pb:pb + 64, :], rc[pb:pb + 64, :], ALU.mult,
                 [("ps", bo, e), rk], [("yT", 6 + j, tt, e)])

        for tt in range(NT):
            c = slice(tt * TT, (tt + 1) * TT)
            for j in range(2):
                b = bank()
                for kc in range(KC):
                    MM(ps[b][:, :], qw[:, kc, j * 128:(j + 1) * 128], hT[:, kc, c], kc == 0, kc == KC - 1,
                       [qwkey, ("hT", kc, tt)], [("ps", b)])
                COPY("act", qraw, ps[b][:, :], [("ps", b)], ["qraw"])
                ACT(sq, ps[b][:, :], AF.Square, [("ps", b)], ["qsq"])
                b2 = bank()
                MM(ps[b2][:, :], blk_bf[:, :], sq, True, True, ["qsq", "blk"], [("ps", b2)])
                rstd_from_ss(b2, TT, 1.0 / 64, tmp, rstd, "qtmp", "qrstd")
                STT(qn[j], qraw, pc("mqg%d" % l), rstd, ALU.mult, ALU.mult, ["qraw", "qrstd", "pcol"], [("qn", j)])
                for e in range(2):
                    hh = 2 * j + e
                    pb = 64 * e
                    pts = []
                    for mt in range(2):
                        b = bank()
                        MM(ps[b][:, :], kn_m[pb:pb + 64, j, mt * 128:(mt + 1) * 128], qn[j][pb:pb + 64, :], True, True,
                           [("kn_m", j), ("qn", j)], [("ps", b)])
                        pi = pti[0] % 4
                        pti[0] += 1
                        ACT(pt[pi], ps[b][:, :], AF.Exp, [("ps", b)], [("pt", pi)], scale=0.125)
                        pts.append(pi)
                    pend_pv.append((tt, j, e, pts))
                    if len(pend_pv) > 1:
                        pv_unit(*pend_pv.pop(0))
        while pend_pv:
            pv_unit(*pend_pv.pop(0))
        return o

    def out_proj(l, o, tile_cb=None):
        wo = fsb("wo%d" % l, [128, KC, D], BF16, o); o += KC * D * 2
        stg = [fsb("ostg%d_%d" % (l, i), [128, 2048], F32, o + i * 8192) for i in range(NSTG)]; o += NSTG * 8192
        load_w(wout_d[l], KC * D, wo[:, :, :].rearrange("p k n -> p (k n)"), "wo", stg)
        for tt in range(NT):
            if tile_cb is not None and tt > 0:
                tile_cb(tt - 1)
            c = slice(tt * TT, (tt + 1) * TT)
            for d in range(KC):
                b = bank()
                for kc in range(KC):
                    MM(ps[b][:, :], wo[:, kc, d * 128:(d + 1) * 128], yT[:, kc, c], kc == 0, kc == KC - 1,
                       ["wo", ("yT", kc, tt)], [("ps", b)])
                TTOP("dve", xT[:, d, c], ps[b][:, :], xT[:, d, c], ALU.add, [("ps", b), ("xT", d, tt)], [("xT", d, tt)])
        if tile_cb is not None:
            tile_cb(NT - 1)
        return o

    def ffn_norm_cb(l, o):
        sq = [fsb("fsq%d_%d" % (l, i), [128, TT], BF16, o + i * TT * 2) for i in range(2)]; o += 2 * TT * 2
        tmp = fsb("ftmp%d" % l, [128, TT], F32, o); o += TT * 4
        rstd = fsb("frstd%d" % l, [128, TT], F32, o); o += TT * 4
        assert o <= FSZ, (o, FSZ)
        return lambda tt: norm_tile(tt, xT, "xT", "ffng%d" % l, hT, "hT", sq, tmp, rstd)

    def ffn(l):
        G = 4
        o = 0
        NUP, NDN, NTR = 2, 3 * G, 4
        wu = [fsb("wu%d_%d" % (l, i), [128, KC, 256], BF16, o + i * 4096) for i in range(NUP)]; o += NUP * 4096
        wd = [fsb("wd%d_%d" % (l, i), [128, D], BF16, o + i * 2048) for i in range(NDN)]; o += NDN * 2048
        stg = [fsb("fstg%d_0" % l, [128, 2048], F32, o), fsb("fstg%d_1" % l, [128, 1024], F32, o + 8192)]; o += 8192 + 4096
        hal = [[fsb("hal%d_%d_%d" % (l, ab, i), [128, 8], F32, o + (ab * 2 + i) * 32) for i in range(2)] for ab in range(2)]
        o += 4 * 32
        tr = [[fsb("tr%d_%d_%d" % (l, ab, i), [128, TT], F32, o + (ab * NTR + i) * TT * 4) for i in range(NTR)] for ab in range(2)]
        o += 2 * NTR * TT * 4
        sa = [fsb("sa%d_%d" % (l, i), [128, TT], BF16, o + i * TT * 2) for i in range(2)]; o += 2 * TT * 2
        tb16 = [fsb("tb16_%d_%d" % (l, i), [128, TT], BF16, o + i * TT * 2) for i in range(NTR)]; o += NTR * TT * 2
        assert o <= FSZ, (o, FSZ)
        if dbg and l == 0:
            for ab in range(2):
                for i in range(NTR):
                    DBG["tr%d_%d" % (ab, i)] = (tr[ab][i], ("tr", ab, i), [128, TT], F32)
            for i in range(2):
                DBG["sa_%d" % i] = (sa[i], ("sa", i), [128, TT], BF16)
        actg = [nc.alloc_sbuf_tensor_at("actg%d_%d" % (l, i), [128, G, S], BF16,
                                        offset=BASE + KC * S * 4 + KC * S * 2 + i * G * S * 2) for i in range(2)]
        groups = [list(range(g, min(g + G, NPAIR))) for g in range(0, NPAIR, G)]
        dbank = [0]

        def dma_pair(i):
            DMA(stg[0][:, 0:2048], wup_d[l, i], (), [("stg", 0)])
            DMA(stg[1][:, 0:1024], wdn_d[l, i], (), [("stg", 1)])

        def cast_pair(i):
            COPY("act", wu[i % NUP][:, :, :].rearrange("p k n -> p (k n)"), stg[0][:, 0:2048], [("stg", 0)], [("wu", i % NUP)])
            COPY("act", wd[i % NDN][:, :], stg[1][:, 0:1024], [("stg", 1)], [("wd", i % NDN)])

        def down_mm(gi, d, tt):
            grp = groups[gi]
            ag = actg[gi % 2]
            c = slice(tt * TT, (tt + 1) * TT)
            b = 6 + dbank[0] % 2
            dbank[0] += 1
            for ii, i in enumerate(grp):
                MM(ps[b][:, :], wd[i % NDN][:, d * 128:(d + 1) * 128], ag[:, ii, c], ii == 0, ii == len(grp) - 1,
                   [("wd", i % NDN), ("actg", gi % 2, ii, tt)], [("ps", b)])
            return (b, d, tt)

        def down_add(b, d, tt):
            c = slice(tt * TT, (tt + 1) * TT)
            TTOP("dve", xT[:, d, c], ps[b][:, :], xT[:, d, c], ALU.add, [("ps", b), ("xT", d, tt)], [("xT", d, tt)])

        def stage_d(k, gi, ii, tt):
            c = slice(tt * TT, (tt + 1) * TT)
            ACT(sa[k % 2], tr[0][k % NTR], AF.Silu, [("tr", 0, k % NTR)], [("sa", k % 2)])
            TTOP("dve", actg[gi % 2][:, ii, c], sa[k % 2], tb16[k % NTR], ALU.mult,
                 [("sa", k % 2), ("tb16", k % NTR)], [("actg", gi % 2, ii, tt)])

        dma_pair(0)
        cast_pair(0)
        k = 0
        pend_d = []
        pend_down = []
        pend_add = []
        for gi, grp in enumerate(groups):
            niter = len(grp) * NT
            it = 0
            for ii, i in enumerate(grp):
                if i + 1 < NPAIR:
                    dma_pair(i + 1)
                wus = i % NUP
                for tt in range(NT):
                    c = slice(tt * TT, (tt + 1) * TT)
                    banks = [(k % 3) * 2, (k % 3) * 2 + 1]
                    while pend_add:
                        down_add(*pend_add.pop(0))
                    for ab in range(2):
                        b = banks[ab]
                        for kc in range(KC):
                            MM(ps[b][:, :], wu[wus][:, kc, ab * 128:(ab + 1) * 128], hT[:, kc, c], kc == 0, kc == KC - 1,
                               [("wu", wus), ("hT", kc, tt)], [("ps", b)])
                    for ab in range(2):
                        b = banks[ab]
                        col = ab * NPAIR + i
                        if tt + 1 < NT:
                            COPY("act", hal[ab][tt % 2][:, 0:2], ps[b][:, TT - 2:TT], [("ps", b)], [("hal", ab, tt % 2)])
                        ACT(tr[ab][k % NTR], ps[b][:, :], AF.Identity, [("ps", b), "pcol"], [("tr", ab, k % NTR)],
                            bias=pc("cb%d" % l, col), scale=pc("cw2_%d" % l, col))
                    for step in range(4):
                        for ab in range(2):
                            b = banks[ab]
                            col = ab * NPAIR + i
                            t = tr[ab][k % NTR]
                            tk = ("tr", ab, k % NTR)
                            if step == 0:
                                STT(t[:, 1:TT], ps[b][:, 0:TT - 1], pc("cw1_%d" % l, col), t[:, 1:TT], ALU.mult, ALU.add,
                                    [("ps", b), tk, "pcol"], [tk])
                            elif step == 1:
                                dst, dk = (t, tk) if ab == 0 else (tb16[k % NTR], ("tb16", k % NTR, "m"))
                                STT(dst[:, 2:TT], ps[b][:, 0:TT - 2], pc("cw0_%d" % l, col), t[:, 2:TT], ALU.mult, ALU.add,
                                    [("ps", b), tk, "pcol"], [dk])
                            elif tt > 0:
                                hp = hal[ab][(tt - 1) % 2]
                                hk = ("hal", ab, (tt - 1) % 2)
                                if step == 2:
                                    STT(t[:, 0:1], hp[:, 1:2], pc("cw1_%d" % l, col), t[:, 0:1], ALU.mult, ALU.add, [hk, tk, "pcol"], [tk])
                                else:
                                    dst, dk = (t, tk) if ab == 0 else (tb16[k % NTR], ("tb16", k % NTR, "h"))
                                    STT(dst[:, 0:2], hp[:, 0:2], pc("cw0_%d" % l, col), t[:, 0:2], ALU.mult, ALU.add, [hk, tk, "pcol"], [dk])
                            elif step == 3 and ab == 1:
                                COPY("dve", tb16[k % NTR][:, 0:2], t[:, 0:2], [tk], [("tb16", k % NTR, "h")])
                    if tt == min(1, NT - 1) and i + 1 < NPAIR:
                        cast_pair(i + 1)
                    pend_d.append((k, gi, ii, tt))
                    if len(pend_d) > 2:
                        stage_d(*pend_d.pop(0))
                    if pend_down:
                        n_left = niter - it
                        take = -(-len(pend_down) // max(1, n_left))
                        for _ in range(take):
                            if len(pend_add) >= 2:
                                down_add(*pend_add.pop(0))
                            pend_add.append(down_mm(*pend_down.pop(0)))
                    k += 1
                    it += 1
            while pend_down:
                if len(pend_add) >= 2:
                    down_add(*pend_add.pop(0))
                pend_add.append(down_mm(*pend_down.pop(0)))
            pend_down = [(gi, d, tt) for tt in range(NT) for d in range(KC)]
        while pend_d:
            stage_d(*pend_d.pop(0))
        while pend_down:
            if len(pend_add) >= 2:
                down_add(*pend_add.pop(0))
            pend_add.append(down_mm(*pend_down.pop(0)))
        while pend_add:
            down_add(*pend_add.pop(0))

    def hgrn():
        o = 0
        e3 = fsb("lb_e", [128, 18], F32, o); o += 128
        ssum = fsb("lb_s", [128, 6], F32, o); o += 64
        srec = fsb("lb_r", [128, 6], F32, o); o += 64
        o0, _ = pidx["lbl0"]
        ACT(e3[:, :], pcol[:, o0:o0 + 18], AF.Exp, ["pcol"], ["lb_e"])
        TTOP("dve", ssum[:, :], e3[:, 0:6], e3[:, 6:12], ALU.add, ["lb_e"], ["lb_s"])
        TTOP("dve", ssum[:, :], ssum[:, :], e3[:, 12:18], ALU.add, ["lb_e", "lb_s"], ["lb_s"])
        A("dve", lambda h: h.reciprocal(out=srec[:, :], in_=ssum[:, :]), ["lb_s"], ["lb_r"])
        TTOP("dve", lbc[:, 0:6], e3[:, 0:6], srec[:, :], ALU.mult, ["lb_e", "lb_r"], [("lbc", 0)])
        TS("dve", lbc[:, 6:12], lbc[:, 0:6], -1.0, 1.0, ALU.mult, ALU.add, [("lbc", 0)], [("lbc", 1)])
        o = 256
        wh2 = [fsb("wh%d" % i, [128, KC, 512], BF16, o + i * KC * 512 * 2) for i in range(2)]; o += 2 * KC * 512 * 2
        stg = [fsb("hstg%d" % i, [128, 2048], F32, o + i * 8192) for i in range(NSTG)]; o += NSTG * 8192
        QT = fsb("QT", [128, S], BF16, o); o += S * 2
        KT = fsb("KT", [128, S], BF16, o); o += S * 2
        Ktok = fsb("Ktok", [128, NB, 128], BF16, o); o += S * 2
        vtok = fsb("vtok", [128, NB, 128], BF16, o); o += S * 2
        gT = fsb("gT", [128, S], BF16, o); o += S * 2
        Bl = fsb("Bl", [128, NCH], F32, o); o += NCH * 4
        mscan = fsb("mscan", [128, TT], F32, o); o += TT * 4
        tA = [fsb("tA%d" % i, [128, TT], F32, o + i * TT * 4) for i in range(6)]; o += 6 * TT * 4
        fR = fsb("fR", [128, TT], F32, o); o += TT * 4
        fP = fsb("fP", [128, TT], F32, o); o += TT * 4
        Tst = [fsb("Tst%d" % i, [128, 128], F32, o + i * 512) for i in range(2)]; o += 1024
        stb = [fsb("stb%d" % i, [128, 128], BF16, o + i * 256) for i in range(3)]; o += 768
        aT = [fsb("aT%d" % i, [128, 64], BF16, o + i * 128) for i in range(2)]; o += 256
        sqo = fsb("sqo", [128, TT], BF16, o); o += TT * 2
        assert o <= FSZ, (o, FSZ)
        MEMSET(mscan[:, :], 1.0, ["mscan"])
        MEMSET(mscan[:, :].rearrange("p (n c) -> p n c", c=64)[:, :, 0:1], 0.0, ["mscan"])
        qs, sg, fg, lf, bb, eb = tA
        prr = [0]
        rcr = [0]

        def pbank():
            b = prr[0]
            prr[0] = (prr[0] + 1) % 3
            return b

        def rbank():
            b = (3, 6, 7)[rcr[0]]
            rcr[0] = (rcr[0] + 1) % 3
            return b

        def load_head(hh):
            load_w(w0_d[hh], 4096, wh2[hh % 2][:, :, :].rearrange("p k n -> p (k n)"), ("wh", hh % 2), stg)

        def pre_stages(hh, tt):
            wh = wh2[hh % 2]
            wk = ("wh", hh % 2)
            c = slice(tt * TT, (tt + 1) * TT)
            st = {}

            def P1():
                st["bq"] = pbank()
                for kc in range(KC):
                    MM(ps[st["bq"]][:, :], wh[:, kc, 0:128], hT[:, kc, c], kc == 0, kc == KC - 1, [wk, ("hT", kc, tt)], [("ps", st["bq"])])
                st["bf"] = pbank()
                for kc in range(KC):
                    MM(ps[st["bf"]][:, :], wh[:, kc, 128:256], hT[:, kc, c], kc == 0, kc == KC - 1, [wk, ("hT", kc, tt)], [("ps", st["bf"])])

            def P2():
                ACT(qs, ps[st["bq"]][:, :], AF.Silu, [("ps", st["bq"])], ["qs"])
                ACT(sg, ps[st["bf"]][:, :], AF.Sigmoid, [("ps", st["bf"])], ["sg"])
                st["bg"] = pbank()
                for kc in range(KC):
                    MM(ps[st["bg"]][:, :], wh[:, kc, 256:384], hT[:, kc, c], kc == 0, kc == KC - 1, [wk, ("hT", kc, tt)], [("ps", st["bg"])])

            def P3():
                TS("dve", fg, sg, lbc[:, 6 + hh:7 + hh], lbc[:, hh:hh + 1], ALU.mult, ALU.add, ["sg", "lbc"], ["fg"])
                ACT(gT[:, c], ps[st["bg"]][:, :], AF.Silu, [("ps", st["bg"])], [("gT", tt)])

            def P4():
                ACT(lf, fg, AF.Ln, ["fg"], ["lf"])
                st["bv"] = pbank()
                for j in range(4):
                    tb = tt * 4 + j
                    for kc in range(KC):
                        MM(ps[st["bv"]][:, j * 128:(j + 1) * 128], hT[:, kc, tb * 128:(tb + 1) * 128], wh[:, kc, 384:512], kc == 0, kc == KC - 1,
                           [wk, ("hT", kc, tt)], [("ps", st["bv"])])

            def P5():
                A("dve", lambda h: h.tensor_tensor_scan(out=bb[:, :], data0=mscan[:, :], data1=lf[:, :], initial=0.0, op0=ALU.mult, op1=ALU.add),
                  ["mscan", "lf"], ["bb"])
                COPY("act", vtok[:, tt * 4:(tt + 1) * 4, :].rearrange("p j n -> p (j n)"), ps[st["bv"]][:, :], [("ps", st["bv"])], [("vtok", tt)])

            def P6():
                ACT(eb, bb, AF.Exp, ["bb"], ["eb"])
                ACT(sg, bb, AF.Exp, ["bb"], ["sg"], scale=-1.0)

            def P7():
                STT(QT[:, c], qs, 128.0 ** -0.5, eb, ALU.mult, ALU.mult, ["qs", "eb"], [("QT", tt)])
                COPY("dve", Bl[:, tt * 8:(tt + 1) * 8], eb[:, :].rearrange("p (n c) -> p n c", c=64)[:, :, 63], ["eb"], [("Bl", tt)])
                TS("dve", lf, fg, -1.0, 1.0, ALU.mult, ALU.add, ["fg"], ["lf"])
                TTOP("dve", KT[:, c], lf, sg, ALU.mult, ["lf", "sg"], [("KT", tt)])

            def P8():
                b = pbank()
                pb16 = ps[b][:, :].bitcast(BF16)
                for j in range(4):
                    tb = tt * 4 + j
                    A("pe", lambda h, j=j, tb=tb, pb16=pb16: h.transpose(out=pb16[:, j * 128:(j + 1) * 128], in_=KT[:, tb * 128:(tb + 1) * 128],
                                                                           identity=ident_bf[:, :]),
                      [("KT", tt), "ident"], [("ps", b)])
                COPY("act", Ktok[:, tt * 4:(tt + 1) * 4, :].rearrange("p j n -> p (j n)"), pb16[:, 0:512], [("ps", b)], [("Ktok", tt)])

            return [P1, P2, P3, P4, P5, P6, P7, P8]

        BO = 4

        def rec_step(hh, n):
            if n < NCH:
                base = (n % 2) * 64
                blkI = n // 2
                tt = n // 8
                cs = slice(n * 64, (n + 1) * 64)
                bA = rbank()
                MM(ps[bA][base:base + 64, 0:64], KT[:, cs], QT[:, cs], True, True, [("KT", tt), ("QT", tt)], [("ps", bA)])
                a = aT[n % 2]
                TTOP("dve", a[base:base + 64, :], ps[bA][base:base + 64, 0:64], tri64[base:base + 64, :], ALU.mult,
                     [("ps", bA), "tri64"], [("aT", n % 2)])
                bU = rbank()
                MM(ps[bU][:, 0:128], Ktok[base:base + 64, blkI, :], vtok[base:base + 64, blkI, :], True, True,
                   [("Ktok", tt), ("vtok", tt)], [("ps", bU)])
                if n == 0:
                    COPY("dve", Tst[0][:, :], ps[bU][:, 0:128], [("ps", bU)], [("Tst", 0)])
                else:
                    STT(Tst[n % 2][:, :], Tst[(n - 1) % 2][:, :], Bl[:, n - 1:n], ps[bU][:, 0:128], ALU.mult, ALU.add,
                        [("Tst", (n - 1) % 2), ("Bl", (n - 1) // 8), ("ps", bU)], [("Tst", n % 2)])
                if n + 1 < NCH:
                    TS("pool", stb[(n + 1) % 3][:, :], Tst[n % 2][:, :], Bl[:, n:n + 1], None, ALU.mult, ALU.bypass,
                       [("Tst", n % 2), ("Bl", n // 8)], [("stb", (n + 1) % 3)])
            if n > 0:
                m = n - 1
                mbase = (m % 2) * 64
                mblk = m // 2
                mtt = m // 8
                mcs = slice(m * 64, (m + 1) * 64)
                bo = BO + (mtt % 2)
                oc = slice((m % 8) * 64, (m % 8 + 1) * 64)
                MM(ps[bo][:, oc], vtok[mbase:mbase + 64, mblk, :], aT[m % 2][mbase:mbase + 64, :], True, m == 0,
                   [("vtok", mtt), ("aT", m % 2)], [("ps", bo, m % 8)])
                if m > 0:
                    MM(ps[bo][:, oc], stb[m % 3][:, :], QT[:, mcs], False, True, [("stb", m % 3), ("QT", mtt)], [("ps", bo, m % 8)])
                if m % 8 == 7:
                    c = slice(mtt * TT, (mtt + 1) * TT)
                    ACT(sqo, ps[bo][:, :], AF.Square, [("ps", bo)], ["sqo"])
                    b2 = rbank()
                    MM(ps[b2][:, :], ones_bf[:, :], sqo, True, True, ["sqo", "ones"], [("ps", b2)])
                    rstd_from_ss(b2, TT, 1.0 / 128, fR, fR, "fR", "fR")
                    STT(fP, ps[bo][:, :], pc("hgo"), fR, ALU.mult, ALU.mult, [("ps", bo), "fR", "pcol"], ["fP"])
                    TTOP("dve", yT[:, hh, c], fP, gT[:, c], ALU.mult, ["fP", ("gT", mtt)], [("yT", hh, mtt)])

        load_head(0)
        tiles = [(hh, tt) for hh in range(6) for tt in range(NT)]
        for f_ in pre_stages(*tiles[0]):
            f_()
        for i, (hh, tt) in enumerate(tiles):
            if tt == 0 and hh + 1 < 6:
                load_head(hh + 1)
            steps = [(hh, n) for n in range(tt * 8, tt * 8 + 8)]
            if tt == NT - 1:
                steps.append((hh, NCH))
            stages = pre_stages(*tiles[i + 1]) if i + 1 < len(tiles) else []
            for j in range(max(len(steps), len(stages))):
                if j < len(steps):
                    rec_step(*steps[j])
                if j < len(stages):
                    stages[j]()

    def mla_layer():
        mixer_norm(1, 65536 if FSZ >= 65536 + 6144 else 0)
        if FSZ < 65536 + 6144:
            sch.barrier()
        mem_prep(1)
        sch.barrier()
        o = 0
        cqn = fsb("cqn", [128, 3, S], BF16, o); o += 3 * S * 2
        ckvn = fsb("ckvn", [128, 2, S], BF16, o); o += 2 * S * 2
        kpe = fsb("kpe", [128, S], BF16, o); o += S * 2
        kpes = fsb("kpes", [128, S], BF16, o); o += S * 2
        cosT = fsb("cosT", [128, S], BF16, o); o += S * 2
        sinT = fsb("sinT", [128, S], BF16, o); o += S * 2
        qw1 = fsb("qw1", [128, KC, 256], BF16, o); o += KC * 256 * 2
        P0 = o
        pi32 = fsb("pi32", [128, TT], I32, o); o += TT * 4
        ang = fsb("ang", [128, TT], F32, o); o += TT * 4
        kf = fsb("kf", [128, TT], F32, o); o += TT * 4
        rr_ = fsb("rr_", [128, TT], F32, o); o += TT * 4
        msk = fsb("msk", [128, TT], F32, o); o += TT * 4
        TWO_PI = 2.0 * np.pi
        C1 = 6.28125
        C2 = TWO_PI - C1
        for tt in range(NT):
            c = slice(tt * TT, (tt + 1) * TT)
            DMA(pi32[:, :], pos_d[:, c].partition_broadcast(128), (), ["pi32"])
            for which in range(2):
                COPY("dve", ang, pi32, ["pi32"], ["ang"])
                if which == 0:
                    TS("dve", ang, ang, pc("invf"), pc("halfpi"), ALU.mult, ALU.add, ["ang", "pcol"], ["ang"])
                else:
                    TS("dve", ang, ang, pc("invf"), None, ALU.mult, ALU.bypass, ["ang", "pcol"], ["ang"])
                TS("dve", kf, ang, 1.0 / TWO_PI, None, ALU.mult, ALU.bypass, ["ang"], ["kf"])
                COPY("dve", pi32, kf, ["kf"], ["pi32"])
                COPY("dve", kf, pi32, ["pi32"], ["kf"])
                STT(rr_, kf, -C1, ang, ALU.mult, ALU.add, ["kf", "ang"], ["rr_"])
                STT(rr_, kf, -C2, rr_, ALU.mult, ALU.add, ["kf", "rr_"], ["rr_"])
                TS("dve", msk, rr_, float(np.pi), -TWO_PI, ALU.is_gt, ALU.mult, ["rr_"], ["msk"])
                TTOP("dve", rr_, rr_, msk, ALU.add, ["rr_", "msk"], ["rr_"])
                TS("dve", msk, rr_, float(-np.pi), TWO_PI, ALU.is_lt, ALU.mult, ["rr_"], ["msk"])
                TTOP("dve", rr_, rr_, msk, ALU.add, ["rr_", "msk"], ["rr_"])
                TS("dve", rr_, rr_, float(np.pi), float(-np.pi), ALU.min, ALU.max, ["rr_"], ["rr_"])
                if which == 0:
                    ACT(cosT[:, c], rr_, AF.Sin, ["rr_"], [("cosT", tt)])
                else:
                    ACT(kf, rr_, AF.Sin, ["rr_"], ["kf"])
                    TS("dve", sinT[:, c], kf, pc("sgn"), None, ALU.mult, ALU.bypass, ["kf", "pcol"], [("sinT", tt)])
                if which == 0:
                    DMA(pi32[:, :], pos_d[:, c].partition_broadcast(128), (), ["pi32"])
        wg = fsb("w1g", [128, 3, KC, 128], BF16, o); o += 3 * KC * 128 * 2
        stg = [fsb("bstg%d" % i, [128, 1024], F32, o + i * 4096) for i in range(NSTG)]; o += NSTG * 4096
        sq = [fsb("bsq%d" % i, [128, TT], BF16, o + i * TT * 2) for i in range(2)]; o += 2 * TT * 2
        tmp = fsb("btmp", [128, TT], F32, o); o += TT * 4
        rstd = fsb("brstd", [128, TT], F32, o); o += TT * 4
        assert o <= FSZ, (o, FSZ)
        groups = [([0, 1, 2], cqn, "cqn", "qag", 384), ([3, 4], ckvn, "ckvn", "kvag", 256)]
        for tiles, dst, dkey, gname, width in groups:
            for gi, t in enumerate(tiles):
                load_w(w1in_d[t], 1024, wg[:, gi, :, :].rearrange("p k n -> p (k n)"), ("w1g", gi), stg)
            for tt in range(NT):
                c = slice(tt * TT, (tt + 1) * TT)
                bs = 6 + tt % 2
                for gi, t in enumerate(tiles):
                    b = (tt % 2) * 3 + gi
                    for kc in range(KC):
                        MM(ps[b][:, :], wg[:, gi, kc, :], hT[:, kc, c], kc == 0, kc == KC - 1, [("w1g", gi), ("hT", kc, tt)], [("ps", b)])
                    ACT(sq[gi % 2], ps[b][:, :], AF.Square, [("ps", b)], [("bsq", gi % 2)])
                    MM(ps[bs][:, :], ones_bf[:, :], sq[gi % 2], gi == 0, gi == len(tiles) - 1, [("bsq", gi % 2), "ones"], [("ps", bs)])
                rstd_from_ss(bs, TT, 1.0 / width, tmp, rstd, "btmp", "brstd")
                for gi, t in enumerate(tiles):
                    STT(dst[:, gi, c], ps[(tt % 2) * 3 + gi][:, :], pc(gname, gi), rstd, ALU.mult, ALU.mult,
                        [("ps", (tt % 2) * 3 + gi), "brstd", "pcol"], [(dkey, gi, tt)])
        for t, dst, dkey in ((5, kpe, "kpe"), (6, kpes, "kpes")):
            load_w(w1in_d[t], 1024, wg[:, 0, :, :].rearrange("p k n -> p (k n)"), ("w1g", 0), stg)
            for tt in range(NT):
                c = slice(tt * TT, (tt + 1) * TT)
                b = bank()
                for kc in range(KC):
                    MM(ps[b][:, :], wg[:, 0, kc, :], hT[:, kc, c], kc == 0, kc == KC - 1, [("w1g", 0), ("hT", kc, tt)], [("ps", b)])
                COPY("act", dst[:, c], ps[b][:, :], [("ps", b)], [(dkey, tt)])
        for j, t in enumerate((7, 8)):
            load_w(w1in_d[t], 1024, wg[:, 1, :, :].rearrange("p k n -> p (k n)"), ("w1g", 1), stg)
            for kc in range(KC):
                COPY("pool", qw1[:, kc, j * 128:(j + 1) * 128], wg[:, 1, kc, :], [("w1g", 1)], [("qw1", j)])
        H_ = slice(0, 64)
        for tt in range(NT):
            c = slice(tt * TT, (tt + 1) * TT)
            STT(tmp[H_, :], kpe[H_, c], pc("kg_r", 0, H_), cosT[H_, c], ALU.mult, ALU.mult, [("kpe", tt), ("cosT", tt), "pcol"], ["btmp"])
            STT(rstd[H_, :], kpes[H_, c], pc("kg_rs", 0, H_), sinT[H_, c], ALU.mult, ALU.mult, [("kpes", tt), ("sinT", tt), "pcol"], ["brstd"])
            ACT(kpes[H_, c], kpe[H_, c], AF.Square, [("kpe", tt)], [("kpes", tt)])
            TTOP("dve", kpe[H_, c], tmp[H_, :], rstd[H_, :], ALU.add, ["btmp", "brstd"], [("kpe", tt)])
        sch.barrier()
        mem_attn(1, qw1, "qw1", P0)
        sch.barrier()
        HB = BASE + KC * S * 4
        ho = [HB]

        def hsb(name, shape, dt):
            nb = int(np.prod(shape[1:])) * (4 if dt == F32 else 2)
            t = nc.alloc_sbuf_tensor_at(name, list(shape), dt, offset=ho[0])
            ho[0] += (nb + 63) // 64 * 64
            assert ho[0] <= HB + KC * S * 2, (name, ho[0] - HB)
            return t
        Kn = hsb("Kn", [128, S], BF16)
        Kr = hsb("Kr", [128, S], BF16)
        Vh = hsb("Vh", [128, NB, 128], BF16)
        Qn = hsb("Qn", [128, S], BF16)
        Qr = hsb("Qr", [128, S], BF16)
        wq2 = [hsb("wq%d" % i, [128, 3, 256], BF16) for i in range(2)]
        wkv2 = [hsb("wkv%d" % i, [128, 2, 256], BF16) for i in range(2)]
        o = P0
        pt = [fsb("apt%d" % i, [128, TT], BF16, o + i * TT * 2) for i in range(4)]; o += 4 * TT * 2
        stg = [fsb("cstg%d" % i, [128, 1024], F32, o + i * 4096) for i in range(NSTG)]; o += NSTG * 4096
        sq = [fsb("csq%d" % i, [128, TT], BF16, o + i * TT * 2) for i in range(2)]; o += 2 * TT * 2
        tmp = fsb("ctmp", [128, TT], F32, o); o += TT * 4
        rstd = fsb("crstd", [128, TT], F32, o); o += TT * 4
        sqQ = [fsb("csqQ%d" % i, [128, TT], BF16, o + i * TT * 2) for i in range(2)]; o += 2 * TT * 2
        tmpQ = fsb("ctmpQ", [128, TT], F32, o); o += TT * 4
        rstdQ = fsb("crstdQ", [128, TT], F32, o); o += TT * 4
        r1 = fsb("cr1", [128, TT], F32, o); o += TT * 4
        r2 = fsb("cr2", [128, TT], F32, o); o += TT * 4
        rec = fsb("crec", [128, TT], F32, o); o += TT * 4
        assert o <= FSZ, (o, FSZ)
        H = slice(0, 64)
        SC = 192.0 ** -0.5
        pti = [0]

        def rope_part(raw, rawkeys, g_r, g_rs, dst, dkey, tt):
            c = slice(tt * TT, (tt + 1) * TT)
            STT(r1[H, :], raw[0], pc(g_r, 0, H), rstdQ[H, :], ALU.mult, ALU.mult, rawkeys + ["crstdQ", "pcol"], ["cr1"])
            STT(r2[H, :], raw[1], pc(g_rs, 0, H), rstdQ[H, :], ALU.mult, ALU.mult, rawkeys + ["crstdQ", "pcol"], ["cr2"])
            TTOP("dve", r1[H, :], r1[H, :], cosT[H, c], ALU.mult, ["cr1", ("cosT", tt)], ["cr1"])
            TTOP("dve", r2[H, :], r2[H, :], sinT[H, c], ALU.mult, ["cr2", ("sinT", tt)], ["cr2"])
            TTOP("dve", dst[H, c], r1[H, :], r2[H, :], ALU.add, ["cr1", "cr2"], [(dkey, tt)])

        def load_mla_head(hh):
            load_w(wuq_d[hh], 768, wq2[hh % 2][:, :, :].rearrange("p k n -> p (k n)"), ("wq", hh % 2), stg)
            load_w(wukv_d[hh], 512, wkv2[hh % 2][:, :, :].rearrange("p k n -> p (k n)"), ("wkv", hh % 2), stg)

        load_mla_head(0)
        for hh in range(6):
            wq, wkv = wq2[hh % 2], wkv2[hh % 2]
            wqk, wkvk = ("wq", hh % 2), ("wkv", hh % 2)
            if hh + 1 < 6:
                load_mla_head(hh + 1)
            for tt in range(NT):
                c = slice(tt * TT, (tt + 1) * TT)
                bK = bank8()
                for kc in range(2):
                    MM(ps[bK][:, :], wkv[:, kc, 0:128], ckvn[:, kc, c], kc == 0, kc == 1, [wkvk, ("ckvn", kc, tt)], [("ps", bK)])
                ACT(sq[0], ps[bK][:, :], AF.Square, [("ps", bK)], [("csq", 0)])
                bs = bank8()
                MM(ps[bs][:, :], ones_bf[:, :], sq[0], True, False, [("csq", 0), "ones"], [("ps", bs)])
                MM(ps[bs][:, :], ones_bf[H, :], kpes[H, c], False, True, [("kpes", tt), "ones"], [("ps", bs)])
                rstd_from_ss(bs, TT, 1.0 / 192, tmp, rstd, "ctmp", "crstd")
                STT(Kn[:, c], ps[bK][:, :], pc("kg_n"), rstd, ALU.mult, ALU.mult, [("ps", bK), "crstd", "pcol"], [("Kn", tt)])
                TTOP("dve", Kr[H, c], kpe[H, c], rstd[H, :], ALU.mult, [("kpe", tt), "crstd"], [("Kr", tt)])
                bV = bank8()
                for j in range(4):
                    tb = tt * 4 + j
                    for kc in range(2):
                        MM(ps[bV][:, j * 128:(j + 1) * 128], ckvn[:, kc, tb * 128:(tb + 1) * 128], wkv[:, kc, 128:256], kc == 0, kc == 1,
                           [wkvk, ("ckvn", kc, tt)], [("ps", bV)])
                COPY("act", Vh[:, tt * 4:(tt + 1) * 4, :].rearrange("p j n -> p (j n)"), ps[bV][:, :], [("ps", bV)], [("Vh", tt)])
                bQ = bank8()
                for kc in range(3):
                    MM(ps[bQ][:, :], wq[:, kc, 0:128], cqn[:, kc, c], kc == 0, kc == 2, [wqk, ("cqn", kc, tt)], [("ps", bQ)])
                bR = bank8()
                for kc in range(3):
                    MM(ps[bR][H, :], wq[:, kc, 128:192], cqn[:, kc, c], kc == 0, kc == 2, [wqk, ("cqn", kc, tt)], [("ps", bR)])
                bS2 = bank8()
                for kc in range(3):
                    MM(ps[bS2][H, :], wq[:, kc, 192:256], cqn[:, kc, c], kc == 0, kc == 2, [wqk, ("cqn", kc, tt)], [("ps", bS2)])
                ACT(sqQ[0], ps[bQ][:, :], AF.Square, [("ps", bQ)], [("csqQ", 0)])
                ACT(sqQ[1][H, :], ps[bR][H, :], AF.Square, [("ps", bR)], [("csqQ", 1)])
                bs = bank8()
                MM(ps[bs][:, :], ones_bf[:, :], sqQ[0], True, False, [("csqQ", 0), "ones"], [("ps", bs)])
                MM(ps[bs][:, :], ones_bf[H, :], sqQ[1][H, :], False, True, [("csqQ", 1), "ones"], [("ps", bs)])
                rstd_from_ss(bs, TT, 1.0 / 192, tmpQ, rstdQ, "ctmpQ", "crstdQ")
                STT(Qn[:, c], ps[bQ][:, :], pc("qg_n"), rstdQ, ALU.mult, ALU.mult, [("ps", bQ), "crstdQ", "pcol"], [("Qn", tt)])
                rope_part((ps[bR][H, :], ps[bS2][H, :]), [("ps", bR), ("ps", bS2)], "qg_r", "qg_rs", Qr, "Qr", tt)
            for qt in range(NT):
                c0 = qt * TT
                nkt = 4 * (qt + 1)
                par = (hh * NT + qt) % 2
                bo, bd = 4 + par, 6 + par
                pend_pv = []

                def pv(kt, pi, q0, ktt, bo=bo, bd=bd, nkt=nkt):
                    MM(ps[bo][:, q0:TT], Vh[:, kt, :], pt[pi][:, q0:TT], kt == 0, kt == nkt - 1, [("Vh", ktt), ("apt", pi)], [("ps", bo)])
                    MM(ps[bd][:, q0:TT], ones_bf[:, :], pt[pi][:, q0:TT], kt == 0, kt == nkt - 1, ["ones", ("apt", pi)], [("ps", bd)])

                for kt in range(nkt):
                    j = kt - 4 * qt
                    q0 = max(0, j) * 128
                    ktt = kt // 4
                    bSc = bank()
                    MM(ps[bSc][:, q0:TT], Kn[:, kt * 128:(kt + 1) * 128], Qn[:, c0 + q0:c0 + TT], True, False,
                       [("Kn", ktt), ("Qn", qt)], [("ps", bSc)])
                    MM(ps[bSc][:, q0:TT], Kr[H, kt * 128:(kt + 1) * 128], Qr[H, c0 + q0:c0 + TT], False, True,
                       [("Kr", ktt), ("Qr", qt)], [("ps", bSc)])
                    pi = pti[0] % 4
                    pti[0] += 1
                    ACT(pt[pi][:, q0:TT], ps[bSc][:, q0:TT], AF.Exp, [("ps", bSc)], [("apt", pi)], scale=SC)
                    if j >= 0:
                        A("pool", lambda h, pi=pi, q0=q0: h.affine_select(out=pt[pi][:, q0:q0 + 128], in_=pt[pi][:, q0:q0 + 128], pattern=[[1, 128]],
                                                                           compare_op=ALU.is_ge, fill=0.0, base=0, channel_multiplier=-1),
                          [("apt", pi)], [("apt", pi)])
                    pend_pv.append((kt, pi, q0, ktt))
                    if len(pend_pv) > 2:
                        pv(*pend_pv.pop(0))
                while pend_pv:
                    pv(*pend_pv.pop(0))
                A("dve", lambda h, bd=bd: h.reciprocal(out=rec[:, :], in_=ps[bd][:, :]), [("ps", bd)], ["crec"])
                TTOP("dve", yT[:, hh, c0:c0 + TT], ps[bo][:, :], rec[:, :], ALU.mult, [("ps", bo), "crec"], [("yT", hh, qt)])
        sch.barrier()
        out_proj(1, 0, ffn_norm_cb(1, KC * D * 2 + NSTG * 8192))
        sch.barrier()
        ffn(1)
        sch.barrier()

    def mixer_norm(l, o=0):
        sq = [fsb("nsq%d_%d" % (l, i), [128, TT], BF16, o + i * TT * 2) for i in range(2)]; o += 2 * TT * 2
        tmp = fsb("ntmp%d" % l, [128, TT], F32, o); o += TT * 4
        rstd = fsb("nrstd%d" % l, [128, TT], F32, o); o += TT * 4
        norm_full(xT, "xT", "mixg%d" % l, hT, "hT", sq, tmp, rstd)

    mem_prep(0)
    for tt in range(NT):
        for kc in range(KC):
            DMA(xT[:, kc, tt * TT:(tt + 1) * TT], xT_d[kc * 128:(kc + 1) * 128, tt * TT:(tt + 1) * TT], (), [("xT", kc, tt)])
    mixer_norm(0, 65536 if FSZ >= 65536 + 6144 else 0)
    sch.barrier()
    hgrn()
    sch.barrier()
    qw = fsb("qw0", [128, KC, 256], BF16, 0)
    _st = fsb("qstg0", [128, 2048], F32, 4096)
    stg0 = [_st] * NSTG
    load_w(w0m_d, 2048, qw[:, :, :].rearrange("p k n -> p (k n)"), "qw", stg0)
    o_end = mem_attn(0, qw, "qw", 4096 + 8192)
    if o_end + KC * D * 2 + NSTG * 8192 + 6144 <= FSZ:
        out_proj(0, o_end, ffn_norm_cb(0, o_end + KC * D * 2 + NSTG * 8192))
    else:
        sch.barrier()
        out_proj(0, 0, ffn_norm_cb(0, KC * D * 2 + NSTG * 8192))
    sch.barrier()
    ffn(0)
    sch.barrier()
    if nlayers > 1:
        mla_layer()
    outs = []
    for name, (t, key, shape, dt) in DBG.items():
        outs.append(dump(name, t, key, shape, dt))
    for kc in range(KC):
        outs.append(DMA(out_d[kc * 128:(kc + 1) * 128, :], xT[:, kc, :], [("xT", kc)], [("outd", kc)]))
    sch.emit(nc, final_wait_ops=outs)
    return nc


def make_in_maps(inputs, S, ncores):
    shared, pidx = pack_shared(inputs)
    x = np.asarray(inputs["x"], np.float32)
    mem = np.asarray(inputs["mem"], np.float32)
    pos = np.asarray(inputs["positions"]).astype(np.int32)
    in_maps = []
    for b in range(ncores):
        m = dict(shared)
        m["xT"] = np.ascontiguousarray(x[b].T)
        m["memT"] = np.ascontiguousarray(mem[b].T)
        m["pos"] = np.ascontiguousarray(pos[b][None, :])
        in_maps.append(m)
    return in_maps, pidx, shared["pcol"].shape[1]


def kernel(**inputs):
    x = np.asarray(inputs["x"])
    B, S, _ = x.shape
    in_maps, pidx, ncol = make_in_maps(inputs, S, B)
    nc = build(S, pidx, ncol)
    res = run_bass_kernel_spmd(nc, in_maps, core_ids=list(range(B)))
    out = np.stack([np.ascontiguousarray(res.results[b]["outT"].T) for b in range(B)])
    return out.astype(np.float32)
```

```python
import contextlib
import numpy as np
import concourse.bass as bass
import concourse.mybir as mybir
from concourse.bass_utils import run_bass_kernel_spmd

F32 = mybir.dt.float32
BF16 = mybir.dt.bfloat16
I32 = mybir.dt.int32
AF = mybir.ActivationFunctionType
ALU = mybir.AluOpType

D = 1024
KC = 8
MEM = 256
DFF = 2816
NPAIR = 22
EPS = 1e-6
ENGS = ("pe", "act", "dve", "pool", "sp")
NDSEM = 24


class Op:
    __slots__ = ("eng", "idx", "fn", "waits", "dwaits", "need_inc", "incval", "is_dma", "dma_id", "known", "dknown")

    def __init__(self, eng, idx, fn, is_dma):
        self.eng = eng
        self.idx = idx
        self.fn = fn
        self.waits = {}
        self.dwaits = []
        self.need_inc = False
        self.incval = 0
        self.is_dma = is_dma
        self.dma_id = -1
        self.known = None
        self.dknown = None


class Sched:
    def __init__(self):
        self.ops = {e: [] for e in ENGS}
        self.known = {e: {f: -1 for f in ENGS} for e in ENGS}
        self.dknown = {e: set() for e in ENGS}
        self.regs = {}
        self.ndma = 0
        self.dma_ops = []
        self.pending = {e: [] for e in ENGS}

    @staticmethod
    def _conf(k1, k2):
        n = min(len(k1), len(k2))
        return k1[:n] == k2[:n]

    def add(self, eng, fn, reads=(), writes=(), dma=False):
        op = Op(eng, len(self.ops[eng]), fn, dma)
        reads = [k if isinstance(k, tuple) else (k,) for k in reads]
        writes = [k if isinstance(k, tuple) else (k,) for k in writes]
        deps = []
        odeps = []
        for k in reads:
            root = self.regs.setdefault(k[0], {})
            for kk, ent in root.items():
                if self._conf(kk, k):
                    deps.extend(ent[0])
        for k in writes:
            root = self.regs.setdefault(k[0], {})
            for kk, ent in root.items():
                if self._conf(kk, k):
                    odeps.extend(ent[0])
                    odeps.extend(ent[1])
        deps.extend(odeps)
        pend = self.pending[eng]
        if pend:
            deps.extend(pend)
            self.pending[eng] = []
        for k in reads:
            ent = self.regs[k[0]].setdefault(k, [[], []])
            ent[1].append(op)
        for k in writes:
            root = self.regs[k[0]]
            for kk in list(root.keys()):
                if kk != k and self._conf(kk, k):
                    if len(kk) >= len(k):
                        del root[kk]
                    else:
                        root[kk][0].append(op)
            root[k] = [[op], []]
        kn = self.known[eng]
        dk = self.dknown[eng]
        for d in deps:
            if d is op:
                continue
            if d.is_dma:
                if d.dma_id in dk:
                    continue
                if d not in op.dwaits:
                    op.dwaits.append(d)
            else:
                if d.eng == eng and eng in ("pe", "sp"):
                    continue
                if kn[d.eng] >= d.idx:
                    continue
                cur = op.waits.get(d.eng)
                if cur is None or cur.idx < d.idx:
                    op.waits[d.eng] = d
        if dma:
            op.dma_id = self.ndma
            self.ndma += 1
            self.dma_ops.append(op)
            if op.dma_id >= NDSEM:
                prev = self.dma_ops[op.dma_id - NDSEM]
                if prev.dma_id not in dk and prev not in op.dwaits:
                    op.dwaits.append(prev)
            op.need_inc = True
        for d in op.waits.values():
            d.need_inc = True
            if kn[d.eng] < d.idx:
                kn[d.eng] = d.idx
            for f, v in d.known.items():
                if f != eng and kn[f] < v:
                    kn[f] = v
            dk |= d.dknown
        for d in op.dwaits:
            dk.add(d.dma_id)
            for f, v in d.known.items():
                if f != eng and kn[f] < v:
                    kn[f] = v
            dk |= d.dknown
        op.known = dict(kn)
        op.dknown = set(dk)
        self.ops[eng].append(op)
        return op

    def barrier(self):
        lasts = []
        for f in ENGS:
            if f == "sp":
                continue
            if self.ops[f]:
                lasts.append(self.ops[f][-1])
        for e in ENGS:
            self.pending[e] = [o for o in lasts if o.eng != e] + list(self.dma_ops)

    def emit(self, nc, final_wait_ops=()):
        for e in ENGS:
            c = 0
            for op in self.ops[e]:
                if op.is_dma:
                    op.incval = 16 * (op.dma_id // NDSEM + 1)
                elif op.need_inc:
                    c += 1
                    op.incval = c
        with contextlib.ExitStack() as st:
            sems = {e: st.enter_context(nc.semaphore("s_" + e)) for e in ENGS}
            dsems = [st.enter_context(nc.semaphore("d%d" % i)) for i in range(NDSEM)]
            block = st.enter_context(nc.Block())

            def run(e, h):
                for op in self.ops[e]:
                    for d in op.waits.values():
                        h.wait_ge(sems[d.eng], d.incval)
                    for d in op.dwaits:
                        h.wait_ge(dsems[d.dma_id % NDSEM], d.incval)
                    ins = op.fn(h)
                    if op.is_dma:
                        ins.then_inc(dsems[op.dma_id % NDSEM], 16)
                    elif op.need_inc:
                        ins.then_inc(sems[e], 1)
                if e == "sp":
                    for d in final_wait_ops:
                        h.wait_ge(dsems[d.dma_id % NDSEM], d.incval)

            @block.tensor
            def _(h):
                run("pe", h)

            @block.scalar
            def _(h):
                run("act", h)

            @block.vector
            def _(h):
                run("dve", h)

            @block.gpsimd
            def _(h):
                run("pool", h)

            @block.sync
            def _(h):
                run("sp", h)


def _cols(vec, n=128):
    v = np.asarray(vec, np.float32).reshape(-1, n)
    return np.ascontiguousarray(v.T)


def _wpack(w, kc):
    n = w.shape[1]
    return np.ascontiguousarray(w.reshape(kc, 128, n).transpose(1, 0, 2).reshape(128, kc * n))


class PCol:
    def __init__(self):
        self.cols = []
        self.idx = {}

    def put(self, name, arr):
        arr = np.asarray(arr, np.float32)
        if arr.ndim == 1:
            arr = arr[:, None]
        assert arr.shape[0] == 128, (name, arr.shape)
        self.idx[name] = (sum(c.shape[1] for c in self.cols), arr.shape[1])
        self.cols.append(arr)

    def array(self):
        return np.ascontiguousarray(np.concatenate(self.cols, axis=1))


def pack_shared(inp):
    pc = PCol()
    f = lambda a: np.asarray(a, np.float32)
    for l in range(2):
        pc.put("mixg%d" % l, _cols(f(inp["mix_norm_g"])[l]))
        pc.put("ffng%d" % l, _cols(f(inp["ffn_norm_g"])[l]))
        pc.put("memg%d" % l, _cols(f(inp["mem_norm_g"])[l]))
        pc.put("mqg%d" % l, np.tile(f(inp["mem_q_norm_g"])[l], 2))
        pc.put("mkg%d" % l, np.tile(f(inp["mem_k_norm_g"])[l], 2))
        cw = f(inp["conv_w"])[l]
        for j in range(3):
            pc.put("cw%d_%d" % (j, l), _cols(cw[j]))
        pc.put("cb%d" % l, _cols(f(inp["conv_b"])[l]))
    lbl = f(inp["hg_lb_logits"])
    for j in range(3):
        pc.put("lbl%d" % j, _cols(lbl[j]))
    pc.put("hgo", f(inp["hg_out_norm_g"])[0])
    pc.put("qag", _cols(f(inp["mla_qa_norm_g"])[0]))
    pc.put("kvag", _cols(f(inp["mla_kva_norm_g"])[0]))
    qg = f(inp["mla_q_norm_g"])[0]
    kg = f(inp["mla_k_norm_g"])[0]
    sw = np.concatenate([np.arange(160, 192), np.arange(128, 160)])
    pc.put("qg_n", qg[:128])
    pc.put("qg_r", np.tile(qg[128:192], 2))
    pc.put("qg_rs", np.tile(qg[sw], 2))
    pc.put("kg_n", kg[:128])
    pc.put("kg_r", np.tile(kg[128:192], 2))
    pc.put("kg_rs", np.tile(kg[sw], 2))
    inv = (10000.0 ** (-np.arange(32, dtype=np.float32) / 32)).astype(np.float32)
    pc.put("invf", np.tile(inv, 4))
    pc.put("sgn", np.tile(np.concatenate([-np.ones(32), np.ones(32)]), 2))
    pc.put("eps", np.full(128, EPS))
    pc.put("halfpi", np.full(128, np.pi / 2))

    out = {"pcol": pc.array()}
    w = f(inp["hg_w_in"])[0]
    units = []
    for h in range(6):
        cs = np.concatenate([np.arange(h * 128, h * 128 + 128), 768 + np.arange(h * 128, h * 128 + 128),
                             2304 + np.arange(h * 128, h * 128 + 128), 1536 + np.arange(h * 128, h * 128 + 128)])
        units.append(_wpack(w[:, cs], 8))
    out["w0"] = np.stack(units)
    out["w0m"] = _wpack(w[:, 3072:3328], 8)
    out["wmem"] = np.stack([_wpack(f(inp["w_mem_kv"])[l], 8) for l in range(2)])
    out["wout"] = np.stack([_wpack(f(inp["w_out"])[l], 8) for l in range(2)])
    wup = f(inp["w_up"])
    wdn = f(inp["w_down"])
    ups, dns = [], []
    for l in range(2):
        for i in range(NPAIR):
            cs = np.concatenate([np.arange(i * 128, i * 128 + 128), DFF + np.arange(i * 128, i * 128 + 128)])
            ups.append(_wpack(wup[l][:, cs], 8))
            dns.append(np.ascontiguousarray(wdn[l][i * 128:(i + 1) * 128, :]))
    out["wup"] = np.stack(ups).reshape(2, NPAIR, 128, 2048)
    out["wdn"] = np.stack(dns).reshape(2, NPAIR, 128, 1024)
    w1 = f(inp["mla_w_in"])[0]
    swk = np.concatenate([np.arange(32, 64), np.arange(0, 32)])
    tiles = [w1[:, 0:128], w1[:, 128:256], w1[:, 256:384], w1[:, 384:512], w1[:, 512:640],
             np.concatenate([w1[:, 640:704], w1[:, 640:704]], axis=1),
             np.concatenate([w1[:, 640 + swk], w1[:, 640 + swk]], axis=1),
             w1[:, 704:832], w1[:, 832:960]]
    out["w1in"] = np.stack([_wpack(t, 8) for t in tiles])
    wuq = f(inp["mla_w_uq"])[0]
    wukv = f(inp["mla_w_ukv"])[0]
    uqs, ukvs = [], []
    for h in range(6):
        b = h * 192
        cs = np.concatenate([np.arange(b, b + 128), np.arange(b + 128, b + 192), b + 128 + swk])
        uqs.append(_wpack(wuq[:, cs], 3))
        ukvs.append(_wpack(wukv[:, h * 256:(h + 1) * 256], 2))
    out["wuq"] = np.stack(uqs)
    out["wukv"] = np.stack(ukvs)
    return out, pc.idx


def build(S, pidx, ncol, nlayers=2, dbg=()):
    nc = bass.Bass("TRN2", target_bir_lowering=False)
    TT = 512
    NT = S // TT
    NB = S // 128
    NCH = S // 64
    sch = Sched()
    A = sch.add

    def dram(name, shape, dt=F32, kind="ExternalInput"):
        return nc.dram_tensor(name, list(shape), dt, kind=kind).ap()

    xT_d = dram("xT", [D, S])
    memT_d = dram("memT", [D, MEM])
    pos_d = dram("pos", [1, S], I32)
    pcol_d = dram("pcol", [128, ncol])
    w0_d = dram("w0", [6, 128, 4096])
    w0m_d = dram("w0m", [128, 2048])
    wmem_d = dram("wmem", [2, 128, 4096])
    wout_d = dram("wout", [2, 128, 8192])
    wup_d = dram("wup", [2, NPAIR, 128, 2048])
    wdn_d = dram("wdn", [2, NPAIR, 128, 1024])
    w1in_d = dram("w1in", [9, 128, 1024])
    wuq_d = dram("wuq", [6, 128, 768])
    wukv_d = dram("wukv", [6, 128, 512])
    out_d = dram("outT", [D, S], F32, "ExternalOutput")
    DBG = {}

    BASE = 16640
    LIMIT = 229376
    cur = [BASE]

    def sb(name, shape, dt, off=None):
        nbytes = int(np.prod(shape[1:])) * (4 if dt in (F32, I32) else 2)
        if off is None:
            off = cur[0]
            cur[0] += (nbytes + 63) // 64 * 64
            assert cur[0] <= LIMIT, (name, cur[0])
        assert off + nbytes <= LIMIT, (name, off, nbytes)
        return nc.alloc_sbuf_tensor_at(name, list(shape), dt, offset=off)

    xT = sb("xTs", [128, KC, S], F32)
    hT = sb("hTs", [128, KC, S], BF16)
    yT = sb("yTs", [128, KC, S], BF16)
    pcol = sb("pcols", [128, ncol], F32)
    ones_bf = sb("ones", [128, 128], BF16)
    ident_bf = sb("ident", [128, 128], BF16)
    tri_bf = sb("tri", [128, 128], BF16)
    tri64 = sb("tri64", [128, 64], BF16)
    blk_bf = sb("blk", [128, 128], BF16)
    kn_m = sb("kn_m", [128, 2, MEM], BF16)
    vm_m = sb("vm_m", [128, 2, 256], BF16)
    lbc = sb("lbc", [128, 12], F32)
    F0 = cur[0]
    FSZ = LIMIT - F0

    def fsb(name, shape, dt, off):
        return sb(name, shape, dt, F0 + off)

    ps = [nc.alloc_psum_tensor("ps%d" % i, [128, 512], F32) for i in range(8)]
    rr = [0]

    def bank():
        b = rr[0]
        rr[0] = (rr[0] + 1) % 4
        return b

    rr8 = [0]

    def bank8():
        b = rr8[0]
        rr8[0] = (rr8[0] + 1) % 8
        return b

    def pc(name, j=0, rows=slice(0, 128)):
        o, n = pidx[name]
        return pcol[rows, o + j:o + j + 1]

    def _ap(x):
        if x is None or isinstance(x, (int, float)) or isinstance(x, bass.AP):
            return x
        return x[:]

    def MM(out, lhsT, rhs, start, stop, r, w):
        out, lhsT, rhs = _ap(out), _ap(lhsT), _ap(rhs)
        return A("pe", lambda h: h.matmul(out, lhsT=lhsT, rhs=rhs, start=start, stop=stop), r, w)

    def ACT(out, in_, func, r, w, bias=None, scale=None):
        kw = {}
        out, in_ = _ap(out), _ap(in_)
        if bias is not None:
            kw["bias"] = _ap(bias)
        if scale is not None:
            kw["scale"] = _ap(scale)
        return A("act", lambda h: h.activation(out=out, in_=in_, func=func, **kw), r, w)

    def TTOP(eng, out, in0, in1, op, r, w):
        out, in0, in1 = _ap(out), _ap(in0), _ap(in1)
        return A(eng, lambda h: h.tensor_tensor(out=out, in0=in0, in1=in1, op=op), r, w)

    def TS(eng, out, in0, s1, s2, op0, op1, r, w):
        out, in0, s1, s2 = _ap(out), _ap(in0), _ap(s1), _ap(s2)
        return A(eng, lambda h: h.tensor_scalar(out=out, in0=in0, scalar1=s1, scalar2=s2, op0=op0, op1=op1), r, w)

    def STT(out, in0, scalar, in1, op0, op1, r, w):
        out, in0, scalar, in1 = _ap(out), _ap(in0), _ap(scalar), _ap(in1)
        return A("dve", lambda h: h.scalar_tensor_tensor(out=out, in0=in0, scalar=scalar, in1=in1, op0=op0, op1=op1), r, w)

    def COPY(eng, out, in_, r, w):
        out, in_ = _ap(out), _ap(in_)
        if eng == "act":
            return A("act", lambda h: h.copy(out=out, in_=in_), r, w)
        return A(eng, lambda h: h.tensor_copy(out=out, in_=in_), r, w)

    def DMA(out, in_, r, w):
        out, in_ = _ap(out), _ap(in_)
        return A("sp", lambda h: h.dma_start(out=out, in_=in_), r, w, dma=True)

    def MEMSET(ap, val, w):
        return A("pool", lambda h: h.memset(ap, val), (), w)

    NSTG = 2
    stg_i = [0]

    def load_w(dram_ap, n, dst_flat, wkey, stg):
        p = 0
        while p < n:
            m = min(2048, n - p)
            s = stg_i[0] % NSTG
            stg_i[0] += 1
            DMA(stg[s][:, 0:m], dram_ap[:, p:p + m], (), [("stg", s)])
            COPY("pool", dst_flat[:, p:p + m], stg[s][:, 0:m], [("stg", s)], [wkey])
            p += m

    DMA(pcol[:, :], pcol_d[:, :], (), ["pcol"])
    MEMSET(ones_bf[:, :], 1.0, ["ones"])
    A("pool", lambda h: h.affine_select(out=ident_bf[:, :], in_=ones_bf[:, :], pattern=[[-1, 128]], compare_op=ALU.is_equal,
                                        fill=0.0, base=0, channel_multiplier=1), ["ones"], ["ident"])
    A("pool", lambda h: h.affine_select(out=tri_bf[:, :], in_=ones_bf[:, :], pattern=[[1, 128]], compare_op=ALU.is_ge,
                                        fill=0.0, base=0, channel_multiplier=-1), ["ones"], ["tri"])
    for hb in (0, 64):
        A("pool", lambda h, hb=hb: h.affine_select(out=tri64[hb:hb + 64, :], in_=ones_bf[hb:hb + 64, 0:64], pattern=[[1, 64]],
                                                   compare_op=ALU.is_ge, fill=0.0, base=0, channel_multiplier=-1), ["ones"], [("tri64", hb)])
    MEMSET(blk_bf[:, :], 0.0, ["blk"])
    MEMSET(blk_bf[0:64, 0:64], 1.0, ["blk"])
    MEMSET(blk_bf[64:128, 64:128], 1.0, ["blk"])

    def dump(name, ap, key, shape, dt=F32):
        d = dram("dbg_" + name, shape, dt, "ExternalOutput")
        return DMA(d, ap, [key], [("dbgout", name)])

    def rstd_from_ss(ssbank, width, inv_n, tmp, rstd, keyt, keyr, rows=slice(0, 128)):
        ACT(tmp, ps[ssbank][rows, 0:width], AF.Ln, [("ps", ssbank), "pcol"], [keyt], bias=pc("eps", 0, rows), scale=inv_n)
        ACT(rstd, tmp, AF.Exp, [keyt], [keyr], scale=-0.5)

    def norm_tile(tt, src, skey, gname, dst, dkey, sq, tmp, rstd):
        c = slice(tt * TT, (tt + 1) * TT)
        b = bank()
        for kc in range(KC):
            s_ = kc % 2
            ACT(sq[s_], src[:, kc, c], AF.Square, [(skey, kc, tt)], [("sq", id(sq), s_)])
            MM(ps[b][:, :], ones_bf[:, :], sq[s_], kc == 0, kc == KC - 1, [("sq", id(sq), s_), "ones"], [("ps", b)])
        rstd_from_ss(b, TT, 1.0 / D, tmp, rstd, ("ntmp", id(tmp)), ("nrstd", id(rstd)))
        for kc in range(KC):
            STT(dst[:, kc, c], src[:, kc, c], pc(gname, kc), rstd, ALU.mult, ALU.mult,
                [(skey, kc, tt), ("nrstd", id(rstd)), "pcol"], [(dkey, kc, tt)])

    def norm_full(src, skey, gname, dst, dkey, sq, tmp, rstd):
        for tt in range(NT):
            norm_tile(tt, src, skey, gname, dst, dkey, sq, tmp, rstd)

    def mem_prep(l):
        o = 0
        memT = fsb("memT", [128, KC, MEM], F32, o); o += KC * MEM * 4
        memn = fsb("memn", [128, KC, MEM], BF16, o); o += KC * MEM * 2
        wm = fsb("wm", [128, KC, 512], BF16, o); o += KC * 512 * 2
        stg = [fsb("mstg%d" % i, [128, 2048], F32, o + i * 8192) for i in range(NSTG)]; o += NSTG * 8192
        sq = [fsb("msq%d" % i, [128, MEM], BF16, o + i * 512) for i in range(2)]; o += 1024
        tmp = fsb("mtmp", [128, MEM], F32, o); o += 1024
        rstd = fsb("mrstd", [128, MEM], F32, o); o += 1024
        kraw = fsb("mkraw", [128, MEM], F32, o); o += 1024
        for kc in range(KC):
            DMA(memT[:, kc, :], memT_d[kc * 128:(kc + 1) * 128, :], (), [("memT", kc)])
        load_w(wmem_d[l], 4096, wm[:, :, :].rearrange("p k n -> p (k n)"), "wm", stg)
        b = bank()
        for kc in range(KC):
            s = kc % 2
            ACT(sq[s], memT[:, kc, :], AF.Square, [("memT", kc)], [("msq", s)])
            MM(ps[b][:, 0:MEM], ones_bf[:, :], sq[s], kc == 0, kc == KC - 1, [("msq", s), "ones"], [("ps", b)])
        rstd_from_ss(b, MEM, 1.0 / D, tmp, rstd, "mtmp", "mrstd")
        for kc in range(KC):
            STT(memn[:, kc, :], memT[:, kc, :], pc("memg%d" % l, kc), rstd, ALU.mult, ALU.mult,
                [("memT", kc), "mrstd", "pcol"], [("memn", kc)])
        for j in range(2):
            b = bank()
            for kc in range(KC):
                MM(ps[b][:, 0:MEM], wm[:, kc, j * 128:(j + 1) * 128], memn[:, kc, :], kc == 0, kc == KC - 1,
                   ["wm", ("memn", kc)], [("ps", b)])
            COPY("act", kraw, ps[b][:, 0:MEM], [("ps", b)], ["mkraw"])
            ACT(sq[0], ps[b][:, 0:MEM], AF.Square, [("ps", b)], [("msq", 0)])
            b2 = bank()
            MM(ps[b2][:, 0:MEM], blk_bf[:, :], sq[0], True, True, [("msq", 0), "blk"], [("ps", b2)])
            rstd_from_ss(b2, MEM, 1.0 / 64, tmp, rstd, "mtmp", "mrstd")
            STT(kn_m[:, j, :], kraw, pc("mkg%d" % l), rstd, ALU.mult, ALU.mult, ["mkraw", "mrstd", "pcol"], [("kn_m", j)])
        for mt in range(2):
            b = bank()
            for kc in range(KC):
                MM(ps[b][:, 0:256], memn[:, kc, mt * 128:(mt + 1) * 128], wm[:, kc, 256:512], kc == 0, kc == KC - 1,
                   ["wm", ("memn", kc)], [("ps", b)])
            COPY("act", vm_m[:, mt, :], ps[b][:, 0:256], [("ps", b)], [("vm_m", mt)])

    def mem_attn(l, qw, qwkey, o):
        qraw = fsb("qraw%d" % l, [128, TT], F32, o); o += TT * 4
        sq = fsb("qsq%d" % l, [128, TT], BF16, o); o += TT * 2
        tmp = fsb("qtmp%d" % l, [128, TT], F32, o); o += TT * 4
        rstd = fsb("qrstd%d" % l, [128, TT], F32, o); o += TT * 4
        qn = [fsb("qn%d_%d" % (l, i), [128, TT], BF16, o + i * TT * 2) for i in range(2)]; o += 2 * TT * 2
        pt = [fsb("pt%d_%d" % (l, i), [128, TT], BF16, o + i * TT * 2) for i in range(4)]; o += 4 * TT * 2
        rec2 = [fsb("rec%d_%d" % (l, i), [128, TT], F32, o + i * TT * 4) for i in range(2)]; o += 2 * TT * 4
        pti = [0]
        pend_pv = []

        def pv_unit(tt, j, e, pts):
            c = slice(tt * TT, (tt + 1) * TT)
            hh = 2 * j + e
            pb = 64 * e
            bo = 4 + 2 * ((tt * 2 + j) % 2)
            bd = bo + 1
            for mt in range(2):
                MM(ps[bo][pb:pb + 64, :], vm_m[:, mt, hh * 64:(hh + 1) * 64], pt[pts[mt]], mt == 0, mt == 1,
                   [("vm_m", mt), ("pt", pts[mt])], [("ps", bo, e)])
            for mt in range(2):
                MM(ps[bd][pb:pb + 64, :], ones_bf[:, 0:64], pt[pts[mt]], mt == 0, mt == 1,
                   ["ones", ("pt", pts[mt])], [("ps", bd, e)])
            rk = ("rec", (tt * 2 + j) % 2, e)
            rc = rec2[(tt * 2 + j) % 2]
            A("dve", lambda h, bd=bd, pb=pb, rc=rc: h.reciprocal(out=rc[pb:pb + 64, :], in_=ps[bd][pb:pb + 64, :]),
              [("ps", bd, e)], [rk])
            TTOP("dve", yT[pb:pb + 64, 6 + j, c], ps[bo][pb:pb + 64, :], rc[pb:pb + 64, :], ALU.mult,
                 [("ps", bo, e), rk], [("yT", 6 + j, tt, e)])

        for tt in range(NT):
            c = slice(tt * TT, (tt + 1) * TT)
            for j in range(2):
                b = bank()
                for kc in range(KC):
                    MM(ps[b][:, :], qw[:, kc, j * 128:(j + 1) * 128], hT[:, kc, c], kc == 0, kc == KC - 1,
                       [qwkey, ("hT", kc, tt)], [("ps", b)])
                COPY("act", qraw, ps[b][:, :], [("ps", b)], ["qraw"])
                ACT(sq, ps[b][:, :], AF.Square, [("ps", b)], ["qsq"])
                b2 = bank()
                MM(ps[b2][:, :], blk_bf[:, :], sq, True, True, ["qsq", "blk"], [("ps", b2)])
                rstd_from_ss(b2, TT, 1.0 / 64, tmp, rstd, "qtmp", "qrstd")
                STT(qn[j], qraw, pc("mqg%d" % l), rstd, ALU.mult, ALU.mult, ["qraw", "qrstd", "pcol"], [("qn", j)])
                for e in range(2):
                    hh = 2 * j + e
                    pb = 64 * e
                    pts = []
                    for mt in range(2):
                        b = bank()
                        MM(ps[b][:, :], kn_m[pb:pb + 64, j, mt * 128:(mt + 1) * 128], qn[j][pb:pb + 64, :], True, True,
                           [("kn_m", j), ("qn", j)], [("ps", b)])
                        pi = pti[0] % 4
                        pti[0] += 1
                        ACT(pt[pi], ps[b][:, :], AF.Exp, [("ps", b)], [("pt", pi)], scale=0.125)
                        pts.append(pi)
                    pend_pv.append((tt, j, e, pts))
                    if len(pend_pv) > 1:
                        pv_unit(*pend_pv.pop(0))
        while pend_pv:
            pv_unit(*pend_pv.pop(0))
        return o

    def out_proj(l, o, tile_cb=None):
        wo = fsb("wo%d" % l, [128, KC, D], BF16, o); o += KC * D * 2
        stg = [fsb("ostg%d_%d" % (l, i), [128, 2048], F32, o + i * 8192) for i in range(NSTG)]; o += NSTG * 8192
        load_w(wout_d[l], KC * D, wo[:, :, :].rearrange("p k n -> p (k n)"), "wo", stg)
        for tt in range(NT):
            if tile_cb is not None and tt > 0:
                tile_cb(tt - 1)
            c = slice(tt * TT, (tt + 1) * TT)
            for d in range(KC):
                b = bank()
                for kc in range(KC):
                    MM(ps[b][:, :], wo[:, kc, d * 128:(d + 1) * 128], yT[:, kc, c], kc == 0, kc == KC - 1,
                       ["wo", ("yT", kc, tt)], [("ps", b)])
                TTOP("dve", xT[:, d, c], ps[b][:, :], xT[:, d, c], ALU.add, [("ps", b), ("xT", d, tt)], [("xT", d, tt)])
        if tile_cb is not None:
            tile_cb(NT - 1)
        return o

    def ffn_norm_cb(l, o):
        sq = [fsb("fsq%d_%d" % (l, i), [128, TT], BF16, o + i * TT * 2) for i in range(2)]; o += 2 * TT * 2
        tmp = fsb("ftmp%d" % l, [128, TT], F32, o); o += TT * 4
        rstd = fsb("frstd%d" % l, [128, TT], F32, o); o += TT * 4
        assert o <= FSZ, (o, FSZ)
        return lambda tt: norm_tile(tt, xT, "xT", "ffng%d" % l, hT, "hT", sq, tmp, rstd)

    def ffn(l):
        G = 4
        o = 0
        NUP, NDN, NTR = 2, 3 * G, 4
        wu = [fsb("wu%d_%d" % (l, i), [128, KC, 256], BF16, o + i * 4096) for i in range(NUP)]; o += NUP * 4096
        wd = [fsb("wd%d_%d" % (l, i), [128, D], BF16, o + i * 2048) for i in range(NDN)]; o += NDN * 2048
        stg = [fsb("fstg%d_0" % l, [128, 2048], F32, o), fsb("fstg%d_1" % l, [128, 1024], F32, o + 8192)]; o += 8192 + 4096
        hal = [[fsb("hal%d_%d_%d" % (l, ab, i), [128, 8], F32, o + (ab * 2 + i) * 32) for i in range(2)] for ab in range(2)]
        o += 4 * 32
        tr = [[fsb("tr%d_%d_%d" % (l, ab, i), [128, TT], F32, o + (ab * NTR + i) * TT * 4) for i in range(NTR)] for ab in range(2)]
        o += 2 * NTR * TT * 4
        sa = [fsb("sa%d_%d" % (l, i), [128, TT], BF16, o + i * TT * 2) for i in range(2)]; o += 2 * TT * 2
        tb16 = [fsb("tb16_%d_%d" % (l, i), [128, TT], BF16, o + i * TT * 2) for i in range(NTR)]; o += NTR * TT * 2
        assert o <= FSZ, (o, FSZ)
        if dbg and l == 0:
            for ab in range(2):
                for i in range(NTR):
                    DBG["tr%d_%d" % (ab, i)] = (tr[ab][i], ("tr", ab, i), [128, TT], F32)
            for i in range(2):
                DBG["sa_%d" % i] = (sa[i], ("sa", i), [128, TT], BF16)
        actg = [nc.alloc_sbuf_tensor_at("actg%d_%d" % (l, i), [128, G, S], BF16,
                                        offset=BASE + KC * S * 4 + KC * S * 2 + i * G * S * 2) for i in range(2)]
        groups = [list(range(g, min(g + G, NPAIR))) for g in range(0, NPAIR, G)]
        dbank = [0]

        def dma_pair(i):
            DMA(stg[0][:, 0:2048], wup_d[l, i], (), [("stg", 0)])
            DMA(stg[1][:, 0:1024], wdn_d[l, i], (), [("stg", 1)])

        def cast_pair(i):
            COPY("act", wu[i % NUP][:, :, :].rearrange("p k n -> p (k n)"), stg[0][:, 0:2048], [("stg", 0)], [("wu", i % NUP)])
            COPY("act", wd[i % NDN][:, :], stg[1][:, 0:1024], [("stg", 1)], [("wd", i % NDN)])

        def down_mm(gi, d, tt):
            grp = groups[gi]
            ag = actg[gi % 2]
            c = slice(tt * TT, (tt + 1) * TT)
            b = 6 + dbank[0] % 2
            dbank[0] += 1
            for ii, i in enumerate(grp):
                MM(ps[b][:, :], wd[i % NDN][:, d * 128:(d + 1) * 128], ag[:, ii, c], ii == 0, ii == len(grp) - 1,
                   [("wd", i % NDN), ("actg", gi % 2, ii, tt)], [("ps", b)])
            return (b, d, tt)

        def down_add(b, d, tt):
            c = slice(tt * TT, (tt + 1) * TT)
            TTOP("dve", xT[:, d, c], ps[b][:, :], xT[:, d, c], ALU.add, [("ps", b), ("xT", d, tt)], [("xT", d, tt)])

        def stage_d(k, gi, ii, tt):
            c = slice(tt * TT, (tt + 1) * TT)
            ACT(sa[k % 2], tr[0][k % NTR], AF.Silu, [("tr", 0, k % NTR)], [("sa", k % 2)])
            TTOP("dve", actg[gi % 2][:, ii, c], sa[k % 2], tb16[k % NTR], ALU.mult,
                 [("sa", k % 2), ("tb16", k % NTR)], [("actg", gi % 2, ii, tt)])

        dma_pair(0)
        cast_pair(0)
        k = 0
        pend_d = []
        pend_down = []
        pend_add = []
        for gi, grp in enumerate(groups):
            niter = len(grp) * NT
            it = 0
            for ii, i in enumerate(grp):
                if i + 1 < NPAIR:
                    dma_pair(i + 1)
                wus = i % NUP
                for tt in range(NT):
                    c = slice(tt * TT, (tt + 1) * TT)
                    banks = [(k % 3) * 2, (k % 3) * 2 + 1]
                    while pend_add:
                        down_add(*pend_add.pop(0))
                    for ab in range(2):
                        b = banks[ab]
                        for kc in range(KC):
                            MM(ps[b][:, :], wu[wus][:, kc, ab * 128:(ab + 1) * 128], hT[:, kc, c], kc == 0, kc == KC - 1,
                               [("wu", wus), ("hT", kc, tt)], [("ps", b)])
                    for ab in range(2):
                        b = banks[ab]
                        col = ab * NPAIR + i
                        if tt + 1 < NT:
                            COPY("act", hal[ab][tt % 2][:, 0:2], ps[b][:, TT - 2:TT], [("ps", b)], [("hal", ab, tt % 2)])
                        ACT(tr[ab][k % NTR], ps[b][:, :], AF.Identity, [("ps", b), "pcol"], [("tr", ab, k % NTR)],
                            bias=pc("cb%d" % l, col), scale=pc("cw2_%d" % l, col))
                    for step in range(4):
                        for ab in range(2):
                            b = banks[ab]
                            col = ab * NPAIR + i
                            t = tr[ab][k % NTR]
                            tk = ("tr", ab, k % NTR)
                            if step == 0:
                                STT(t[:, 1:TT], ps[b][:, 0:TT - 1], pc("cw1_%d" % l, col), t[:, 1:TT], ALU.mult, ALU.add,
                                    [("ps", b), tk, "pcol"], [tk])
                            elif step == 1:
                                dst, dk = (t, tk) if ab == 0 else (tb16[k % NTR], ("tb16", k % NTR, "m"))
                                STT(dst[:, 2:TT], ps[b][:, 0:TT - 2], pc("cw0_%d" % l, col), t[:, 2:TT], ALU.mult, ALU.add,
                                    [("ps", b), tk, "pcol"], [dk])
                            elif tt > 0:
                                hp = hal[ab][(tt - 1) % 2]
                                hk = ("hal", ab, (tt - 1) % 2)
                                if step == 2:
                                    STT(t[:, 0:1], hp[:, 1:2], pc("cw1_%d" % l, col), t[:, 0:1], ALU.mult, ALU.add, [hk, tk, "pcol"], [tk])
                                else:
                                    dst, dk = (t, tk) if ab == 0 else (tb16[k % NTR], ("tb16", k % NTR, "h"))
                                    STT(dst[:, 0:2], hp[:, 0:2], pc("cw0_%d" % l, col), t[:, 0:2], ALU.mult, ALU.add, [hk, tk, "pcol"], [dk])
                            elif step == 3 and ab == 1:
                                COPY("dve", tb16[k % NTR][:, 0:2], t[:, 0:2], [tk], [("tb16", k % NTR, "h")])
                    if tt == min(1, NT - 1) and i + 1 < NPAIR:
                        cast_pair(i + 1)
                    pend_d.append((k, gi, ii, tt))
                    if len(pend_d) > 2:
                        stage_d(*pend_d.pop(0))
                    if pend_down:
                        n_left = niter - it
                        take = -(-len(pend_down) // max(1, n_left))
                        for _ in range(take):
                            if len(pend_add) >= 2:
                                down_add(*pend_add.pop(0))
                            pend_add.append(down_mm(*pend_down.pop(0)))
                    k += 1
                    it += 1
            while pend_down:
                if len(pend_add) >= 2:
                    down_add(*pend_add.pop(0))
                pend_add.append(down_mm(*pend_down.pop(0)))
            pend_down = [(gi, d, tt) for tt in range(NT) for d in range(KC)]
        while pend_d:
            stage_d(*pend_d.pop(0))
        while pend_down:
            if len(pend_add) >= 2:
                down_add(*pend_add.pop(0))
            pend_add.append(down_mm(*pend_down.pop(0)))
        while pend_add:
            down_add(*pend_add.pop(0))

    def hgrn():
        o = 0
        e3 = fsb("lb_e", [128, 18], F32, o); o += 128
        ssum = fsb("lb_s", [128, 6], F32, o); o += 64
        srec = fsb("lb_r", [128, 6], F32, o); o += 64
        o0, _ = pidx["lbl0"]
        ACT(e3[:, :], pcol[:, o0:o0 + 18], AF.Exp, ["pcol"], ["lb_e"])
        TTOP("dve", ssum[:, :], e3[:, 0:6], e3[:, 6:12], ALU.add, ["lb_e"], ["lb_s"])
        TTOP("dve", ssum[:, :], ssum[:, :], e3[:, 12:18], ALU.add, ["lb_e", "lb_s"], ["lb_s"])
        A("dve", lambda h: h.reciprocal(out=srec[:, :], in_=ssum[:, :]), ["lb_s"], ["lb_r"])
        TTOP("dve", lbc[:, 0:6], e3[:, 0:6], srec[:, :], ALU.mult, ["lb_e", "lb_r"], [("lbc", 0)])
        TS("dve", lbc[:, 6:12], lbc[:, 0:6], -1.0, 1.0, ALU.mult, ALU.add, [("lbc", 0)], [("lbc", 1)])
        o = 256
        wh2 = [fsb("wh%d" % i, [128, KC, 512], BF16, o + i * KC * 512 * 2) for i in range(2)]; o += 2 * KC * 512 * 2
        stg = [fsb("hstg%d" % i, [128, 2048], F32, o + i * 8192) for i in range(NSTG)]; o += NSTG * 8192
        QT = fsb("QT", [128, S], BF16, o); o += S * 2
        KT = fsb("KT", [128, S], BF16, o); o += S * 2
        Ktok = fsb("Ktok", [128, NB, 128], BF16, o); o += S * 2
        vtok = fsb("vtok", [128, NB, 128], BF16, o); o += S * 2
        gT = fsb("gT", [128, S], BF16, o); o += S * 2
        Bl = fsb("Bl", [128, NCH], F32, o); o += NCH * 4
        mscan = fsb("mscan", [128, TT], F32, o); o += TT * 4
        tA = [fsb("tA%d" % i, [128, TT], F32, o + i * TT * 4) for i in range(6)]; o += 6 * TT * 4
        fR = fsb("fR", [128, TT], F32, o); o += TT * 4
        fP = fsb("fP", [128, TT], F32, o); o += TT * 4
        Tst = [fsb("Tst%d" % i, [128, 128], F32, o + i * 512) for i in range(2)]; o += 1024
        stb = [fsb("stb%d" % i, [128, 128], BF16, o + i * 256) for i in range(3)]; o += 768
        aT = [fsb("aT%d" % i, [128, 64], BF16, o + i * 128) for i in range(2)]; o += 256
        sqo = fsb("sqo", [128, TT], BF16, o); o += TT * 2
        assert o <= FSZ, (o, FSZ)
        MEMSET(mscan[:, :], 1.0, ["mscan"])
        MEMSET(mscan[:, :].rearrange("p (n c) -> p n c", c=64)[:, :, 0:1], 0.0, ["mscan"])
        qs, sg, fg, lf, bb, eb = tA
        prr = [0]
        rcr = [0]

        def pbank():
            b = prr[0]
            prr[0] = (prr[0] + 1) % 3
            return b

        def rbank():
            b = (3, 6, 7)[rcr[0]]
            rcr[0] = (rcr[0] + 1) % 3
            return b

        def load_head(hh):
            load_w(w0_d[hh], 4096, wh2[hh % 2][:, :, :].rearrange("p k n -> p (k n)"), ("wh", hh % 2), stg)

        def pre_stages(hh, tt):
            wh = wh2[hh % 2]
            wk = ("wh", hh % 2)
            c = slice(tt * TT, (tt + 1) * TT)
            st = {}

            def P1():
                st["bf"] = pbank()
                for kc in range(KC):
                    MM(ps[st["bf"]][:, :], wh[:, kc, 128:256], hT[:, kc, c], kc == 0, kc == KC - 1, [wk, ("hT", kc, tt)], [("ps", st["bf"])])
                st["bq"] = pbank()
                for kc in range(KC):
                    MM(ps[st["bq"]][:, :], wh[:, kc, 0:128], hT[:, kc, c], kc == 0, kc == KC - 1, [wk, ("hT", kc, tt)], [("ps", st["bq"])])

            def P2():
                ACT(sg, ps[st["bf"]][:, :], AF.Sigmoid, [("ps", st["bf"])], ["sg"])
                st["bg"] = pbank()
                for kc in range(KC):
                    MM(ps[st["bg"]][:, :], wh[:, kc, 256:384], hT[:, kc, c], kc == 0, kc == KC - 1, [wk, ("hT", kc, tt)], [("ps", st["bg"])])

            def P3():
                TS("dve", fg, sg, lbc[:, 6 + hh:7 + hh], lbc[:, hh:hh + 1], ALU.mult, ALU.add, ["sg", "lbc"], ["fg"])

            def P4():
                ACT(lf, fg, AF.Ln, ["fg"], ["lf"])
                st["bv"] = pbank()
                for j in range(4):
                    tb = tt * 4 + j
                    for kc in range(KC):
                        MM(ps[st["bv"]][:, j * 128:(j + 1) * 128], hT[:, kc, tb * 128:(tb + 1) * 128], wh[:, kc, 384:512], kc == 0, kc == KC - 1,
                           [wk, ("hT", kc, tt)], [("ps", st["bv"])])

            def P5():
                A("dve", lambda h: h.tensor_tensor_scan(out=bb[:, :], data0=mscan[:, :], data1=lf[:, :], initial=0.0, op0=ALU.mult, op1=ALU.add),
                  ["mscan", "lf"], ["bb"])
                ACT(qs, ps[st["bq"]][:, :], AF.Silu, [("ps", st["bq"])], ["qs"])
                ACT(gT[:, c], ps[st["bg"]][:, :], AF.Silu, [("ps", st["bg"])], [("gT", tt)])

            def P6():
                ACT(eb, bb, AF.Exp, ["bb"], ["eb"])
                ACT(sg, bb, AF.Exp, ["bb"], ["sg"], scale=-1.0)
                COPY("act", vtok[:, tt * 4:(tt + 1) * 4, :].rearrange("p j n -> p (j n)"), ps[st["bv"]][:, :], [("ps", st["bv"])], [("vtok", tt)])

            def P7():
                STT(QT[:, c], qs, 128.0 ** -0.5, eb, ALU.mult, ALU.mult, ["qs", "eb"], [("QT", tt)])
                COPY("dve", Bl[:, tt * 8:(tt + 1) * 8], eb[:, :].rearrange("p (n c) -> p n c", c=64)[:, :, 63], ["eb"], [("Bl", tt)])
                TS("dve", lf, fg, -1.0, 1.0, ALU.mult, ALU.add, ["fg"], ["lf"])
                TTOP("dve", KT[:, c], lf, sg, ALU.mult, ["lf", "sg"], [("KT", tt)])

            def P8():
                b = pbank()
                pb16 = ps[b][:, :].bitcast(BF16)
                for j in range(4):
                    tb = tt * 4 + j
                    A("pe", lambda h, j=j, tb=tb, pb16=pb16: h.transpose(out=pb16[:, j * 128:(j + 1) * 128], in_=KT[:, tb * 128:(tb + 1) * 128],
                                                                           identity=ident_bf[:, :]),
                      [("KT", tt), "ident"], [("ps", b)])
                COPY("act", Ktok[:, tt * 4:(tt + 1) * 4, :].rearrange("p j n -> p (j n)"), pb16[:, 0:512], [("ps", b)], [("Ktok", tt)])

            return [P1, P2, P3, P4, P5, P6, P7, P8]

        BO = 4

        def rec_step(hh, n):
            if n < NCH:
                base = (n % 2) * 64
                blkI = n // 2
                tt = n // 8
                cs = slice(n * 64, (n + 1) * 64)
                bA = rbank()
                MM(ps[bA][base:base + 64, 0:64], KT[:, cs], QT[:, cs], True, True, [("KT", tt), ("QT", tt)], [("ps", bA)])
                a = aT[n % 2]
                TTOP("dve", a[base:base + 64, :], ps[bA][base:base + 64, 0:64], tri64[base:base + 64, :], ALU.mult,
                     [("ps", bA), "tri64"], [("aT", n % 2)])
                bU = rbank()
                MM(ps[bU][:, 0:128], Ktok[base:base + 64, blkI, :], vtok[base:base + 64, blkI, :], True, True,
                   [("Ktok", tt), ("vtok", tt)], [("ps", bU)])
                if n == 0:
                    COPY("dve", Tst[0][:, :], ps[bU][:, 0:128], [("ps", bU)], [("Tst", 0)])
                else:
                    STT(Tst[n % 2][:, :], Tst[(n - 1) % 2][:, :], Bl[:, n - 1:n], ps[bU][:, 0:128], ALU.mult, ALU.add,
                        [("Tst", (n - 1) % 2), ("Bl", (n - 1) // 8), ("ps", bU)], [("Tst", n % 2)])
                if n + 1 < NCH:
                    TS("pool", stb[(n + 1) % 3][:, :], Tst[n % 2][:, :], Bl[:, n:n + 1], None, ALU.mult, ALU.bypass,
                       [("Tst", n % 2), ("Bl", n // 8)], [("stb", (n + 1) % 3)])
            if n > 0:
                m = n - 1
                mbase = (m % 2) * 64
                mblk = m // 2
                mtt = m // 8
                mcs = slice(m * 64, (m + 1) * 64)
                bo = BO + (mtt % 2)
                oc = slice((m % 8) * 64, (m % 8 + 1) * 64)
                MM(ps[bo][:, oc], vtok[mbase:mbase + 64, mblk, :], aT[m % 2][mbase:mbase + 64, :], True, m == 0,
                   [("vtok", mtt), ("aT", m % 2)], [("ps", bo, m % 8)])
                if m > 0:
                    MM(ps[bo][:, oc], stb[m % 3][:, :], QT[:, mcs], False, True, [("stb", m % 3), ("QT", mtt)], [("ps", bo, m % 8)])
                if m % 8 == 7:
                    c = slice(mtt * TT, (mtt + 1) * TT)
                    ACT(sqo, ps[bo][:, :], AF.Square, [("ps", bo)], ["sqo"])
                    b2 = rbank()
                    MM(ps[b2][:, :], ones_bf[:, :], sqo, True, True, ["sqo", "ones"], [("ps", b2)])
                    rstd_from_ss(b2, TT, 1.0 / 128, fR, fR, "fR", "fR")
                    STT(fP, ps[bo][:, :], pc("hgo"), fR, ALU.mult, ALU.mult, [("ps", bo), "fR", "pcol"], ["fP"])
                    TTOP("dve", yT[:, hh, c], fP, gT[:, c], ALU.mult, ["fP", ("gT", mtt)], [("yT", hh, mtt)])

        load_head(0)
        tiles = [(hh, tt) for hh in range(6) for tt in range(NT)]
        for f_ in pre_stages(*tiles[0]):
            f_()
        for i, (hh, tt) in enumerate(tiles):
            if tt == 0 and hh + 1 < 6:
                load_head(hh + 1)
            steps = [(hh, n) for n in range(tt * 8, tt * 8 + 8)]
            if tt == NT - 1:
                steps.append((hh, NCH))
            stages = pre_stages(*tiles[i + 1]) if i + 1 < len(tiles) else []
            for j in range(max(len(steps), len(stages))):
                if j < len(steps):
                    rec_step(*steps[j])
                if j < len(stages):
                    stages[j]()

    def mla_layer():
        mixer_norm(1, 65536 if FSZ >= 65536 + 6144 else 0)
        if FSZ < 65536 + 6144:
            sch.barrier()
        mem_prep(1)
        sch.barrier()
        o = 0
        cqn = fsb("cqn", [128, 3, S], BF16, o); o += 3 * S * 2
        ckvn = fsb("ckvn", [128, 2, S], BF16, o); o += 2 * S * 2
        kpe = fsb("kpe", [128, S], BF16, o); o += S * 2
        kpes = fsb("kpes", [128, S], BF16, o); o += S * 2
        cosT = fsb("cosT", [128, S], BF16, o); o += S * 2
        sinT = fsb("sinT", [128, S], BF16, o); o += S * 2
        qw1 = fsb("qw1", [128, KC, 256], BF16, o); o += KC * 256 * 2
        P0 = o
        pi32 = fsb("pi32", [128, TT], I32, o); o += TT * 4
        ang = fsb("ang", [128, TT], F32, o); o += TT * 4
        kf = fsb("kf", [128, TT], F32, o); o += TT * 4
        rr_ = fsb("rr_", [128, TT], F32, o); o += TT * 4
        msk = fsb("msk", [128, TT], F32, o); o += TT * 4
        TWO_PI = 2.0 * np.pi
        C1 = 6.28125
        C2 = TWO_PI - C1
        for tt in range(NT):
            c = slice(tt * TT, (tt + 1) * TT)
            DMA(pi32[:, :], pos_d[:, c].partition_broadcast(128), (), ["pi32"])
            for which in range(2):
                COPY("dve", ang, pi32, ["pi32"], ["ang"])
                if which == 0:
                    TS("dve", ang, ang, pc("invf"), pc("halfpi"), ALU.mult, ALU.add, ["ang", "pcol"], ["ang"])
                else:
                    TS("dve", ang, ang, pc("invf"), None, ALU.mult, ALU.bypass, ["ang", "pcol"], ["ang"])
                TS("dve", kf, ang, 1.0 / TWO_PI, None, ALU.mult, ALU.bypass, ["ang"], ["kf"])
                COPY("dve", pi32, kf, ["kf"], ["pi32"])
                COPY("dve", kf, pi32, ["pi32"], ["kf"])
                STT(rr_, kf, -C1, ang, ALU.mult, ALU.add, ["kf", "ang"], ["rr_"])
                STT(rr_, kf, -C2, rr_, ALU.mult, ALU.add, ["kf", "rr_"], ["rr_"])
                TS("dve", msk, rr_, float(np.pi), -TWO_PI, ALU.is_gt, ALU.mult, ["rr_"], ["msk"])
                TTOP("dve", rr_, rr_, msk, ALU.add, ["rr_", "msk"], ["rr_"])
                TS("dve", msk, rr_, float(-np.pi), TWO_PI, ALU.is_lt, ALU.mult, ["rr_"], ["msk"])
                TTOP("dve", rr_, rr_, msk, ALU.add, ["rr_", "msk"], ["rr_"])
                TS("dve", rr_, rr_, float(np.pi), float(-np.pi), ALU.min, ALU.max, ["rr_"], ["rr_"])
                if which == 0:
                    ACT(cosT[:, c], rr_, AF.Sin, ["rr_"], [("cosT", tt)])
                else:
                    ACT(kf, rr_, AF.Sin, ["rr_"], ["kf"])
                    TS("dve", sinT[:, c], kf, pc("sgn"), None, ALU.mult, ALU.bypass, ["kf", "pcol"], [("sinT", tt)])
                if which == 0:
                    DMA(pi32[:, :], pos_d[:, c].partition_broadcast(128), (), ["pi32"])
        wg = fsb("w1g", [128, 3, KC, 128], BF16, o); o += 3 * KC * 128 * 2
        stg = [fsb("bstg%d" % i, [128, 1024], F32, o + i * 4096) for i in range(NSTG)]; o += NSTG * 4096
        sq = [fsb("bsq%d" % i, [128, TT], BF16, o + i * TT * 2) for i in range(2)]; o += 2 * TT * 2
        tmp = fsb("btmp", [128, TT], F32, o); o += TT * 4
        rstd = fsb("brstd", [128, TT], F32, o); o += TT * 4
        assert o <= FSZ, (o, FSZ)
        groups = [([0, 1, 2], cqn, "cqn", "qag", 384), ([3, 4], ckvn, "ckvn", "kvag", 256)]
        for tiles, dst, dkey, gname, width in groups:
            for gi, t in enumerate(tiles):
                load_w(w1in_d[t], 1024, wg[:, gi, :, :].rearrange("p k n -> p (k n)"), ("w1g", gi), stg)
            for tt in range(NT):
                c = slice(tt * TT, (tt + 1) * TT)
                bs = 6 + tt % 2
                for gi, t in enumerate(tiles):
                    b = (tt % 2) * 3 + gi
                    for kc in range(KC):
                        MM(ps[b][:, :], wg[:, gi, kc, :], hT[:, kc, c], kc == 0, kc == KC - 1, [("w1g", gi), ("hT", kc, tt)], [("ps", b)])
                    ACT(sq[gi % 2], ps[b][:, :], AF.Square, [("ps", b)], [("bsq", gi % 2)])
                    MM(ps[bs][:, :], ones_bf[:, :], sq[gi % 2], gi == 0, gi == len(tiles) - 1, [("bsq", gi % 2), "ones"], [("ps", bs)])
                rstd_from_ss(bs, TT, 1.0 / width, tmp, rstd, "btmp", "brstd")
                for gi, t in enumerate(tiles):
                    STT(dst[:, gi, c], ps[(tt % 2) * 3 + gi][:, :], pc(gname, gi), rstd, ALU.mult, ALU.mult,
                        [("ps", (tt % 2) * 3 + gi), "brstd", "pcol"], [(dkey, gi, tt)])
        for t, dst, dkey in ((5, kpe, "kpe"), (6, kpes, "kpes")):
            load_w(w1in_d[t], 1024, wg[:, 0, :, :].rearrange("p k n -> p (k n)"), ("w1g", 0), stg)
            for tt in range(NT):
                c = slice(tt * TT, (tt + 1) * TT)
                b = bank()
                for kc in range(KC):
                    MM(ps[b][:, :], wg[:, 0, kc, :], hT[:, kc, c], kc == 0, kc == KC - 1, [("w1g", 0), ("hT", kc, tt)], [("ps", b)])
                COPY("act", dst[:, c], ps[b][:, :], [("ps", b)], [(dkey, tt)])
        for j, t in enumerate((7, 8)):
            load_w(w1in_d[t], 1024, wg[:, 1, :, :].rearrange("p k n -> p (k n)"), ("w1g", 1), stg)
            for kc in range(KC):
                COPY("pool", qw1[:, kc, j * 128:(j + 1) * 128], wg[:, 1, kc, :], [("w1g", 1)], [("qw1", j)])
        H_ = slice(0, 64)
        for tt in range(NT):
            c = slice(tt * TT, (tt + 1) * TT)
            STT(tmp[H_, :], kpe[H_, c], pc("kg_r", 0, H_), cosT[H_, c], ALU.mult, ALU.mult, [("kpe", tt), ("cosT", tt), "pcol"], ["btmp"])
            STT(rstd[H_, :], kpes[H_, c], pc("kg_rs", 0, H_), sinT[H_, c], ALU.mult, ALU.mult, [("kpes", tt), ("sinT", tt), "pcol"], ["brstd"])
            ACT(kpes[H_, c], kpe[H_, c], AF.Square, [("kpe", tt)], [("kpes", tt)])
            TTOP("dve", kpe[H_, c], tmp[H_, :], rstd[H_, :], ALU.add, ["btmp", "brstd"], [("kpe", tt)])
        sch.barrier()
        mem_attn(1, qw1, "qw1", P0)
        sch.barrier()
        HB = BASE + KC * S * 4
        ho = [HB]

        def hsb(name, shape, dt):
            nb = int(np.prod(shape[1:])) * (4 if dt == F32 else 2)
            t = nc.alloc_sbuf_tensor_at(name, list(shape), dt, offset=ho[0])
            ho[0] += (nb + 63) // 64 * 64
            assert ho[0] <= HB + KC * S * 2, (name, ho[0] - HB)
            return t
        Kn = hsb("Kn", [128, S], BF16)
        Kr = hsb("Kr", [128, S], BF16)
        Vh = hsb("Vh", [128, NB, 128], BF16)
        Qn = hsb("Qn", [128, S], BF16)
        Qr = hsb("Qr", [128, S], BF16)
        wq2 = [hsb("wq%d" % i, [128, 3, 256], BF16) for i in range(2)]
        wkv2 = [hsb("wkv%d" % i, [128, 2, 256], BF16) for i in range(2)]
        o = P0
        pt = [fsb("apt%d" % i, [128, TT], BF16, o + i * TT * 2) for i in range(4)]; o += 4 * TT * 2
        stg = [fsb("cstg%d" % i, [128, 2048], F32, o + i * 8192) for i in range(NSTG)]; o += NSTG * 8192
        sq = [fsb("csq%d" % i, [128, TT], BF16, o + i * TT * 2) for i in range(2)]; o += 2 * TT * 2
        tmp = fsb("ctmp", [128, TT], F32, o); o += TT * 4
        rstd = fsb("crstd", [128, TT], F32, o); o += TT * 4
        r1 = fsb("cr1", [128, TT], F32, o); o += TT * 4
        r2 = fsb("cr2", [128, TT], F32, o); o += TT * 4
        rec = fsb("crec", [128, TT], F32, o); o += TT * 4
        assert o <= FSZ, (o, FSZ)
        H = slice(0, 64)
        SC = 192.0 ** -0.5
        pti = [0]

        def rope_part(raw, rawkeys, g_r, g_rs, dst, dkey, tt):
            c = slice(tt * TT, (tt + 1) * TT)
            STT(r1[H, :], raw[0], pc(g_r, 0, H), rstd[H, :], ALU.mult, ALU.mult, rawkeys + ["crstd", "pcol"], ["cr1"])
            STT(r2[H, :], raw[1], pc(g_rs, 0, H), rstd[H, :], ALU.mult, ALU.mult, rawkeys + ["crstd", "pcol"], ["cr2"])
            TTOP("dve", r1[H, :], r1[H, :], cosT[H, c], ALU.mult, ["cr1", ("cosT", tt)], ["cr1"])
            TTOP("dve", r2[H, :], r2[H, :], sinT[H, c], ALU.mult, ["cr2", ("sinT", tt)], ["cr2"])
            TTOP("dve", dst[H, c], r1[H, :], r2[H, :], ALU.add, ["cr1", "cr2"], [(dkey, tt)])

        def load_mla_head(hh):
            load_w(wuq_d[hh], 768, wq2[hh % 2][:, :, :].rearrange("p k n -> p (k n)"), ("wq", hh % 2), stg)
            load_w(wukv_d[hh], 512, wkv2[hh % 2][:, :, :].rearrange("p k n -> p (k n)"), ("wkv", hh % 2), stg)

        load_mla_head(0)
        for hh in range(6):
            wq, wkv = wq2[hh % 2], wkv2[hh % 2]
            wqk, wkvk = ("wq", hh % 2), ("wkv", hh % 2)
            if hh + 1 < 6:
                load_mla_head(hh + 1)
            for tt in range(NT):
                c = slice(tt * TT, (tt + 1) * TT)
                bK = bank8()
                for kc in range(2):
                    MM(ps[bK][:, :], wkv[:, kc, 0:128], ckvn[:, kc, c], kc == 0, kc == 1, [wkvk, ("ckvn", kc, tt)], [("ps", bK)])
                ACT(sq[0], ps[bK][:, :], AF.Square, [("ps", bK)], [("csq", 0)])
                bs = bank8()
                MM(ps[bs][:, :], ones_bf[:, :], sq[0], True, False, [("csq", 0), "ones"], [("ps", bs)])
                MM(ps[bs][:, :], ones_bf[H, :], kpes[H, c], False, True, [("kpes", tt), "ones"], [("ps", bs)])
                rstd_from_ss(bs, TT, 1.0 / 192, tmp, rstd, "ctmp", "crstd")
                STT(Kn[:, c], ps[bK][:, :], pc("kg_n"), rstd, ALU.mult, ALU.mult, [("ps", bK), "crstd", "pcol"], [("Kn", tt)])
                TTOP("dve", Kr[H, c], kpe[H, c], rstd[H, :], ALU.mult, [("kpe", tt), "crstd"], [("Kr", tt)])
                bV = bank8()
                for j in range(4):
                    tb = tt * 4 + j
                    for kc in range(2):
                        MM(ps[bV][:, j * 128:(j + 1) * 128], ckvn[:, kc, tb * 128:(tb + 1) * 128], wkv[:, kc, 128:256], kc == 0, kc == 1,
                           [wkvk, ("ckvn", kc, tt)], [("ps", bV)])
                COPY("act", Vh[:, tt * 4:(tt + 1) * 4, :].rearrange("p j n -> p (j n)"), ps[bV][:, :], [("ps", bV)], [("Vh", tt)])
                bQ = bank8()
                for kc in range(3):
                    MM(ps[bQ][:, :], wq[:, kc, 0:128], cqn[:, kc, c], kc == 0, kc == 2, [wqk, ("cqn", kc, tt)], [("ps", bQ)])
                bR = bank8()
                for kc in range(3):
                    MM(ps[bR][H, :], wq[:, kc, 128:192], cqn[:, kc, c], kc == 0, kc == 2, [wqk, ("cqn", kc, tt)], [("ps", bR)])
                bS2 = bank8()
                for kc in range(3):
                    MM(ps[bS2][H, :], wq[:, kc, 192:256], cqn[:, kc, c], kc == 0, kc == 2, [wqk, ("cqn", kc, tt)], [("ps", bS2)])
                ACT(sq[0], ps[bQ][:, :], AF.Square, [("ps", bQ)], [("csq", 0)])
                ACT(sq[1][H, :], ps[bR][H, :], AF.Square, [("ps", bR)], [("csq", 1)])
                bs = bank8()
                MM(ps[bs][:, :], ones_bf[:, :], sq[0], True, False, [("csq", 0), "ones"], [("ps", bs)])
                MM(ps[bs][:, :], ones_bf[H, :], sq[1][H, :], False, True, [("csq", 1), "ones"], [("ps", bs)])
                rstd_from_ss(bs, TT, 1.0 / 192, tmp, rstd, "ctmp", "crstd")
                STT(Qn[:, c], ps[bQ][:, :], pc("qg_n"), rstd, ALU.mult, ALU.mult, [("ps", bQ), "crstd", "pcol"], [("Qn", tt)])
                rope_part((ps[bR][H, :], ps[bS2][H, :]), [("ps", bR), ("ps", bS2)], "qg_r", "qg_rs", Qr, "Qr", tt)
            for qt in range(NT):
                c0 = qt * TT
                nkt = 4 * (qt + 1)
                par = (hh * NT + qt) % 2
                bo, bd = 4 + par, 6 + par
                pend_pv = []

                def pv(kt, pi, q0, ktt, bo=bo, bd=bd, nkt=nkt):
                    MM(ps[bo][:, q0:TT], Vh[:, kt, :], pt[pi][:, q0:TT], kt == 0, kt == nkt - 1, [("Vh", ktt), ("apt", pi)], [("ps", bo)])
                    MM(ps[bd][:, q0:TT], ones_bf[:, :], pt[pi][:, q0:TT], kt == 0, kt == nkt - 1, ["ones", ("apt", pi)], [("ps", bd)])

                for kt in range(nkt):
                    j = kt - 4 * qt
                    q0 = max(0, j) * 128
                    ktt = kt // 4
                    bSc = bank()
                    MM(ps[bSc][:, q0:TT], Kn[:, kt * 128:(kt + 1) * 128], Qn[:, c0 + q0:c0 + TT], True, False,
                       [("Kn", ktt), ("Qn", qt)], [("ps", bSc)])
                    MM(ps[bSc][:, q0:TT], Kr[H, kt * 128:(kt + 1) * 128], Qr[H, c0 + q0:c0 + TT], False, True,
                       [("Kr", ktt), ("Qr", qt)], [("ps", bSc)])
                    pi = pti[0] % 4
                    pti[0] += 1
                    ACT(pt[pi][:, q0:TT], ps[bSc][:, q0:TT], AF.Exp, [("ps", bSc)], [("apt", pi)], scale=SC)
                    if j >= 0:
                        A("pool", lambda h, pi=pi, q0=q0: h.affine_select(out=pt[pi][:, q0:q0 + 128], in_=pt[pi][:, q0:q0 + 128], pattern=[[1, 128]],
                                                                           compare_op=ALU.is_ge, fill=0.0, base=0, channel_multiplier=-1),
                          [("apt", pi)], [("apt", pi)])
                    pend_pv.append((kt, pi, q0, ktt))
                    if len(pend_pv) > 2:
                        pv(*pend_pv.pop(0))
                while pend_pv:
                    pv(*pend_pv.pop(0))
                A("dve", lambda h, bd=bd: h.reciprocal(out=rec[:, :], in_=ps[bd][:, :]), [("ps", bd)], ["crec"])
                TTOP("dve", yT[:, hh, c0:c0 + TT], ps[bo][:, :], rec[:, :], ALU.mult, [("ps", bo), "crec"], [("yT", hh, qt)])
        sch.barrier()
        out_proj(1, 0, ffn_norm_cb(1, KC * D * 2 + NSTG * 8192))
        sch.barrier()
        ffn(1)
        sch.barrier()

    def mixer_norm(l, o=0):
        sq = [fsb("nsq%d_%d" % (l, i), [128, TT], BF16, o + i * TT * 2) for i in range(2)]; o += 2 * TT * 2
        tmp = fsb("ntmp%d" % l, [128, TT], F32, o); o += TT * 4
        rstd = fsb("nrstd%d" % l, [128, TT], F32, o); o += TT * 4
        norm_full(xT, "xT", "mixg%d" % l, hT, "hT", sq, tmp, rstd)

    mem_prep(0)
    for tt in range(NT):
        for kc in range(KC):
            DMA(xT[:, kc, tt * TT:(tt + 1) * TT], xT_d[kc * 128:(kc + 1) * 128, tt * TT:(tt + 1) * TT], (), [("xT", kc, tt)])
    mixer_norm(0, 65536 if FSZ >= 65536 + 6144 else 0)
    sch.barrier()
    hgrn()
    sch.barrier()
    qw = fsb("qw0", [128, KC, 256], BF16, 0)
    _st = fsb("qstg0", [128, 2048], F32, 4096)
    stg0 = [_st] * NSTG
    load_w(w0m_d, 2048, qw[:, :, :].rearrange("p k n -> p (k n)"), "qw", stg0)
    o_end = mem_attn(0, qw, "qw", 4096 + 8192)
    if o_end + KC * D * 2 + NSTG * 8192 + 6144 <= FSZ:
        out_proj(0, o_end, ffn_norm_cb(0, o_end + KC * D * 2 + NSTG * 8192))
    else:
        sch.barrier()
        out_proj(0, 0, ffn_norm_cb(0, KC * D * 2 + NSTG * 8192))
    sch.barrier()
    ffn(0)
    sch.barrier()
    if nlayers > 1:
        mla_layer()
    outs = []
    for name, (t, key, shape, dt) in DBG.items():
        outs.append(dump(name, t, key, shape, dt))
    for kc in range(KC):
        outs.append(DMA(out_d[kc * 128:(kc + 1) * 128, :], xT[:, kc, :], [("xT", kc)], [("outd", kc)]))
    sch.emit(nc, final_wait_ops=outs)
    return nc


def make_in_maps(inputs, S, ncores):
    shared, pidx = pack_shared(inputs)
    x = np.asarray(inputs["x"], np.float32)
    mem = np.asarray(inputs["mem"], np.float32)
    pos = np.asarray(inputs["positions"]).astype(np.int32)
    in_maps = []
    for b in range(ncores):
        m = dict(shared)
        m["xT"] = np.ascontiguousarray(x[b].T)
        m["memT"] = np.ascontiguousarray(mem[b].T)
        m["pos"] = np.ascontiguousarray(pos[b][None, :])
        in_maps.append(m)
    return in_maps, pidx, shared["pcol"].shape[1]


def kernel(**inputs):
    x = np.asarray(inputs["x"])
    B, S, _ = x.shape
    in_maps, pidx, ncol = make_in_maps(inputs, S, B)
    nc = build(S, pidx, ncol)
    res = run_bass_kernel_spmd(nc, in_maps, core_ids=list(range(B)))
    out = np.stack([np.ascontiguousarray(res.results[b]["outT"].T) for b in range(B)])
    return out.astype(np.float32)
```

```python
import contextlib
import numpy as np
import concourse.bass as bass
import concourse.mybir as mybir
from concourse.bass_utils import run_bass_kernel_spmd

F32 = mybir.dt.float32
BF16 = mybir.dt.bfloat16
I32 = mybir.dt.int32
AF = mybir.ActivationFunctionType
ALU = mybir.AluOpType

D = 1024
KC = 8
MEM = 256
DFF = 2816
NPAIR = 22
EPS = 1e-6
ENGS = ("pe", "act", "dve", "pool", "sp")
NDSEM = 24


class Op:
    __slots__ = ("eng", "idx", "fn", "waits", "dwaits", "need_inc", "incval", "is_dma", "dma_id", "known", "dknown")

    def __init__(self, eng, idx, fn, is_dma):
        self.eng = eng
        self.idx = idx
        self.fn = fn
        self.waits = {}
        self.dwaits = []
        self.need_inc = False
        self.incval = 0
        self.is_dma = is_dma
        self.dma_id = -1
        self.known = None
        self.dknown = None


class Sched:
    def __init__(self):
        self.ops = {e: [] for e in ENGS}
        self.known = {e: {f: -1 for f in ENGS} for e in ENGS}
        self.dknown = {e: set() for e in ENGS}
        self.regs = {}
        self.ndma = 0
        self.dma_ops = []
        self.pending = {e: [] for e in ENGS}

    @staticmethod
    def _conf(k1, k2):
        n = min(len(k1), len(k2))
        return k1[:n] == k2[:n]

    def add(self, eng, fn, reads=(), writes=(), dma=False):
        op = Op(eng, len(self.ops[eng]), fn, dma)
        reads = [k if isinstance(k, tuple) else (k,) for k in reads]
        writes = [k if isinstance(k, tuple) else (k,) for k in writes]
        deps = []
        odeps = []
        for k in reads:
            root = self.regs.setdefault(k[0], {})
            for kk, ent in root.items():
                if self._conf(kk, k):
                    deps.extend(ent[0])
        for k in writes:
            root = self.regs.setdefault(k[0], {})
            for kk, ent in root.items():
                if self._conf(kk, k):
                    odeps.extend(ent[0])
                    odeps.extend(ent[1])
        deps.extend(odeps)
        pend = self.pending[eng]
        if pend:
            deps.extend(pend)
            self.pending[eng] = []
        for k in reads:
            ent = self.regs[k[0]].setdefault(k, [[], []])
            ent[1].append(op)
        for k in writes:
            root = self.regs[k[0]]
            for kk in list(root.keys()):
                if kk != k and self._conf(kk, k):
                    if len(kk) >= len(k):
                        del root[kk]
                    else:
                        root[kk][0].append(op)
            root[k] = [[op], []]
        kn = self.known[eng]
        dk = self.dknown[eng]
        for d in deps:
            if d is op:
                continue
            if d.is_dma:
                if d.dma_id in dk:
                    continue
                if d not in op.dwaits:
                    op.dwaits.append(d)
            else:
                if d.eng == eng and eng in ("pe", "sp"):
                    continue
                if kn[d.eng] >= d.idx:
                    continue
                cur = op.waits.get(d.eng)
                if cur is None or cur.idx < d.idx:
                    op.waits[d.eng] = d
        if dma:
            op.dma_id = self.ndma
            self.ndma += 1
            self.dma_ops.append(op)
            if op.dma_id >= NDSEM:
                prev = self.dma_ops[op.dma_id - NDSEM]
                if prev.dma_id not in dk and prev not in op.dwaits:
                    op.dwaits.append(prev)
            op.need_inc = True
        for d in op.waits.values():
            d.need_inc = True
            if kn[d.eng] < d.idx:
                kn[d.eng] = d.idx
            for f, v in d.known.items():
                if f != eng and kn[f] < v:
                    kn[f] = v
            dk |= d.dknown
        for d in op.dwaits:
            dk.add(d.dma_id)
            for f, v in d.known.items():
                if f != eng and kn[f] < v:
                    kn[f] = v
            dk |= d.dknown
        op.known = dict(kn)
        op.dknown = set(dk)
        self.ops[eng].append(op)
        return op

    def barrier(self):
        lasts = []
        for f in ENGS:
            if f == "sp":
                continue
            if self.ops[f]:
                lasts.append(self.ops[f][-1])
        for e in ENGS:
            self.pending[e] = [o for o in lasts if o.eng != e] + list(self.dma_ops)

    def emit(self, nc, final_wait_ops=()):
        for e in ENGS:
            c = 0
            for op in self.ops[e]:
                if op.is_dma:
                    op.incval = 16 * (op.dma_id // NDSEM + 1)
                elif op.need_inc:
                    c += 1
                    op.incval = c
        with contextlib.ExitStack() as st:
            sems = {e: st.enter_context(nc.semaphore("s_" + e)) for e in ENGS}
            dsems = [st.enter_context(nc.semaphore("d%d" % i)) for i in range(NDSEM)]
            block = st.enter_context(nc.Block())

            def run(e, h):
                for op in self.ops[e]:
                    for d in op.waits.values():
                        h.wait_ge(sems[d.eng], d.incval)
                    for d in op.dwaits:
                        h.wait_ge(dsems[d.dma_id % NDSEM], d.incval)
                    ins = op.fn(h)
                    if op.is_dma:
                        ins.then_inc(dsems[op.dma_id % NDSEM], 16)
                    elif op.need_inc:
                        ins.then_inc(sems[e], 1)
                if e == "sp":
                    for d in final_wait_ops:
                        h.wait_ge(dsems[d.dma_id % NDSEM], d.incval)

            @block.tensor
            def _(h):
                run("pe", h)

            @block.scalar
            def _(h):
                run("act", h)

            @block.vector
            def _(h):
                run("dve", h)

            @block.gpsimd
            def _(h):
                run("pool", h)

            @block.sync
            def _(h):
                run("sp", h)


def _cols(vec, n=128):
    v = np.asarray(vec, np.float32).reshape(-1, n)
    return np.ascontiguousarray(v.T)


def _wpack(w, kc):
    n = w.shape[1]
    return np.ascontiguousarray(w.reshape(kc, 128, n).transpose(1, 0, 2).reshape(128, kc * n))


class PCol:
    def __init__(self):
        self.cols = []
        self.idx = {}

    def put(self, name, arr):
        arr = np.asarray(arr, np.float32)
        if arr.ndim == 1:
            arr = arr[:, None]
        assert arr.shape[0] == 128, (name, arr.shape)
        self.idx[name] = (sum(c.shape[1] for c in self.cols), arr.shape[1])
        self.cols.append(arr)

    def array(self):
        return np.ascontiguousarray(np.concatenate(self.cols, axis=1))


def pack_shared(inp):
    pc = PCol()
    f = lambda a: np.asarray(a, np.float32)
    for l in range(2):
        pc.put("mixg%d" % l, _cols(f(inp["mix_norm_g"])[l]))
        pc.put("ffng%d" % l, _cols(f(inp["ffn_norm_g"])[l]))
        pc.put("memg%d" % l, _cols(f(inp["mem_norm_g"])[l]))
        pc.put("mqg%d" % l, np.tile(f(inp["mem_q_norm_g"])[l], 2))
        pc.put("mkg%d" % l, np.tile(f(inp["mem_k_norm_g"])[l], 2))
        cw = f(inp["conv_w"])[l]
        for j in range(3):
            pc.put("cw%d_%d" % (j, l), _cols(cw[j]))
        pc.put("cb%d" % l, _cols(f(inp["conv_b"])[l]))
    lbl = f(inp["hg_lb_logits"])
    for j in range(3):
        pc.put("lbl%d" % j, _cols(lbl[j]))
    pc.put("hgo", f(inp["hg_out_norm_g"])[0])
    pc.put("qag", _cols(f(inp["mla_qa_norm_g"])[0]))
    pc.put("kvag", _cols(f(inp["mla_kva_norm_g"])[0]))
    qg = f(inp["mla_q_norm_g"])[0]
    kg = f(inp["mla_k_norm_g"])[0]
    sw = np.concatenate([np.arange(160, 192), np.arange(128, 160)])
    pc.put("qg_n", qg[:128])
    pc.put("qg_r", np.tile(qg[128:192], 2))
    pc.put("qg_rs", np.tile(qg[sw], 2))
    pc.put("kg_n", kg[:128])
    pc.put("kg_r", np.tile(kg[128:192], 2))
    pc.put("kg_rs", np.tile(kg[sw], 2))
    inv = (10000.0 ** (-np.arange(32, dtype=np.float32) / 32)).astype(np.float32)
    pc.put("invf", np.tile(inv, 4))
    pc.put("sgn", np.tile(np.concatenate([-np.ones(32), np.ones(32)]), 2))
    pc.put("eps", np.full(128, EPS))
    pc.put("halfpi", np.full(128, np.pi / 2))

    out = {"pcol": pc.array()}
    w = f(inp["hg_w_in"])[0]
    units = []
    for h in range(6):
        cs = np.concatenate([np.arange(h * 128, h * 128 + 128), 768 + np.arange(h * 128, h * 128 + 128),
                             2304 + np.arange(h * 128, h * 128 + 128), 1536 + np.arange(h * 128, h * 128 + 128)])
        units.append(_wpack(w[:, cs], 8))
    out["w0"] = np.stack(units)
    out["w0m"] = _wpack(w[:, 3072:3328], 8)
    out["wmem"] = np.stack([_wpack(f(inp["w_mem_kv"])[l], 8) for l in range(2)])
    out["wout"] = np.stack([_wpack(f(inp["w_out"])[l], 8) for l in range(2)])
    wup = f(inp["w_up"])
    wdn = f(inp["w_down"])
    ups, dns = [], []
    for l in range(2):
        for i in range(NPAIR):
            cs = np.concatenate([np.arange(i * 128, i * 128 + 128), DFF + np.arange(i * 128, i * 128 + 128)])
            ups.append(_wpack(wup[l][:, cs], 8))
            dns.append(np.ascontiguousarray(wdn[l][i * 128:(i + 1) * 128, :]))
    out["wup"] = np.stack(ups).reshape(2, NPAIR, 128, 2048)
    out["wdn"] = np.stack(dns).reshape(2, NPAIR, 128, 1024)
    w1 = f(inp["mla_w_in"])[0]
    swk = np.concatenate([np.arange(32, 64), np.arange(0, 32)])
    tiles = [w1[:, 0:128], w1[:, 128:256], w1[:, 256:384], w1[:, 384:512], w1[:, 512:640],
             np.concatenate([w1[:, 640:704], w1[:, 640:704]], axis=1),
             np.concatenate([w1[:, 640 + swk], w1[:, 640 + swk]], axis=1),
             w1[:, 704:832], w1[:, 832:960]]
    out["w1in"] = np.stack([_wpack(t, 8) for t in tiles])
    wuq = f(inp["mla_w_uq"])[0]
    wukv = f(inp["mla_w_ukv"])[0]
    uqs, ukvs = [], []
    for h in range(6):
        b = h * 192
        cs = np.concatenate([np.arange(b, b + 128), np.arange(b + 128, b + 192), b + 128 + swk])
        uqs.append(_wpack(wuq[:, cs], 3))
        ukvs.append(_wpack(wukv[:, h * 256:(h + 1) * 256], 2))
    out["wuq"] = np.stack(uqs)
    out["wukv"] = np.stack(ukvs)
    return out, pc.idx


def build(S, pidx, ncol, nlayers=2, dbg=()):
    nc = bass.Bass("TRN2", target_bir_lowering=False)
    TT = 512
    NT = S // TT
    NB = S // 128
    NCH = S // 64
    sch = Sched()
    A = sch.add

    def dram(name, shape, dt=F32, kind="ExternalInput"):
        return nc.dram_tensor(name, list(shape), dt, kind=kind).ap()

    xT_d = dram("xT", [D, S])
    memT_d = dram("memT", [D, MEM])
    pos_d = dram("pos", [1, S], I32)
    pcol_d = dram("pcol", [128, ncol])
    w0_d = dram("w0", [6, 128, 4096])
    w0m_d = dram("w0m", [128, 2048])
    wmem_d = dram("wmem", [2, 128, 4096])
    wout_d = dram("wout", [2, 128, 8192])
    wup_d = dram("wup", [2, NPAIR, 128, 2048])
    wdn_d = dram("wdn", [2, NPAIR, 128, 1024])
    w1in_d = dram("w1in", [9, 128, 1024])
    wuq_d = dram("wuq", [6, 128, 768])
    wukv_d = dram("wukv", [6, 128, 512])
    out_d = dram("outT", [D, S], F32, "ExternalOutput")
    DBG = {}

    BASE = 16640
    LIMIT = 229376
    cur = [BASE]

    def sb(name, shape, dt, off=None):
        nbytes = int(np.prod(shape[1:])) * (4 if dt in (F32, I32) else 2)
        if off is None:
            off = cur[0]
            cur[0] += (nbytes + 63) // 64 * 64
            assert cur[0] <= LIMIT, (name, cur[0])
        assert off + nbytes <= LIMIT, (name, off, nbytes)
        return nc.alloc_sbuf_tensor_at(name, list(shape), dt, offset=off)

    xT = sb("xTs", [128, KC, S], F32)
    hT = sb("hTs", [128, KC, S], BF16)
    yT = sb("yTs", [128, KC, S], BF16)
    pcol = sb("pcols", [128, ncol], F32)
    ones_bf = sb("ones", [128, 128], BF16)
    ident_bf = sb("ident", [128, 128], BF16)
    tri_bf = sb("tri", [128, 128], BF16)
    tri64 = sb("tri64", [128, 64], BF16)
    blk_bf = sb("blk", [128, 128], BF16)
    kn_m = sb("kn_m", [128, 2, MEM], BF16)
    vm_m = sb("vm_m", [128, 2, 256], BF16)
    lbc = sb("lbc", [128, 12], F32)
    F0 = cur[0]
    FSZ = LIMIT - F0

    def fsb(name, shape, dt, off):
        return sb(name, shape, dt, F0 + off)

    ps = [nc.alloc_psum_tensor("ps%d" % i, [128, 512], F32) for i in range(8)]
    rr = [0]

    def bank():
        b = rr[0]
        rr[0] = (rr[0] + 1) % 4
        return b

    rr8 = [0]

    def bank8():
        b = rr8[0]
        rr8[0] = (rr8[0] + 1) % 8
        return b

    def pc(name, j=0, rows=slice(0, 128)):
        o, n = pidx[name]
        return pcol[rows, o + j:o + j + 1]

    def _ap(x):
        if x is None or isinstance(x, (int, float)) or isinstance(x, bass.AP):
            return x
        return x[:]

    def MM(out, lhsT, rhs, start, stop, r, w):
        out, lhsT, rhs = _ap(out), _ap(lhsT), _ap(rhs)
        return A("pe", lambda h: h.matmul(out, lhsT=lhsT, rhs=rhs, start=start, stop=stop), r, w)

    def ACT(out, in_, func, r, w, bias=None, scale=None):
        kw = {}
        out, in_ = _ap(out), _ap(in_)
        if bias is not None:
            kw["bias"] = _ap(bias)
        if scale is not None:
            kw["scale"] = _ap(scale)
        return A("act", lambda h: h.activation(out=out, in_=in_, func=func, **kw), r, w)

    def TTOP(eng, out, in0, in1, op, r, w):
        out, in0, in1 = _ap(out), _ap(in0), _ap(in1)
        return A(eng, lambda h: h.tensor_tensor(out=out, in0=in0, in1=in1, op=op), r, w)

    def TS(eng, out, in0, s1, s2, op0, op1, r, w):
        out, in0, s1, s2 = _ap(out), _ap(in0), _ap(s1), _ap(s2)
        return A(eng, lambda h: h.tensor_scalar(out=out, in0=in0, scalar1=s1, scalar2=s2, op0=op0, op1=op1), r, w)

    def STT(out, in0, scalar, in1, op0, op1, r, w):
        out, in0, scalar, in1 = _ap(out), _ap(in0), _ap(scalar), _ap(in1)
        return A("dve", lambda h: h.scalar_tensor_tensor(out=out, in0=in0, scalar=scalar, in1=in1, op0=op0, op1=op1), r, w)

    def COPY(eng, out, in_, r, w):
        out, in_ = _ap(out), _ap(in_)
        if eng == "act":
            return A("act", lambda h: h.copy(out=out, in_=in_), r, w)
        return A(eng, lambda h: h.tensor_copy(out=out, in_=in_), r, w)

    def DMA(out, in_, r, w):
        out, in_ = _ap(out), _ap(in_)
        return A("sp", lambda h: h.dma_start(out=out, in_=in_), r, w, dma=True)

    def MEMSET(ap, val, w):
        return A("pool", lambda h: h.memset(ap, val), (), w)

    NSTG = 2
    stg_i = [0]

    def load_w(dram_ap, n, dst_flat, wkey, stg):
        p = 0
        while p < n:
            m = min(2048, n - p)
            s = stg_i[0] % NSTG
            stg_i[0] += 1
            DMA(stg[s][:, 0:m], dram_ap[:, p:p + m], (), [("stg", s)])
            COPY("pool", dst_flat[:, p:p + m], stg[s][:, 0:m], [("stg", s)], [wkey])
            p += m

    DMA(pcol[:, :], pcol_d[:, :], (), ["pcol"])
    MEMSET(ones_bf[:, :], 1.0, ["ones"])
    A("pool", lambda h: h.affine_select(out=ident_bf[:, :], in_=ones_bf[:, :], pattern=[[-1, 128]], compare_op=ALU.is_equal,
                                        fill=0.0, base=0, channel_multiplier=1), ["ones"], ["ident"])
    A("pool", lambda h: h.affine_select(out=tri_bf[:, :], in_=ones_bf[:, :], pattern=[[1, 128]], compare_op=ALU.is_ge,
                                        fill=0.0, base=0, channel_multiplier=-1), ["ones"], ["tri"])
    for hb in (0, 64):
        A("pool", lambda h, hb=hb: h.affine_select(out=tri64[hb:hb + 64, :], in_=ones_bf[hb:hb + 64, 0:64], pattern=[[1, 64]],
                                                   compare_op=ALU.is_ge, fill=0.0, base=0, channel_multiplier=-1), ["ones"], [("tri64", hb)])
    MEMSET(blk_bf[:, :], 0.0, ["blk"])
    MEMSET(blk_bf[0:64, 0:64], 1.0, ["blk"])
    MEMSET(blk_bf[64:128, 64:128], 1.0, ["blk"])

    def dump(name, ap, key, shape, dt=F32):
        d = dram("dbg_" + name, shape, dt, "ExternalOutput")
        return DMA(d, ap, [key], [("dbgout", name)])

    def rstd_from_ss(ssbank, width, inv_n, tmp, rstd, keyt, keyr, rows=slice(0, 128)):
        ACT(tmp, ps[ssbank][rows, 0:width], AF.Ln, [("ps", ssbank), "pcol"], [keyt], bias=pc("eps", 0, rows), scale=inv_n)
        ACT(rstd, tmp, AF.Exp, [keyt], [keyr], scale=-0.5)

    def norm_tile(tt, src, skey, gname, dst, dkey, sq, tmp, rstd):
        c = slice(tt * TT, (tt + 1) * TT)
        b = bank()
        for kc in range(KC):
            s_ = kc % 2
            ACT(sq[s_], src[:, kc, c], AF.Square, [(skey, kc, tt)], [("sq", id(sq), s_)])
            MM(ps[b][:, :], ones_bf[:, :], sq[s_], kc == 0, kc == KC - 1, [("sq", id(sq), s_), "ones"], [("ps", b)])
        rstd_from_ss(b, TT, 1.0 / D, tmp, rstd, ("ntmp", id(tmp)), ("nrstd", id(rstd)))
        for kc in range(KC):
            STT(dst[:, kc, c], src[:, kc, c], pc(gname, kc), rstd, ALU.mult, ALU.mult,
                [(skey, kc, tt), ("nrstd", id(rstd)), "pcol"], [(dkey, kc, tt)])

    def norm_full(src, skey, gname, dst, dkey, sq, tmp, rstd):
        for tt in range(NT):
            norm_tile(tt, src, skey, gname, dst, dkey, sq, tmp, rstd)

    def mem_prep(l):
        o = 0
        memT = fsb("memT", [128, KC, MEM], F32, o); o += KC * MEM * 4
        memn = fsb("memn", [128, KC, MEM], BF16, o); o += KC * MEM * 2
        wm = fsb("wm", [128, KC, 512], BF16, o); o += KC * 512 * 2
        stg = [fsb("mstg%d" % i, [128, 2048], F32, o + i * 8192) for i in range(NSTG)]; o += NSTG * 8192
        sq = [fsb("msq%d" % i, [128, MEM], BF16, o + i * 512) for i in range(2)]; o += 1024
        tmp = fsb("mtmp", [128, MEM], F32, o); o += 1024
        rstd = fsb("mrstd", [128, MEM], F32, o); o += 1024
        kraw = fsb("mkraw", [128, MEM], F32, o); o += 1024
        for kc in range(KC):
            DMA(memT[:, kc, :], memT_d[kc * 128:(kc + 1) * 128, :], (), [("memT", kc)])
        load_w(wmem_d[l], 4096, wm[:, :, :].rearrange("p k n -> p (k n)"), "wm", stg)
        b = bank()
        for kc in range(KC):
            s = kc % 2
            ACT(sq[s], memT[:, kc, :], AF.Square, [("memT", kc)], [("msq", s)])
            MM(ps[b][:, 0:MEM], ones_bf[:, :], sq[s], kc == 0, kc == KC - 1, [("msq", s), "ones"], [("ps", b)])
        rstd_from_ss(b, MEM, 1.0 / D, tmp, rstd, "mtmp", "mrstd")
        for kc in range(KC):
            STT(memn[:, kc, :], memT[:, kc, :], pc("memg%d" % l, kc), rstd, ALU.mult, ALU.mult,
                [("memT", kc), "mrstd", "pcol"], [("memn", kc)])
        for j in range(2):
            b = bank()
            for kc in range(KC):
                MM(ps[b][:, 0:MEM], wm[:, kc, j * 128:(j + 1) * 128], memn[:, kc, :], kc == 0, kc == KC - 1,
                   ["wm", ("memn", kc)], [("ps", b)])
            COPY("act", kraw, ps[b][:, 0:MEM], [("ps", b)], ["mkraw"])
            ACT(sq[0], ps[b][:, 0:MEM], AF.Square, [("ps", b)], [("msq", 0)])
            b2 = bank()
            MM(ps[b2][:, 0:MEM], blk_bf[:, :], sq[0], True, True, [("msq", 0), "blk"], [("ps", b2)])
            rstd_from_ss(b2, MEM, 1.0 / 64, tmp, rstd, "mtmp", "mrstd")
            STT(kn_m[:, j, :], kraw, pc("mkg%d" % l), rstd, ALU.mult, ALU.mult, ["mkraw", "mrstd", "pcol"], [("kn_m", j)])
        for mt in range(2):
            b = bank()
            for kc in range(KC):
                MM(ps[b][:, 0:256], memn[:, kc, mt * 128:(mt + 1) * 128], wm[:, kc, 256:512], kc == 0, kc == KC - 1,
                   ["wm", ("memn", kc)], [("ps", b)])
            COPY("act", vm_m[:, mt, :], ps[b][:, 0:256], [("ps", b)], [("vm_m", mt)])

    def mem_attn(l, qw, qwkey, o):
        qraw = fsb("qraw%d" % l, [128, TT], F32, o); o += TT * 4
        sq = fsb("qsq%d" % l, [128, TT], BF16, o); o += TT * 2
        tmp = fsb("qtmp%d" % l, [128, TT], F32, o); o += TT * 4
        rstd = fsb("qrstd%d" % l, [128, TT], F32, o); o += TT * 4
        qn = [fsb("qn%d_%d" % (l, i), [128, TT], BF16, o + i * TT * 2) for i in range(2)]; o += 2 * TT * 2
        pt = [fsb("pt%d_%d" % (l, i), [128, TT], BF16, o + i * TT * 2) for i in range(4)]; o += 4 * TT * 2
        rec2 = [fsb("rec%d_%d" % (l, i), [128, TT], F32, o + i * TT * 4) for i in range(2)]; o += 2 * TT * 4
        pti = [0]
        pend_pv = []

        def pv_unit(tt, j, e, pts):
            c = slice(tt * TT, (tt + 1) * TT)
            hh = 2 * j + e
            pb = 64 * e
            bo = 4 + 2 * ((tt * 2 + j) % 2)
            bd = bo + 1
            for mt in range(2):
                MM(ps[bo][pb:pb + 64, :], vm_m[:, mt, hh * 64:(hh + 1) * 64], pt[pts[mt]], mt == 0, mt == 1,
                   [("vm_m", mt), ("pt", pts[mt])], [("ps", bo, e)])
            for mt in range(2):
                MM(ps[bd][pb:pb + 64, :], ones_bf[:, 0:64], pt[pts[mt]], mt == 0, mt == 1,
                   ["ones", ("pt", pts[mt])], [("ps", bd, e)])
            rk = ("rec", (tt * 2 + j) % 2, e)
            rc = rec2[(tt * 2 + j) % 2]
            A("dve", lambda h, bd=bd, pb=pb, rc=rc: h.reciprocal(out=rc[pb:pb + 64, :], in_=ps[bd][pb:pb + 64, :]),
              [("ps", bd, e)], [rk])
            TTOP("dve", yT[pb:pb + 64, 6 + j, c], ps[bo][pb:pb + 64, :], rc[pb:pb + 64, :], ALU.mult,
                 [("ps", bo, e), rk], [("yT", 6 + j, tt, e)])

        for tt in range(NT):
            c = slice(tt * TT, (tt + 1) * TT)
            for j in range(2):
                b = bank()
                for kc in range(KC):
                    MM(ps[b][:, :], qw[:, kc, j * 128:(j + 1) * 128], hT[:, kc, c], kc == 0, kc == KC - 1,
                       [qwkey, ("hT", kc, tt)], [("ps", b)])
                COPY("act", qraw, ps[b][:, :], [("ps", b)], ["qraw"])
                ACT(sq, ps[b][:, :], AF.Square, [("ps", b)], ["qsq"])
                b2 = bank()
                MM(ps[b2][:, :], blk_bf[:, :], sq, True, True, ["qsq", "blk"], [("ps", b2)])
                rstd_from_ss(b2, TT, 1.0 / 64, tmp, rstd, "qtmp", "qrstd")
                STT(qn[j], qraw, pc("mqg%d" % l), rstd, ALU.mult, ALU.mult, ["qraw", "qrstd", "pcol"], [("qn", j)])
                for e in range(2):
                    hh = 2 * j + e
                    pb = 64 * e
                    pts = []
                    for mt in range(2):
                        b = bank()
                        MM(ps[b][:, :], kn_m[pb:pb + 64, j, mt * 128:(mt + 1) * 128], qn[j][pb:pb + 64, :], True, True,
                           [("kn_m", j), ("qn", j)], [("ps", b)])
                        pi = pti[0] % 4
                        pti[0] += 1
                        ACT(pt[pi], ps[b][:, :], AF.Exp, [("ps", b)], [("pt", pi)], scale=0.125)
                        pts.append(pi)
                    pend_pv.append((tt, j, e, pts))
                    if len(pend_pv) > 1:
                        pv_unit(*pend_pv.pop(0))
        while pend_pv:
            pv_unit(*pend_pv.pop(0))
        return o

    def out_proj(l, o, tile_cb=None):
        wo = fsb("wo%d" % l, [128, KC, D], BF16, o); o += KC * D * 2
        stg = [fsb("ostg%d_%d" % (l, i), [128, 2048], F32, o + i * 8192) for i in range(NSTG)]; o += NSTG * 8192
        load_w(wout_d[l], KC * D, wo[:, :, :].rearrange("p k n -> p (k n)"), "wo", stg)
        for tt in range(NT):
            if tile_cb is not None and tt > 0:
                tile_cb(tt - 1)
            c = slice(tt * TT, (tt + 1) * TT)
            for d in range(KC):
                b = bank()
                for kc in range(KC):
                    MM(ps[b][:, :], wo[:, kc, d * 128:(d + 1) * 128], yT[:, kc, c], kc == 0, kc == KC - 1,
                       ["wo", ("yT", kc, tt)], [("ps", b)])
                TTOP("dve", xT[:, d, c], ps[b][:, :], xT[:, d, c], ALU.add, [("ps", b), ("xT", d, tt)], [("xT", d, tt)])
        if tile_cb is not None:
            tile_cb(NT - 1)
        return o

    def ffn_norm_cb(l, o):
        sq = [fsb("fsq%d_%d" % (l, i), [128, TT], BF16, o + i * TT * 2) for i in range(2)]; o += 2 * TT * 2
        tmp = fsb("ftmp%d" % l, [128, TT], F32, o); o += TT * 4
        rstd = fsb("frstd%d" % l, [128, TT], F32, o); o += TT * 4
        assert o <= FSZ, (o, FSZ)
        return lambda tt: norm_tile(tt, xT, "xT", "ffng%d" % l, hT, "hT", sq, tmp, rstd)

    def ffn(l):
        G = 4
        o = 0
        NUP, NDN, NTR = 2, 3 * G, 4
        wu = [fsb("wu%d_%d" % (l, i), [128, KC, 256], BF16, o + i * 4096) for i in range(NUP)]; o += NUP * 4096
        wd = [fsb("wd%d_%d" % (l, i), [128, D], BF16, o + i * 2048) for i in range(NDN)]; o += NDN * 2048
        stg = [fsb("fstg%d_0" % l, [128, 2048], F32, o), fsb("fstg%d_1" % l, [128, 1024], F32, o + 8192)]; o += 8192 + 4096
        hal = [[fsb("hal%d_%d_%d" % (l, ab, i), [128, 8], F32, o + (ab * 2 + i) * 32) for i in range(2)] for ab in range(2)]
        o += 4 * 32
        tr = [[fsb("tr%d_%d_%d" % (l, ab, i), [128, TT], F32, o + (ab * NTR + i) * TT * 4) for i in range(NTR)] for ab in range(2)]
        o += 2 * NTR * TT * 4
        sa = [fsb("sa%d_%d" % (l, i), [128, TT], BF16, o + i * TT * 2) for i in range(2)]; o += 2 * TT * 2
        tb16 = [fsb("tb16_%d_%d" % (l, i), [128, TT], BF16, o + i * TT * 2) for i in range(NTR)]; o += NTR * TT * 2
        assert o <= FSZ, (o, FSZ)
        if dbg and l == 0:
            for ab in range(2):
                for i in range(NTR):
                    DBG["tr%d_%d" % (ab, i)] = (tr[ab][i], ("tr", ab, i), [128, TT], F32)
            for i in range(2):
                DBG["sa_%d" % i] = (sa[i], ("sa", i), [128, TT], BF16)
        actg = [nc.alloc_sbuf_tensor_at("actg%d_%d" % (l, i), [128, G, S], BF16,
                                        offset=BASE + KC * S * 4 + KC * S * 2 + i * G * S * 2) for i in range(2)]
        groups = [list(range(g, min(g + G, NPAIR))) for g in range(0, NPAIR, G)]
        dbank = [0]

        def dma_pair(i):
            DMA(stg[0][:, 0:2048], wup_d[l, i], (), [("stg", 0)])
            DMA(stg[1][:, 0:1024], wdn_d[l, i], (), [("stg", 1)])

        def cast_pair(i):
            COPY("act", wu[i % NUP][:, :, :].rearrange("p k n -> p (k n)"), stg[0][:, 0:2048], [("stg", 0)], [("wu", i % NUP)])
            COPY("act", wd[i % NDN][:, :], stg[1][:, 0:1024], [("stg", 1)], [("wd", i % NDN)])

        def down_mm(gi, d, tt):
            grp = groups[gi]
            ag = actg[gi % 2]
            c = slice(tt * TT, (tt + 1) * TT)
            b = 6 + dbank[0] % 2
            dbank[0] += 1
            for ii, i in enumerate(grp):
                MM(ps[b][:, :], wd[i % NDN][:, d * 128:(d + 1) * 128], ag[:, ii, c], ii == 0, ii == len(grp) - 1,
                   [("wd", i % NDN), ("actg", gi % 2, ii, tt)], [("ps", b)])
            return (b, d, tt)

        def down_add(b, d, tt):
            c = slice(tt * TT, (tt + 1) * TT)
            TTOP("dve", xT[:, d, c], ps[b][:, :], xT[:, d, c], ALU.add, [("ps", b), ("xT", d, tt)], [("xT", d, tt)])

        def stage_d(k, gi, ii, tt):
            c = slice(tt * TT, (tt + 1) * TT)
            ACT(sa[k % 2], tr[0][k % NTR], AF.Silu, [("tr", 0, k % NTR)], [("sa", k % 2)])
            TTOP("dve", actg[gi % 2][:, ii, c], sa[k % 2], tb16[k % NTR], ALU.mult,
                 [("sa", k % 2), ("tb16", k % NTR)], [("actg", gi % 2, ii, tt)])

        dma_pair(0)
        cast_pair(0)
        k = 0
        pend_d = []
        pend_down = []
        pend_add = []
        for gi, grp in enumerate(groups):
            niter = len(grp) * NT
            it = 0
            for ii, i in enumerate(grp):
                if i + 1 < NPAIR:
                    dma_pair(i + 1)
                wus = i % NUP
                for tt in range(NT):
                    c = slice(tt * TT, (tt + 1) * TT)
                    banks = [(k % 3) * 2, (k % 3) * 2 + 1]
                    while pend_add:
                        down_add(*pend_add.pop(0))
                    for ab in range(2):
                        b = banks[ab]
                        for kc in range(KC):
                            MM(ps[b][:, :], wu[wus][:, kc, ab * 128:(ab + 1) * 128], hT[:, kc, c], kc == 0, kc == KC - 1,
                               [("wu", wus), ("hT", kc, tt)], [("ps", b)])
                    for ab in range(2):
                        b = banks[ab]
                        col = ab * NPAIR + i
                        if tt + 1 < NT:
                            COPY("act", hal[ab][tt % 2][:, 0:2], ps[b][:, TT - 2:TT], [("ps", b)], [("hal", ab, tt % 2)])
                        ACT(tr[ab][k % NTR], ps[b][:, :], AF.Identity, [("ps", b), "pcol"], [("tr", ab, k % NTR)],
                            bias=pc("cb%d" % l, col), scale=pc("cw2_%d" % l, col))
                    for step in range(4):
                        for ab in range(2):
                            b = banks[ab]
                            col = ab * NPAIR + i
                            t = tr[ab][k % NTR]
                            tk = ("tr", ab, k % NTR)
                            if step == 0:
                                STT(t[:, 1:TT], ps[b][:, 0:TT - 1], pc("cw1_%d" % l, col), t[:, 1:TT], ALU.mult, ALU.add,
                                    [("ps", b), tk, "pcol"], [tk])
                            elif step == 1:
                                dst, dk = (t, tk) if ab == 0 else (tb16[k % NTR], ("tb16", k % NTR, "m"))
                                STT(dst[:, 2:TT], ps[b][:, 0:TT - 2], pc("cw0_%d" % l, col), t[:, 2:TT], ALU.mult, ALU.add,
                                    [("ps", b), tk, "pcol"], [dk])
                            elif tt > 0:
                                hp = hal[ab][(tt - 1) % 2]
                                hk = ("hal", ab, (tt - 1) % 2)
                                if step == 2:
                                    STT(t[:, 0:1], hp[:, 1:2], pc("cw1_%d" % l, col), t[:, 0:1], ALU.mult, ALU.add, [hk, tk, "pcol"], [tk])
                                else:
                                    dst, dk = (t, tk) if ab == 0 else (tb16[k % NTR], ("tb16", k % NTR, "h"))
                                    STT(dst[:, 0:2], hp[:, 0:2], pc("cw0_%d" % l, col), t[:, 0:2], ALU.mult, ALU.add, [hk, tk, "pcol"], [dk])
                            elif step == 3 and ab == 1:
                                COPY("dve", tb16[k % NTR][:, 0:2], t[:, 0:2], [tk], [("tb16", k % NTR, "h")])
                    if tt == min(1, NT - 1) and i + 1 < NPAIR:
                        cast_pair(i + 1)
                    pend_d.append((k, gi, ii, tt))
                    if len(pend_d) > 2:
                        stage_d(*pend_d.pop(0))
                    if pend_down:
                        n_left = niter - it
                        take = -(-len(pend_down) // max(1, n_left))
                        for _ in range(take):
                            if len(pend_add) >= 2:
                                down_add(*pend_add.pop(0))
                            pend_add.append(down_mm(*pend_down.pop(0)))
                    k += 1
                    it += 1
            while pend_down:
                if len(pend_add) >= 2:
                    down_add(*pend_add.pop(0))
                pend_add.append(down_mm(*pend_down.pop(0)))
            pend_down = [(gi, d, tt) for tt in range(NT) for d in range(KC)]
        while pend_d:
            stage_d(*pend_d.pop(0))
        while pend_down:
            if len(pend_add) >= 2:
                down_add(*pend_add.pop(0))
            pend_add.append(down_mm(*pend_down.pop(0)))
        while pend_add:
            down_add(*pend_add.pop(0))

    def hgrn():
        o = 0
        e3 = fsb("lb_e", [128, 18], F32, o); o += 128
        ssum = fsb("lb_s", [128, 6], F32, o); o += 64
        srec = fsb("lb_r", [128, 6], F32, o); o += 64
        o0, _ = pidx["lbl0"]
        ACT(e3[:, :], pcol[:, o0:o0 + 18], AF.Exp, ["pcol"], ["lb_e"])
        TTOP("dve", ssum[:, :], e3[:, 0:6], e3[:, 6:12], ALU.add, ["lb_e"], ["lb_s"])
        TTOP("dve", ssum[:, :], ssum[:, :], e3[:, 12:18], ALU.add, ["lb_e", "lb_s"], ["lb_s"])
        A("dve", lambda h: h.reciprocal(out=srec[:, :], in_=ssum[:, :]), ["lb_s"], ["lb_r"])
        TTOP("dve", lbc[:, 0:6], e3[:, 0:6], srec[:, :], ALU.mult, ["lb_e", "lb_r"], [("lbc", 0)])
        TS("dve", lbc[:, 6:12], lbc[:, 0:6], -1.0, 1.0, ALU.mult, ALU.add, [("lbc", 0)], [("lbc", 1)])
        o = 256
        wh2 = [fsb("wh%d" % i, [128, KC, 512], BF16, o + i * KC * 512 * 2) for i in range(2)]; o += 2 * KC * 512 * 2
        stg = [fsb("hstg%d" % i, [128, 2048], F32, o + i * 8192) for i in range(NSTG)]; o += NSTG * 8192
        QT = fsb("QT", [128, S], BF16, o); o += S * 2
        KT = fsb("KT", [128, S], BF16, o); o += S * 2
        Ktok = fsb("Ktok", [128, NB, 128], BF16, o); o += S * 2
        vtok = fsb("vtok", [128, NB, 128], BF16, o); o += S * 2
        gT = fsb("gT", [128, S], BF16, o); o += S * 2
        Bl = fsb("Bl", [128, NCH], F32, o); o += NCH * 4
        mscan = fsb("mscan", [128, TT], F32, o); o += TT * 4
        tA = [fsb("tA%d" % i, [128, TT], F32, o + i * TT * 4) for i in range(6)]; o += 6 * TT * 4
        fR = fsb("fR", [128, TT], F32, o); o += TT * 4
        fP = fsb("fP", [128, TT], F32, o); o += TT * 4
        Tst = [fsb("Tst%d" % i, [128, 128], F32, o + i * 512) for i in range(2)]; o += 1024
        stb = [fsb("stb%d" % i, [128, 128], BF16, o + i * 256) for i in range(3)]; o += 768
        aT = [fsb("aT%d" % i, [128, 64], BF16, o + i * 128) for i in range(2)]; o += 256
        sqo = fsb("sqo", [128, TT], BF16, o); o += TT * 2
        assert o <= FSZ, (o, FSZ)
        MEMSET(mscan[:, :], 1.0, ["mscan"])
        MEMSET(mscan[:, :].rearrange("p (n c) -> p n c", c=64)[:, :, 0:1], 0.0, ["mscan"])
        qs, sg, fg, lf, bb, eb = tA
        prr = [0]
        rcr = [0]

        def pbank():
            b = prr[0]
            prr[0] = (prr[0] + 1) % 3
            return b

        def rbank():
            b = (3, 6, 7)[rcr[0]]
            rcr[0] = (rcr[0] + 1) % 3
            return b

        def load_head(hh):
            load_w(w0_d[hh], 4096, wh2[hh % 2][:, :, :].rearrange("p k n -> p (k n)"), ("wh", hh % 2), stg)

        def pre_stages(hh, tt):
            wh = wh2[hh % 2]
            wk = ("wh", hh % 2)
            c = slice(tt * TT, (tt + 1) * TT)
            st = {}

            def P1():
                st["bf"] = pbank()
                for kc in range(KC):
                    MM(ps[st["bf"]][:, :], wh[:, kc, 128:256], hT[:, kc, c], kc == 0, kc == KC - 1, [wk, ("hT", kc, tt)], [("ps", st["bf"])])
                st["bq"] = pbank()
                for kc in range(KC):
                    MM(ps[st["bq"]][:, :], wh[:, kc, 0:128], hT[:, kc, c], kc == 0, kc == KC - 1, [wk, ("hT", kc, tt)], [("ps", st["bq"])])

            def P2():
                ACT(sg, ps[st["bf"]][:, :], AF.Sigmoid, [("ps", st["bf"])], ["sg"])
                st["bg"] = pbank()
                for kc in range(KC):
                    MM(ps[st["bg"]][:, :], wh[:, kc, 256:384], hT[:, kc, c], kc == 0, kc == KC - 1, [wk, ("hT", kc, tt)], [("ps", st["bg"])])

            def P3():
                TS("dve", fg, sg, lbc[:, 6 + hh:7 + hh], lbc[:, hh:hh + 1], ALU.mult, ALU.add, ["sg", "lbc"], ["fg"])

            def P4():
                ACT(lf, fg, AF.Ln, ["fg"], ["lf"])
                st["bv"] = pbank()
                for j in range(4):
                    tb = tt * 4 + j
                    for kc in range(KC):
                        MM(ps[st["bv"]][:, j * 128:(j + 1) * 128], hT[:, kc, tb * 128:(tb + 1) * 128], wh[:, kc, 384:512], kc == 0, kc == KC - 1,
                           [wk, ("hT", kc, tt)], [("ps", st["bv"])])

            def P5():
                A("dve", lambda h: h.tensor_tensor_scan(out=bb[:, :], data0=mscan[:, :], data1=lf[:, :], initial=0.0, op0=ALU.mult, op1=ALU.add),
                  ["mscan", "lf"], ["bb"])
                ACT(qs, ps[st["bq"]][:, :], AF.Silu, [("ps", st["bq"])], ["qs"])
                ACT(gT[:, c], ps[st["bg"]][:, :], AF.Silu, [("ps", st["bg"])], [("gT", tt)])

            def P6():
                ACT(eb, bb, AF.Exp, ["bb"], ["eb"])
                ACT(sg, bb, AF.Exp, ["bb"], ["sg"], scale=-1.0)
                COPY("act", vtok[:, tt * 4:(tt + 1) * 4, :].rearrange("p j n -> p (j n)"), ps[st["bv"]][:, :], [("ps", st["bv"])], [("vtok", tt)])

            def P7():
                STT(QT[:, c], qs, 128.0 ** -0.5, eb, ALU.mult, ALU.mult, ["qs", "eb"], [("QT", tt)])
                COPY("dve", Bl[:, tt * 8:(tt + 1) * 8], eb[:, :].rearrange("p (n c) -> p n c", c=64)[:, :, 63], ["eb"], [("Bl", tt)])
                TS("dve", lf, fg, -1.0, 1.0, ALU.mult, ALU.add, ["fg"], ["lf"])
                TTOP("dve", KT[:, c], lf, sg, ALU.mult, ["lf", "sg"], [("KT", tt)])

            def P8():
                b = pbank()
                pb16 = ps[b][:, :].bitcast(BF16)
                for j in range(4):
                    tb = tt * 4 + j
                    A("pe", lambda h, j=j, tb=tb, pb16=pb16: h.transpose(out=pb16[:, j * 128:(j + 1) * 128], in_=KT[:, tb * 128:(tb + 1) * 128],
                                                                           identity=ident_bf[:, :]),
                      [("KT", tt), "ident"], [("ps", b)])
                COPY("act", Ktok[:, tt * 4:(tt + 1) * 4, :].rearrange("p j n -> p (j n)"), pb16[:, 0:512], [("ps", b)], [("Ktok", tt)])

            return [P1, P2, P3, P4, P5, P6, P7, P8]

        BO = 4

        def rec_step(hh, n):
            if n < NCH:
                base = (n % 2) * 64
                blkI = n // 2
                tt = n // 8
                cs = slice(n * 64, (n + 1) * 64)
                bA = rbank()
                MM(ps[bA][base:base + 64, 0:64], KT[:, cs], QT[:, cs], True, True, [("KT", tt), ("QT", tt)], [("ps", bA)])
                a = aT[n % 2]
                TTOP("dve", a[base:base + 64, :], ps[bA][base:base + 64, 0:64], tri64[base:base + 64, :], ALU.mult,
                     [("ps", bA), "tri64"], [("aT", n % 2)])
                bU = rbank()
                MM(ps[bU][:, 0:128], Ktok[base:base + 64, blkI, :], vtok[base:base + 64, blkI, :], True, True,
                   [("Ktok", tt), ("vtok", tt)], [("ps", bU)])
                if n == 0:
                    COPY("dve", Tst[0][:, :], ps[bU][:, 0:128], [("ps", bU)], [("Tst", 0)])
                else:
                    STT(Tst[n % 2][:, :], Tst[(n - 1) % 2][:, :], Bl[:, n - 1:n], ps[bU][:, 0:128], ALU.mult, ALU.add,
                        [("Tst", (n - 1) % 2), ("Bl", (n - 1) // 8), ("ps", bU)], [("Tst", n % 2)])
                if n + 1 < NCH:
                    TS("pool", stb[(n + 1) % 3][:, :], Tst[n % 2][:, :], Bl[:, n:n + 1], 1.0, ALU.mult, ALU.mult,
                       [("Tst", n % 2), ("Bl", n // 8)], [("stb", (n + 1) % 3)])
            if n > 0:
                m = n - 1
                mbase = (m % 2) * 64
                mblk = m // 2
                mtt = m // 8
                mcs = slice(m * 64, (m + 1) * 64)
                bo = BO + (mtt % 2)
                oc = slice((m % 8) * 64, (m % 8 + 1) * 64)
                MM(ps[bo][:, oc], vtok[mbase:mbase + 64, mblk, :], aT[m % 2][mbase:mbase + 64, :], True, m == 0,
                   [("vtok", mtt), ("aT", m % 2)], [("ps", bo, m % 8)])
                if m > 0:
                    MM(ps[bo][:, oc], stb[m % 3][:, :], QT[:, mcs], False, True, [("stb", m % 3), ("QT", mtt)], [("ps", bo, m % 8)])
                if m % 8 == 7:
                    c = slice(mtt * TT, (mtt + 1) * TT)
                    ACT(sqo, ps[bo][:, :], AF.Square, [("ps", bo)], ["sqo"])
                    b2 = rbank()
                    MM(ps[b2][:, :], ones_bf[:, :], sqo, True, True, ["sqo", "ones"], [("ps", b2)])
                    rstd_from_ss(b2, TT, 1.0 / 128, fR, fR, "fR", "fR")
                    STT(fP, ps[bo][:, :], pc("hgo"), fR, ALU.mult, ALU.mult, [("ps", bo), "fR", "pcol"], ["fP"])
                    TTOP("dve", yT[:, hh, c], fP, gT[:, c], ALU.mult, ["fP", ("gT", mtt)], [("yT", hh, mtt)])

        load_head(0)
        tiles = [(hh, tt) for hh in range(6) for tt in range(NT)]
        for f_ in pre_stages(*tiles[0]):
            f_()
        for i, (hh, tt) in enumerate(tiles):
            if tt == 0 and hh + 1 < 6:
                load_head(hh + 1)
            steps = [(hh, n) for n in range(tt * 8, tt * 8 + 8)]
            if tt == NT - 1:
                steps.append((hh, NCH))
            stages = pre_stages(*tiles[i + 1]) if i + 1 < len(tiles) else []
            for j in range(max(len(steps), len(stages))):
                if j < len(steps):
                    rec_step(*steps[j])
                if j < len(stages):
                    stages[j]()

    def mla_layer():
        mixer_norm(1, 65536 if FSZ >= 65536 + 6144 else 0)
        if FSZ < 65536 + 6144:
            sch.barrier()
        mem_prep(1)
        sch.barrier()
        o = 0
        cqn = fsb("cqn", [128, 3, S], BF16, o); o += 3 * S * 2
        ckvn = fsb("ckvn", [128, 2, S], BF16, o); o += 2 * S * 2
        kpe = fsb("kpe", [128, S], BF16, o); o += S * 2
        kpes = fsb("kpes", [128, S], BF16, o); o += S * 2
        cosT = fsb("cosT", [128, S], BF16, o); o += S * 2
        sinT = fsb("sinT", [128, S], BF16, o); o += S * 2
        qw1 = fsb("qw1", [128, KC, 256], BF16, o); o += KC * 256 * 2
        P0 = o
        pi32 = fsb("pi32", [128, TT], I32, o); o += TT * 4
        ang = fsb("ang", [128, TT], F32, o); o += TT * 4
        kf = fsb("kf", [128, TT], F32, o); o += TT * 4
        rr_ = fsb("rr_", [128, TT], F32, o); o += TT * 4
        msk = fsb("msk", [128, TT], F32, o); o += TT * 4
        TWO_PI = 2.0 * np.pi
        C1 = 6.28125
        C2 = TWO_PI - C1
        for tt in range(NT):
            c = slice(tt * TT, (tt + 1) * TT)
            DMA(pi32[:, :], pos_d[:, c].partition_broadcast(128), (), ["pi32"])
            for which in range(2):
                COPY("dve", ang, pi32, ["pi32"], ["ang"])
                if which == 0:
                    TS("dve", ang, ang, pc("invf"), pc("halfpi"), ALU.mult, ALU.add, ["ang", "pcol"], ["ang"])
                else:
                    TS("dve", ang, ang, pc("invf"), None, ALU.mult, ALU.bypass, ["ang", "pcol"], ["ang"])
                TS("dve", kf, ang, 1.0 / TWO_PI, None, ALU.mult, ALU.bypass, ["ang"], ["kf"])
                COPY("dve", pi32, kf, ["kf"], ["pi32"])
                COPY("dve", kf, pi32, ["pi32"], ["kf"])
                STT(rr_, kf, -C1, ang, ALU.mult, ALU.add, ["kf", "ang"], ["rr_"])
                STT(rr_, kf, -C2, rr_, ALU.mult, ALU.add, ["kf", "rr_"], ["rr_"])
                TS("dve", msk, rr_, float(np.pi), -TWO_PI, ALU.is_gt, ALU.mult, ["rr_"], ["msk"])
                TTOP("dve", rr_, rr_, msk, ALU.add, ["rr_", "msk"], ["rr_"])
                TS("dve", msk, rr_, float(-np.pi), TWO_PI, ALU.is_lt, ALU.mult, ["rr_"], ["msk"])
                TTOP("dve", rr_, rr_, msk, ALU.add, ["rr_", "msk"], ["rr_"])
                TS("dve", rr_, rr_, float(np.pi), float(-np.pi), ALU.min, ALU.max, ["rr_"], ["rr_"])
                if which == 0:
                    ACT(cosT[:, c], rr_, AF.Sin, ["rr_"], [("cosT", tt)])
                else:
                    ACT(kf, rr_, AF.Sin, ["rr_"], ["kf"])
                    TS("dve", sinT[:, c], kf, pc("sgn"), None, ALU.mult, ALU.bypass, ["kf", "pcol"], [("sinT", tt)])
                if which == 0:
                    DMA(pi32[:, :], pos_d[:, c].partition_broadcast(128), (), ["pi32"])
        wg = fsb("w1g", [128, 3, KC, 128], BF16, o); o += 3 * KC * 128 * 2
        stg = [fsb("bstg%d" % i, [128, 1024], F32, o + i * 4096) for i in range(NSTG)]; o += NSTG * 4096
        sq = [fsb("bsq%d" % i, [128, TT], BF16, o + i * TT * 2) for i in range(2)]; o += 2 * TT * 2
        tmp = fsb("btmp", [128, TT], F32, o); o += TT * 4
        rstd = fsb("brstd", [128, TT], F32, o); o += TT * 4
        assert o <= FSZ, (o, FSZ)
        groups = [([0, 1, 2], cqn, "cqn", "qag", 384), ([3, 4], ckvn, "ckvn", "kvag", 256)]
        for tiles, dst, dkey, gname, width in groups:
            for gi, t in enumerate(tiles):
                load_w(w1in_d[t], 1024, wg[:, gi, :, :].rearrange("p k n -> p (k n)"), ("w1g", gi), stg)
            for tt in range(NT):
                c = slice(tt * TT, (tt + 1) * TT)
                bs = 6 + tt % 2
                for gi, t in enumerate(tiles):
                    b = (tt % 2) * 3 + gi
                    for kc in range(KC):
                        MM(ps[b][:, :], wg[:, gi, kc, :], hT[:, kc, c], kc == 0, kc == KC - 1, [("w1g", gi), ("hT", kc, tt)], [("ps", b)])
                    ACT(sq[gi % 2], ps[b][:, :], AF.Square, [("ps", b)], [("bsq", gi % 2)])
                    MM(ps[bs][:, :], ones_bf[:, :], sq[gi % 2], gi == 0, gi == len(tiles) - 1, [("bsq", gi % 2), "ones"], [("ps", bs)])
                rstd_from_ss(bs, TT, 1.0 / width, tmp, rstd, "btmp", "brstd")
                for gi, t in enumerate(tiles):
                    STT(dst[:, gi, c], ps[(tt % 2) * 3 + gi][:, :], pc(gname, gi), rstd, ALU.mult, ALU.mult,
                        [("ps", (tt % 2) * 3 + gi), "brstd", "pcol"], [(dkey, gi, tt)])
        for t, dst, dkey in ((5, kpe, "kpe"), (6, kpes, "kpes")):
            load_w(w1in_d[t], 1024, wg[:, 0, :, :].rearrange("p k n -> p (k n)"), ("w1g", 0), stg)
            for tt in range(NT):
                c = slice(tt * TT, (tt + 1) * TT)
                b = bank()
                for kc in range(KC):
                    MM(ps[b][:, :], wg[:, 0, kc, :], hT[:, kc, c], kc == 0, kc == KC - 1, [("w1g", 0), ("hT", kc, tt)], [("ps", b)])
                COPY("act", dst[:, c], ps[b][:, :], [("ps", b)], [(dkey, tt)])
        for j, t in enumerate((7, 8)):
            load_w(w1in_d[t], 1024, wg[:, 1, :, :].rearrange("p k n -> p (k n)"), ("w1g", 1), stg)
            for kc in range(KC):
                COPY("pool", qw1[:, kc, j * 128:(j + 1) * 128], wg[:, 1, kc, :], [("w1g", 1)], [("qw1", j)])
        H_ = slice(0, 64)
        for tt in range(NT):
            c = slice(tt * TT, (tt + 1) * TT)
            STT(tmp[H_, :], kpe[H_, c], pc("kg_r", 0, H_), cosT[H_, c], ALU.mult, ALU.mult, [("kpe", tt), ("cosT", tt), "pcol"], ["btmp"])
            STT(rstd[H_, :], kpes[H_, c], pc("kg_rs", 0, H_), sinT[H_, c], ALU.mult, ALU.mult, [("kpes", tt), ("sinT", tt), "pcol"], ["brstd"])
            ACT(kpes[H_, c], kpe[H_, c], AF.Square, [("kpe", tt)], [("kpes", tt)])
            TTOP("dve", kpe[H_, c], tmp[H_, :], rstd[H_, :], ALU.add, ["btmp", "brstd"], [("kpe", tt)])
        sch.barrier()
        mem_attn(1, qw1, "qw1", P0)
        sch.barrier()
        HB = BASE + KC * S * 4
        ho = [HB]

        def hsb(name, shape, dt):
            nb = int(np.prod(shape[1:])) * (4 if dt == F32 else 2)
            t = nc.alloc_sbuf_tensor_at(name, list(shape), dt, offset=ho[0])
            ho[0] += (nb + 63) // 64 * 64
            assert ho[0] <= HB + KC * S * 2, (name, ho[0] - HB)
            return t
        Kn = hsb("Kn", [128, S], BF16)
        Kr = hsb("Kr", [128, S], BF16)
        Vh = hsb("Vh", [128, NB, 128], BF16)
        Qn = hsb("Qn", [128, S], BF16)
        Qr = hsb("Qr", [128, S], BF16)
        wq2 = [hsb("wq%d" % i, [128, 3, 256], BF16) for i in range(2)]
        wkv2 = [hsb("wkv%d" % i, [128, 2, 256], BF16) for i in range(2)]
        o = P0
        pt = [fsb("apt%d" % i, [128, TT], BF16, o + i * TT * 2) for i in range(4)]; o += 4 * TT * 2
        stg = [fsb("cstg%d" % i, [128, 2048], F32, o + i * 8192) for i in range(NSTG)]; o += NSTG * 8192
        sq = [fsb("csq%d" % i, [128, TT], BF16, o + i * TT * 2) for i in range(2)]; o += 2 * TT * 2
        tmp = fsb("ctmp", [128, TT], F32, o); o += TT * 4
        rstd = fsb("crstd", [128, TT], F32, o); o += TT * 4
        r1 = fsb("cr1", [128, TT], F32, o); o += TT * 4
        r2 = fsb("cr2", [128, TT], F32, o); o += TT * 4
        rec = fsb("crec", [128, TT], F32, o); o += TT * 4
        assert o <= FSZ, (o, FSZ)
        H = slice(0, 64)
        SC = 192.0 ** -0.5
        pti = [0]

        def rope_part(raw, rawkeys, g_r, g_rs, dst, dkey, tt):
            c = slice(tt * TT, (tt + 1) * TT)
            STT(r1[H, :], raw[0], pc(g_r, 0, H), rstd[H, :], ALU.mult, ALU.mult, rawkeys + ["crstd", "pcol"], ["cr1"])
            STT(r2[H, :], raw[1], pc(g_rs, 0, H), rstd[H, :], ALU.mult, ALU.mult, rawkeys + ["crstd", "pcol"], ["cr2"])
            TTOP("dve", r1[H, :], r1[H, :], cosT[H, c], ALU.mult, ["cr1", ("cosT", tt)], ["cr1"])
            TTOP("dve", r2[H, :], r2[H, :], sinT[H, c], ALU.mult, ["cr2", ("sinT", tt)], ["cr2"])
            TTOP("dve", dst[H, c], r1[H, :], r2[H, :], ALU.add, ["cr1", "cr2"], [(dkey, tt)])

        def load_mla_head(hh):
            load_w(wuq_d[hh], 768, wq2[hh % 2][:, :, :].rearrange("p k n -> p (k n)"), ("wq", hh % 2), stg)
            load_w(wukv_d[hh], 512, wkv2[hh % 2][:, :, :].rearrange("p k n -> p (k n)"), ("wkv", hh % 2), stg)

        load_mla_head(0)
        for hh in range(6):
            wq, wkv = wq2[hh % 2], wkv2[hh % 2]
            wqk, wkvk = ("wq", hh % 2), ("wkv", hh % 2)
            if hh + 1 < 6:
                load_mla_head(hh + 1)
            for tt in range(NT):
                c = slice(tt * TT, (tt + 1) * TT)
                bK = bank8()
                for kc in range(2):
                    MM(ps[bK][:, :], wkv[:, kc, 0:128], ckvn[:, kc, c], kc == 0, kc == 1, [wkvk, ("ckvn", kc, tt)], [("ps", bK)])
                ACT(sq[0], ps[bK][:, :], AF.Square, [("ps", bK)], [("csq", 0)])
                bs = bank8()
                MM(ps[bs][:, :], ones_bf[:, :], sq[0], True, False, [("csq", 0), "ones"], [("ps", bs)])
                MM(ps[bs][:, :], ones_bf[H, :], kpes[H, c], False, True, [("kpes", tt), "ones"], [("ps", bs)])
                rstd_from_ss(bs, TT, 1.0 / 192, tmp, rstd, "ctmp", "crstd")
                STT(Kn[:, c], ps[bK][:, :], pc("kg_n"), rstd, ALU.mult, ALU.mult, [("ps", bK), "crstd", "pcol"], [("Kn", tt)])
                TTOP("dve", Kr[H, c], kpe[H, c], rstd[H, :], ALU.mult, [("kpe", tt), "crstd"], [("Kr", tt)])
                bV = bank8()
                for j in range(4):
                    tb = tt * 4 + j
                    for kc in range(2):
                        MM(ps[bV][:, j * 128:(j + 1) * 128], ckvn[:, kc, tb * 128:(tb + 1) * 128], wkv[:, kc, 128:256], kc == 0, kc == 1,
                           [wkvk, ("ckvn", kc, tt)], [("ps", bV)])
                COPY("act", Vh[:, tt * 4:(tt + 1) * 4, :].rearrange("p j n -> p (j n)"), ps[bV][:, :], [("ps", bV)], [("Vh", tt)])
                bQ = bank8()
                for kc in range(3):
                    MM(ps[bQ][:, :], wq[:, kc, 0:128], cqn[:, kc, c], kc == 0, kc == 2, [wqk, ("cqn", kc, tt)], [("ps", bQ)])
                bR = bank8()
                for kc in range(3):
                    MM(ps[bR][H, :], wq[:, kc, 128:192], cqn[:, kc, c], kc == 0, kc == 2, [wqk, ("cqn", kc, tt)], [("ps", bR)])
                bS2 = bank8()
                for kc in range(3):
                    MM(ps[bS2][H, :], wq[:, kc, 192:256], cqn[:, kc, c], kc == 0, kc == 2, [wqk, ("cqn", kc, tt)], [("ps", bS2)])
                ACT(sq[0], ps[bQ][:, :], AF.Square, [("ps", bQ)], [("csq", 0)])
                ACT(sq[1][H, :], ps[bR][H, :], AF.Square, [("ps", bR)], [("csq", 1)])
                bs = bank8()
                MM(ps[bs][:, :], ones_bf[:, :], sq[0], True, False, [("csq", 0), "ones"], [("ps", bs)])
                MM(ps[bs][:, :], ones_bf[H, :], sq[1][H, :], False, True, [("csq", 1), "ones"], [("ps", bs)])
                rstd_from_ss(bs, TT, 1.0 / 192, tmp, rstd, "ctmp", "crstd")
                STT(Qn[:, c], ps[bQ][:, :], pc("qg_n"), rstd, ALU.mult, ALU.mult, [("ps", bQ), "crstd", "pcol"], [("Qn", tt)])
                rope_part((ps[bR][H, :], ps[bS2][H, :]), [("ps", bR), ("ps", bS2)], "qg_r", "qg_rs", Qr, "Qr", tt)
            for qt in range(NT):
                c0 = qt * TT
                nkt = 4 * (qt + 1)
                par = (hh * NT + qt) % 2
                bo, bd = 4 + par, 6 + par
                pend_pv = []

                def pv(kt, pi, q0, ktt, bo=bo, bd=bd, nkt=nkt):
                    MM(ps[bo][:, q0:TT], Vh[:, kt, :], pt[pi][:, q0:TT], kt == 0, kt == nkt - 1, [("Vh", ktt), ("apt", pi)], [("ps", bo)])
                    MM(ps[bd][:, q0:TT], ones_bf[:, :], pt[pi][:, q0:TT], kt == 0, kt == nkt - 1, ["ones", ("apt", pi)], [("ps", bd)])

                for kt in range(nkt):
                    j = kt - 4 * qt
                    q0 = max(0, j) * 128
                    ktt = kt // 4
                    bSc = bank()
                    MM(ps[bSc][:, q0:TT], Kn[:, kt * 128:(kt + 1) * 128], Qn[:, c0 + q0:c0 + TT], True, False,
                       [("Kn", ktt), ("Qn", qt)], [("ps", bSc)])
                    MM(ps[bSc][:, q0:TT], Kr[H, kt * 128:(kt + 1) * 128], Qr[H, c0 + q0:c0 + TT], False, True,
                       [("Kr", ktt), ("Qr", qt)], [("ps", bSc)])
                    pi = pti[0] % 4
                    pti[0] += 1
                    ACT(pt[pi][:, q0:TT], ps[bSc][:, q0:TT], AF.Exp, [("ps", bSc)], [("apt", pi)], scale=SC)
                    if j >= 0:
                        A("pool", lambda h, pi=pi, q0=q0: h.affine_select(out=pt[pi][:, q0:q0 + 128], in_=pt[pi][:, q0:q0 + 128], pattern=[[1, 128]],
                                                                           compare_op=ALU.is_ge, fill=0.0, base=0, channel_multiplier=-1),
                          [("apt", pi)], [("apt", pi)])
                    pend_pv.append((kt, pi, q0, ktt))
                    if len(pend_pv) > 2:
                        pv(*pend_pv.pop(0))
                while pend_pv:
                    pv(*pend_pv.pop(0))
                A("dve", lambda h, bd=bd: h.reciprocal(out=rec[:, :], in_=ps[bd][:, :]), [("ps", bd)], ["crec"])
                TTOP("dve", yT[:, hh, c0:c0 + TT], ps[bo][:, :], rec[:, :], ALU.mult, [("ps", bo), "crec"], [("yT", hh, qt)])
        sch.barrier()
        out_proj(1, 0, ffn_norm_cb(1, KC * D * 2 + NSTG * 8192))
        sch.barrier()
        ffn(1)
        sch.barrier()

    def mixer_norm(l, o=0):
        sq = [fsb("nsq%d_%d" % (l, i), [128, TT], BF16, o + i * TT * 2) for i in range(2)]; o += 2 * TT * 2
        tmp = fsb("ntmp%d" % l, [128, TT], F32, o); o += TT * 4
        rstd = fsb("nrstd%d" % l, [128, TT], F32, o); o += TT * 4
        norm_full(xT, "xT", "mixg%d" % l, hT, "hT", sq, tmp, rstd)

    mem_prep(0)
    for tt in range(NT):
        for kc in range(KC):
            DMA(xT[:, kc, tt * TT:(tt + 1) * TT], xT_d[kc * 128:(kc + 1) * 128, tt * TT:(tt + 1) * TT], (), [("xT", kc, tt)])
    mixer_norm(0, 65536 if FSZ >= 65536 + 6144 else 0)
    sch.barrier()
    hgrn()
    sch.barrier()
    qw = fsb("qw0", [128, KC, 256], BF16, 0)
    _st = fsb("qstg0", [128, 2048], F32, 4096)
    stg0 = [_st] * NSTG
    load_w(w0m_d, 2048, qw[:, :, :].rearrange("p k n -> p (k n)"), "qw", stg0)
    o_end = mem_attn(0, qw, "qw", 4096 + 8192)
    if o_end + KC * D * 2 + NSTG * 8192 + 6144 <= FSZ:
        out_proj(0, o_end, ffn_norm_cb(0, o_end + KC * D * 2 + NSTG * 8192))
    else:
        sch.barrier()
        out_proj(0, 0, ffn_norm_cb(0, KC * D * 2 + NSTG * 8192))
    sch.barrier()
    ffn(0)
    sch.barrier()
    if nlayers > 1:
        mla_layer()
    outs = []
    for name, (t, key, shape, dt) in DBG.items():
        outs.append(dump(name, t, key, shape, dt))
    for kc in range(KC):
        outs.append(DMA(out_d[kc * 128:(kc + 1) * 128, :], xT[:, kc, :], [("xT", kc)], [("outd", kc)]))
    sch.emit(nc, final_wait_ops=outs)
    return nc


def make_in_maps(inputs, S, ncores):
    shared, pidx = pack_shared(inputs)
    x = np.asarray(inputs["x"], np.float32)
    mem = np.asarray(inputs["mem"], np.float32)
    pos = np.asarray(inputs["positions"]).astype(np.int32)
    in_maps = []
    for b in range(ncores):
        m = dict(shared)
        m["xT"] = np.ascontiguousarray(x[b].T)
        m["memT"] = np.ascontiguousarray(mem[b].T)
        m["pos"] = np.ascontiguousarray(pos[b][None, :])
        in_maps.append(m)
    return in_maps, pidx, shared["pcol"].shape[1]


def kernel(**inputs):
    x = np.asarray(inputs["x"])
    B, S, _ = x.shape
    in_maps, pidx, ncol = make_in_maps(inputs, S, B)
    nc = build(S, pidx, ncol)
    res = run_bass_kernel_spmd(nc, in_maps, core_ids=list(range(B)))
    out = np.stack([np.ascontiguousarray(res.results[b]["outT"].T) for b in range(B)])
    return out.astype(np.float32)
```

```python
import contextlib
import numpy as np
import concourse.bass as bass
import concourse.mybir as mybir
from concourse.bass_utils import run_bass_kernel_spmd

F32 = mybir.dt.float32
BF16 = mybir.dt.bfloat16
I32 = mybir.dt.int32
AF = mybir.ActivationFunctionType
ALU = mybir.AluOpType

D = 1024
KC = 8
MEM = 256
DFF = 2816
NPAIR = 22
EPS = 1e-6
ENGS = ("pe", "act", "dve", "pool", "sp")
NDSEM = 24


class Op:
    __slots__ = ("eng", "idx", "fn", "waits", "dwaits", "need_inc", "incval", "is_dma", "dma_id", "known", "dknown")

    def __init__(self, eng, idx, fn, is_dma):
        self.eng = eng
        self.idx = idx
        self.fn = fn
        self.waits = {}
        self.dwaits = []
        self.need_inc = False
        self.incval = 0
        self.is_dma = is_dma
        self.dma_id = -1
        self.known = None
        self.dknown = None


class Sched:
    def __init__(self):
        self.ops = {e: [] for e in ENGS}
        self.known = {e: {f: -1 for f in ENGS} for e in ENGS}
        self.dknown = {e: set() for e in ENGS}
        self.regs = {}
        self.ndma = 0
        self.dma_ops = []
        self.pending = {e: [] for e in ENGS}

    @staticmethod
    def _conf(k1, k2):
        n = min(len(k1), len(k2))
        return k1[:n] == k2[:n]

    def add(self, eng, fn, reads=(), writes=(), dma=False):
        op = Op(eng, len(self.ops[eng]), fn, dma)
        reads = [k if isinstance(k, tuple) else (k,) for k in reads]
        writes = [k if isinstance(k, tuple) else (k,) for k in writes]
        deps = []
        odeps = []
        for k in reads:
            root = self.regs.setdefault(k[0], {})
            for kk, ent in root.items():
                if self._conf(kk, k):
                    deps.extend(ent[0])
        for k in writes:
            root = self.regs.setdefault(k[0], {})
            for kk, ent in root.items():
                if self._conf(kk, k):
                    odeps.extend(ent[0])
                    odeps.extend(ent[1])
        deps.extend(odeps)
        pend = self.pending[eng]
        if pend:
            deps.extend(pend)
            self.pending[eng] = []
        for k in reads:
            ent = self.regs[k[0]].setdefault(k, [[], []])
            ent[1].append(op)
        for k in writes:
            root = self.regs[k[0]]
            for kk in list(root.keys()):
                if kk != k and self._conf(kk, k):
                    if len(kk) >= len(k):
                        del root[kk]
                    else:
                        root[kk][0].append(op)
            root[k] = [[op], []]
        kn = self.known[eng]
        dk = self.dknown[eng]
        for d in deps:
            if d is op:
                continue
            if d.is_dma:
                if d.dma_id in dk:
                    continue
                if d not in op.dwaits:
                    op.dwaits.append(d)
            else:
                if d.eng == eng and eng in ("pe", "sp"):
                    continue
                if kn[d.eng] >= d.idx:
                    continue
                cur = op.waits.get(d.eng)
                if cur is None or cur.idx < d.idx:
                    op.waits[d.eng] = d
        if dma:
            op.dma_id = self.ndma
            self.ndma += 1
            self.dma_ops.append(op)
            if op.dma_id >= NDSEM:
                prev = self.dma_ops[op.dma_id - NDSEM]
                if prev.dma_id not in dk and prev not in op.dwaits:
                    op.dwaits.append(prev)
            op.need_inc = True
        for d in op.waits.values():
            d.need_inc = True
            if kn[d.eng] < d.idx:
                kn[d.eng] = d.idx
            for f, v in d.known.items():
                if f != eng and kn[f] < v:
                    kn[f] = v
            dk |= d.dknown
        for d in op.dwaits:
            dk.add(d.dma_id)
            for f, v in d.known.items():
                if f != eng and kn[f] < v:
                    kn[f] = v
            dk |= d.dknown
        op.known = dict(kn)
        op.dknown = set(dk)
        self.ops[eng].append(op)
        return op

    def barrier(self):
        lasts = []
        for f in ENGS:
            if f == "sp":
                continue
            if self.ops[f]:
                lasts.append(self.ops[f][-1])
        for e in ENGS:
            self.pending[e] = [o for o in lasts if o.eng != e] + list(self.dma_ops)

    def emit(self, nc, final_wait_ops=()):
        for e in ENGS:
            c = 0
            for op in self.ops[e]:
                if op.is_dma:
                    op.incval = 16 * (op.dma_id // NDSEM + 1)
                elif op.need_inc:
                    c += 1
                    op.incval = c
        with contextlib.ExitStack() as st:
            sems = {e: st.enter_context(nc.semaphore("s_" + e)) for e in ENGS}
            dsems = [st.enter_context(nc.semaphore("d%d" % i)) for i in range(NDSEM)]
            block = st.enter_context(nc.Block())

            def run(e, h):
                for op in self.ops[e]:
                    for d in op.waits.values():
                        h.wait_ge(sems[d.eng], d.incval)
                    for d in op.dwaits:
                        h.wait_ge(dsems[d.dma_id % NDSEM], d.incval)
                    ins = op.fn(h)
                    if op.is_dma:
                        ins.then_inc(dsems[op.dma_id % NDSEM], 16)
                    elif op.need_inc:
                        ins.then_inc(sems[e], 1)
                if e == "sp":
                    for d in final_wait_ops:
                        h.wait_ge(dsems[d.dma_id % NDSEM], d.incval)

            @block.tensor
            def _(h):
                run("pe", h)

            @block.scalar
            def _(h):
                run("act", h)

            @block.vector
            def _(h):
                run("dve", h)

            @block.gpsimd
            def _(h):
                run("pool", h)

            @block.sync
            def _(h):
                run("sp", h)


def _cols(vec, n=128):
    v = np.asarray(vec, np.float32).reshape(-1, n)
    return np.ascontiguousarray(v.T)


def _wpack(w, kc):
    n = w.shape[1]
    return np.ascontiguousarray(w.reshape(kc, 128, n).transpose(1, 0, 2).reshape(128, kc * n))


class PCol:
    def __init__(self):
        self.cols = []
        self.idx = {}

    def put(self, name, arr):
        arr = np.asarray(arr, np.float32)
        if arr.ndim == 1:
            arr = arr[:, None]
        assert arr.shape[0] == 128, (name, arr.shape)
        self.idx[name] = (sum(c.shape[1] for c in self.cols), arr.shape[1])
        self.cols.append(arr)

    def array(self):
        return np.ascontiguousarray(np.concatenate(self.cols, axis=1))


def pack_shared(inp):
    pc = PCol()
    f = lambda a: np.asarray(a, np.float32)
    for l in range(2):
        pc.put("mixg%d" % l, _cols(f(inp["mix_norm_g"])[l]))
        pc.put("ffng%d" % l, _cols(f(inp["ffn_norm_g"])[l]))
        pc.put("memg%d" % l, _cols(f(inp["mem_norm_g"])[l]))
        pc.put("mqg%d" % l, np.tile(f(inp["mem_q_norm_g"])[l], 2))
        pc.put("mkg%d" % l, np.tile(f(inp["mem_k_norm_g"])[l], 2))
        cw = f(inp["conv_w"])[l]
        for j in range(3):
            pc.put("cw%d_%d" % (j, l), _cols(cw[j]))
        pc.put("cb%d" % l, _cols(f(inp["conv_b"])[l]))
    lbl = f(inp["hg_lb_logits"])
    for j in range(3):
        pc.put("lbl%d" % j, _cols(lbl[j]))
    pc.put("hgo", f(inp["hg_out_norm_g"])[0])
    pc.put("qag", _cols(f(inp["mla_qa_norm_g"])[0]))
    pc.put("kvag", _cols(f(inp["mla_kva_norm_g"])[0]))
    qg = f(inp["mla_q_norm_g"])[0]
    kg = f(inp["mla_k_norm_g"])[0]
    sw = np.concatenate([np.arange(160, 192), np.arange(128, 160)])
    pc.put("qg_n", qg[:128])
    pc.put("qg_r", np.tile(qg[128:192], 2))
    pc.put("qg_rs", np.tile(qg[sw], 2))
    pc.put("kg_n", kg[:128])
    pc.put("kg_r", np.tile(kg[128:192], 2))
    pc.put("kg_rs", np.tile(kg[sw], 2))
    inv = (10000.0 ** (-np.arange(32, dtype=np.float32) / 32)).astype(np.float32)
    pc.put("invf", np.tile(inv, 4))
    pc.put("sgn", np.tile(np.concatenate([-np.ones(32), np.ones(32)]), 2))
    pc.put("eps", np.full(128, EPS))
    pc.put("halfpi", np.full(128, np.pi / 2))

    out = {"pcol": pc.array()}
    w = f(inp["hg_w_in"])[0]
    units = []
    for h in range(6):
        cs = np.concatenate([np.arange(h * 128, h * 128 + 128), 768 + np.arange(h * 128, h * 128 + 128),
                             2304 + np.arange(h * 128, h * 128 + 128), 1536 + np.arange(h * 128, h * 128 + 128)])
        units.append(_wpack(w[:, cs], 8))
    out["w0"] = np.stack(units)
    out["w0m"] = _wpack(w[:, 3072:3328], 8)
    out["wmem"] = np.stack([_wpack(f(inp["w_mem_kv"])[l], 8) for l in range(2)])
    out["wout"] = np.stack([_wpack(f(inp["w_out"])[l], 8) for l in range(2)])
    wup = f(inp["w_up"])
    wdn = f(inp["w_down"])
    ups, dns = [], []
    for l in range(2):
        for i in range(NPAIR):
            cs = np.concatenate([np.arange(i * 128, i * 128 + 128), DFF + np.arange(i * 128, i * 128 + 128)])
            ups.append(_wpack(wup[l][:, cs], 8))
            dns.append(np.ascontiguousarray(wdn[l][i * 128:(i + 1) * 128, :]))
    out["wup"] = np.stack(ups).reshape(2, NPAIR, 128, 2048)
    out["wdn"] = np.stack(dns).reshape(2, NPAIR, 128, 1024)
    w1 = f(inp["mla_w_in"])[0]
    swk = np.concatenate([np.arange(32, 64), np.arange(0, 32)])
    tiles = [w1[:, 0:128], w1[:, 128:256], w1[:, 256:384], w1[:, 384:512], w1[:, 512:640],
             np.concatenate([w1[:, 640:704], w1[:, 640:704]], axis=1),
             np.concatenate([w1[:, 640 + swk], w1[:, 640 + swk]], axis=1),
             w1[:, 704:832], w1[:, 832:960]]
    out["w1in"] = np.stack([_wpack(t, 8) for t in tiles])
    wuq = f(inp["mla_w_uq"])[0]
    wukv = f(inp["mla_w_ukv"])[0]
    uqs, ukvs = [], []
    for h in range(6):
        b = h * 192
        cs = np.concatenate([np.arange(b, b + 128), np.arange(b + 128, b + 192), b + 128 + swk])
        uqs.append(_wpack(wuq[:, cs], 3))
        ukvs.append(_wpack(wukv[:, h * 256:(h + 1) * 256], 2))
    out["wuq"] = np.stack(uqs)
    out["wukv"] = np.stack(ukvs)
    return out, pc.idx


def build(S, pidx, ncol, nlayers=2, dbg=()):
    nc = bass.Bass("TRN2", target_bir_lowering=False)
    TT = 512
    NT = S // TT
    NB = S // 128
    NCH = S // 64
    sch = Sched()
    A = sch.add

    def dram(name, shape, dt=F32, kind="ExternalInput"):
        return nc.dram_tensor(name, list(shape), dt, kind=kind).ap()

    xT_d = dram("xT", [D, S])
    memT_d = dram("memT", [D, MEM])
    pos_d = dram("pos", [1, S], I32)
    pcol_d = dram("pcol", [128, ncol])
    w0_d = dram("w0", [6, 128, 4096])
    w0m_d = dram("w0m", [128, 2048])
    wmem_d = dram("wmem", [2, 128, 4096])
    wout_d = dram("wout", [2, 128, 8192])
    wup_d = dram("wup", [2, NPAIR, 128, 2048])
    wdn_d = dram("wdn", [2, NPAIR, 128, 1024])
    w1in_d = dram("w1in", [9, 128, 1024])
    wuq_d = dram("wuq", [6, 128, 768])
    wukv_d = dram("wukv", [6, 128, 512])
    out_d = dram("outT", [D, S], F32, "ExternalOutput")
    DBG = {}

    BASE = 16640
    LIMIT = 229376
    cur = [BASE]

    def sb(name, shape, dt, off=None):
        nbytes = int(np.prod(shape[1:])) * (4 if dt in (F32, I32) else 2)
        if off is None:
            off = cur[0]
            cur[0] += (nbytes + 63) // 64 * 64
            assert cur[0] <= LIMIT, (name, cur[0])
        assert off + nbytes <= LIMIT, (name, off, nbytes)
        return nc.alloc_sbuf_tensor_at(name, list(shape), dt, offset=off)

    xT = sb("xTs", [128, KC, S], F32)
    hT = sb("hTs", [128, KC, S], BF16)
    yT = sb("yTs", [128, KC, S], BF16)
    pcol = sb("pcols", [128, ncol], F32)
    ones_bf = sb("ones", [128, 128], BF16)
    ident_bf = sb("ident", [128, 128], BF16)
    tri_bf = sb("tri", [128, 128], BF16)
    tri64 = sb("tri64", [128, 64], BF16)
    blk_bf = sb("blk", [128, 128], BF16)
    kn_m = sb("kn_m", [128, 2, MEM], BF16)
    vm_m = sb("vm_m", [128, 2, 256], BF16)
    lbc = sb("lbc", [128, 12], F32)
    F0 = cur[0]
    FSZ = LIMIT - F0

    def fsb(name, shape, dt, off):
        return sb(name, shape, dt, F0 + off)

    ps = [nc.alloc_psum_tensor("ps%d" % i, [128, 512], F32) for i in range(8)]
    rr = [0]

    def bank():
        b = rr[0]
        rr[0] = (rr[0] + 1) % 4
        return b

    rr8 = [0]

    def bank8():
        b = rr8[0]
        rr8[0] = (rr8[0] + 1) % 8
        return b

    def pc(name, j=0, rows=slice(0, 128)):
        o, n = pidx[name]
        return pcol[rows, o + j:o + j + 1]

    def _ap(x):
        if x is None or isinstance(x, (int, float)) or isinstance(x, bass.AP):
            return x
        return x[:]

    def MM(out, lhsT, rhs, start, stop, r, w):
        out, lhsT, rhs = _ap(out), _ap(lhsT), _ap(rhs)
        return A("pe", lambda h: h.matmul(out, lhsT=lhsT, rhs=rhs, start=start, stop=stop), r, w)

    def ACT(out, in_, func, r, w, bias=None, scale=None):
        kw = {}
        out, in_ = _ap(out), _ap(in_)
        if bias is not None:
            kw["bias"] = _ap(bias)
        if scale is not None:
            kw["scale"] = _ap(scale)
        return A("act", lambda h: h.activation(out=out, in_=in_, func=func, **kw), r, w)

    def TTOP(eng, out, in0, in1, op, r, w):
        out, in0, in1 = _ap(out), _ap(in0), _ap(in1)
        return A(eng, lambda h: h.tensor_tensor(out=out, in0=in0, in1=in1, op=op), r, w)

    def TS(eng, out, in0, s1, s2, op0, op1, r, w):
        out, in0, s1, s2 = _ap(out), _ap(in0), _ap(s1), _ap(s2)
        return A(eng, lambda h: h.tensor_scalar(out=out, in0=in0, scalar1=s1, scalar2=s2, op0=op0, op1=op1), r, w)

    def STT(out, in0, scalar, in1, op0, op1, r, w):
        out, in0, scalar, in1 = _ap(out), _ap(in0), _ap(scalar), _ap(in1)
        return A("dve", lambda h: h.scalar_tensor_tensor(out=out, in0=in0, scalar=scalar, in1=in1, op0=op0, op1=op1), r, w)

    def COPY(eng, out, in_, r, w):
        out, in_ = _ap(out), _ap(in_)
        if eng == "act":
            return A("act", lambda h: h.copy(out=out, in_=in_), r, w)
        return A(eng, lambda h: h.tensor_copy(out=out, in_=in_), r, w)

    def DMA(out, in_, r, w):
        out, in_ = _ap(out), _ap(in_)
        return A("sp", lambda h: h.dma_start(out=out, in_=in_), r, w, dma=True)

    def MEMSET(ap, val, w):
        return A("pool", lambda h: h.memset(ap, val), (), w)

    NSTG = 2
    stg_i = [0]

    def load_w(dram_ap, n, dst_flat, wkey, stg):
        p = 0
        while p < n:
            m = min(2048, n - p)
            s = stg_i[0] % NSTG
            stg_i[0] += 1
            DMA(stg[s][:, 0:m], dram_ap[:, p:p + m], (), [("stg", s)])
            COPY("pool", dst_flat[:, p:p + m], stg[s][:, 0:m], [("stg", s)], [wkey])
            p += m

    DMA(pcol[:, :], pcol_d[:, :], (), ["pcol"])
    MEMSET(ones_bf[:, :], 1.0, ["ones"])
    A("pool", lambda h: h.affine_select(out=ident_bf[:, :], in_=ones_bf[:, :], pattern=[[-1, 128]], compare_op=ALU.is_equal,
                                        fill=0.0, base=0, channel_multiplier=1), ["ones"], ["ident"])
    A("pool", lambda h: h.affine_select(out=tri_bf[:, :], in_=ones_bf[:, :], pattern=[[1, 128]], compare_op=ALU.is_ge,
                                        fill=0.0, base=0, channel_multiplier=-1), ["ones"], ["tri"])
    for hb in (0, 64):
        A("pool", lambda h, hb=hb: h.affine_select(out=tri64[hb:hb + 64, :], in_=ones_bf[hb:hb + 64, 0:64], pattern=[[1, 64]],
                                                   compare_op=ALU.is_ge, fill=0.0, base=0, channel_multiplier=-1), ["ones"], [("tri64", hb)])
    MEMSET(blk_bf[:, :], 0.0, ["blk"])
    MEMSET(blk_bf[0:64, 0:64], 1.0, ["blk"])
    MEMSET(blk_bf[64:128, 64:128], 1.0, ["blk"])

    def dump(name, ap, key, shape, dt=F32):
        d = dram("dbg_" + name, shape, dt, "ExternalOutput")
        return DMA(d, ap, [key], [("dbgout", name)])

    def rstd_from_ss(ssbank, width, inv_n, tmp, rstd, keyt, keyr, rows=slice(0, 128)):
        ACT(tmp, ps[ssbank][rows, 0:width], AF.Ln, [("ps", ssbank), "pcol"], [keyt], bias=pc("eps", 0, rows), scale=inv_n)
        ACT(rstd, tmp, AF.Exp, [keyt], [keyr], scale=-0.5)

    def norm_tile(tt, src, skey, gname, dst, dkey, sq, tmp, rstd):
        c = slice(tt * TT, (tt + 1) * TT)
        b = bank()
        for kc in range(KC):
            s_ = kc % 2
            ACT(sq[s_], src[:, kc, c], AF.Square, [(skey, kc, tt)], [("sq", id(sq), s_)])
            MM(ps[b][:, :], ones_bf[:, :], sq[s_], kc == 0, kc == KC - 1, [("sq", id(sq), s_), "ones"], [("ps", b)])
        rstd_from_ss(b, TT, 1.0 / D, tmp, rstd, ("ntmp", id(tmp)), ("nrstd", id(rstd)))
        for kc in range(KC):
            STT(dst[:, kc, c], src[:, kc, c], pc(gname, kc), rstd, ALU.mult, ALU.mult,
                [(skey, kc, tt), ("nrstd", id(rstd)), "pcol"], [(dkey, kc, tt)])

    def norm_full(src, skey, gname, dst, dkey, sq, tmp, rstd):
        for tt in range(NT):
            norm_tile(tt, src, skey, gname, dst, dkey, sq, tmp, rstd)

    def mem_prep(l):
        o = 0
        memT = fsb("memT", [128, KC, MEM], F32, o); o += KC * MEM * 4
        memn = fsb("memn", [128, KC, MEM], BF16, o); o += KC * MEM * 2
        wm = fsb("wm", [128, KC, 512], BF16, o); o += KC * 512 * 2
        stg = [fsb("mstg%d" % i, [128, 2048], F32, o + i * 8192) for i in range(NSTG)]; o += NSTG * 8192
        sq = [fsb("msq%d" % i, [128, MEM], BF16, o + i * 512) for i in range(2)]; o += 1024
        tmp = fsb("mtmp", [128, MEM], F32, o); o += 1024
        rstd = fsb("mrstd", [128, MEM], F32, o); o += 1024
        kraw = fsb("mkraw", [128, MEM], F32, o); o += 1024
        for kc in range(KC):
            DMA(memT[:, kc, :], memT_d[kc * 128:(kc + 1) * 128, :], (), [("memT", kc)])
        load_w(wmem_d[l], 4096, wm[:, :, :].rearrange("p k n -> p (k n)"), "wm", stg)
        b = bank()
        for kc in range(KC):
            s = kc % 2
            ACT(sq[s], memT[:, kc, :], AF.Square, [("memT", kc)], [("msq", s)])
            MM(ps[b][:, 0:MEM], ones_bf[:, :], sq[s], kc == 0, kc == KC - 1, [("msq", s), "ones"], [("ps", b)])
        rstd_from_ss(b, MEM, 1.0 / D, tmp, rstd, "mtmp", "mrstd")
        for kc in range(KC):
            STT(memn[:, kc, :], memT[:, kc, :], pc("memg%d" % l, kc), rstd, ALU.mult, ALU.mult,
                [("memT", kc), "mrstd", "pcol"], [("memn", kc)])
        for j in range(2):
            b = bank()
            for kc in range(KC):
                MM(ps[b][:, 0:MEM], wm[:, kc, j * 128:(j + 1) * 128], memn[:, kc, :], kc == 0, kc == KC - 1,
                   ["wm", ("memn", kc)], [("ps", b)])
            COPY("act", kraw, ps[b][:, 0:MEM], [("ps", b)], ["mkraw"])
            ACT(sq[0], ps[b][:, 0:MEM], AF.Square, [("ps", b)], [("msq", 0)])
            b2 = bank()
            MM(ps[b2][:, 0:MEM], blk_bf[:, :], sq[0], True, True, [("msq", 0), "blk"], [("ps", b2)])
            rstd_from_ss(b2, MEM, 1.0 / 64, tmp, rstd, "mtmp", "mrstd")
            STT(kn_m[:, j, :], kraw, pc("mkg%d" % l), rstd, ALU.mult, ALU.mult, ["mkraw", "mrstd", "pcol"], [("kn_m", j)])
        for mt in range(2):
            b = bank()
            for kc in range(KC):
                MM(ps[b][:, 0:256], memn[:, kc, mt * 128:(mt + 1) * 128], wm[:, kc, 256:512], kc == 0, kc == KC - 1,
                   ["wm", ("memn", kc)], [("ps", b)])
            COPY("act", vm_m[:, mt, :], ps[b][:, 0:256], [("ps", b)], [("vm_m", mt)])

    def mem_attn(l, qw, qwkey, o):
        qraw = fsb("qraw%d" % l, [128, TT], F32, o); o += TT * 4
        sq = fsb("qsq%d" % l, [128, TT], BF16, o); o += TT * 2
        tmp = fsb("qtmp%d" % l, [128, TT], F32, o); o += TT * 4
        rstd = fsb("qrstd%d" % l, [128, TT], F32, o); o += TT * 4
        qn = [fsb("qn%d_%d" % (l, i), [128, TT], BF16, o + i * TT * 2) for i in range(2)]; o += 2 * TT * 2
        pt = [fsb("pt%d_%d" % (l, i), [128, TT], BF16, o + i * TT * 2) for i in range(4)]; o += 4 * TT * 2
        rec2 = [fsb("rec%d_%d" % (l, i), [128, TT], F32, o + i * TT * 4) for i in range(2)]; o += 2 * TT * 4
        pti = [0]
        pend_pv = []

        def pv_unit(tt, j, e, pts):
            c = slice(tt * TT, (tt + 1) * TT)
            hh = 2 * j + e
            pb = 64 * e
            bo = 4 + 2 * ((tt * 2 + j) % 2)
            bd = bo + 1
            for mt in range(2):
                MM(ps[bo][pb:pb + 64, :], vm_m[:, mt, hh * 64:(hh + 1) * 64], pt[pts[mt]], mt == 0, mt == 1,
                   [("vm_m", mt), ("pt", pts[mt])], [("ps", bo, e)])
            for mt in range(2):
                MM(ps[bd][pb:pb + 64, :], ones_bf[:, 0:64], pt[pts[mt]], mt == 0, mt == 1,
                   ["ones", ("pt", pts[mt])], [("ps", bd, e)])
            rk = ("rec", (tt * 2 + j) % 2, e)
            rc = rec2[(tt * 2 + j) % 2]
            A("dve", lambda h, bd=bd, pb=pb, rc=rc: h.reciprocal(out=rc[pb:pb + 64, :], in_=ps[bd][pb:pb + 64, :]),
              [("ps", bd, e)], [rk])
            TTOP("dve", yT[pb:pb + 64, 6 + j, c], ps[bo][pb:pb + 64, :], rc[pb:pb + 64, :], ALU.mult,
                 [("ps", bo, e), rk], [("yT", 6 + j, tt, e)])

        for tt in range(NT):
            c = slice(tt * TT, (tt + 1) * TT)
            for j in range(2):
                b = bank()
                for kc in range(KC):
                    MM(ps[b][:, :], qw[:, kc, j * 128:(j + 1) * 128], hT[:, kc, c], kc == 0, kc == KC - 1,
                       [qwkey, ("hT", kc, tt)], [("ps", b)])
                COPY("act", qraw, ps[b][:, :], [("ps", b)], ["qraw"])
                ACT(sq, ps[b][:, :], AF.Square, [("ps", b)], ["qsq"])
                b2 = bank()
                MM(ps[b2][:, :], blk_bf[:, :], sq, True, True, ["qsq", "blk"], [("ps", b2)])
                rstd_from_ss(b2, TT, 1.0 / 64, tmp, rstd, "qtmp", "qrstd")
                STT(qn[j], qraw, pc("mqg%d" % l), rstd, ALU.mult, ALU.mult, ["qraw", "qrstd", "pcol"], [("qn", j)])
                for e in range(2):
                    hh = 2 * j + e
                    pb = 64 * e
                    pts = []
                    for mt in range(2):
                        b = bank()
                        MM(ps[b][:, :], kn_m[pb:pb + 64, j, mt * 128:(mt + 1) * 128], qn[j][pb:pb + 64, :], True, True,
                           [("kn_m", j), ("qn", j)], [("ps", b)])
                        pi = pti[0] % 4
                        pti[0] += 1
                        ACT(pt[pi], ps[b][:, :], AF.Exp, [("ps", b)], [("pt", pi)], scale=0.125)
                        pts.append(pi)
                    pend_pv.append((tt, j, e, pts))
                    if len(pend_pv) > 1:
                        pv_unit(*pend_pv.pop(0))
        while pend_pv:
            pv_unit(*pend_pv.pop(0))
        return o

    def out_proj(l, o, tile_cb=None):
        wo = fsb("wo%d" % l, [128, KC, D], BF16, o); o += KC * D * 2
        stg = [fsb("ostg%d_%d" % (l, i), [128, 2048], F32, o + i * 8192) for i in range(NSTG)]; o += NSTG * 8192
        load_w(wout_d[l], KC * D, wo[:, :, :].rearrange("p k n -> p (k n)"), "wo", stg)
        for tt in range(NT):
            if tile_cb is not None and tt > 0:
                tile_cb(tt - 1)
            c = slice(tt * TT, (tt + 1) * TT)
            for d in range(KC):
                b = bank()
                for kc in range(KC):
                    MM(ps[b][:, :], wo[:, kc, d * 128:(d + 1) * 128], yT[:, kc, c], kc == 0, kc == KC - 1,
                       ["wo", ("yT", kc, tt)], [("ps", b)])
                TTOP("dve", xT[:, d, c], ps[b][:, :], xT[:, d, c], ALU.add, [("ps", b), ("xT", d, tt)], [("xT", d, tt)])
        if tile_cb is not None:
            tile_cb(NT - 1)
        return o

    def ffn_norm_cb(l, o):
        sq = [fsb("fsq%d_%d" % (l, i), [128, TT], BF16, o + i * TT * 2) for i in range(2)]; o += 2 * TT * 2
        tmp = fsb("ftmp%d" % l, [128, TT], F32, o); o += TT * 4
        rstd = fsb("frstd%d" % l, [128, TT], F32, o); o += TT * 4
        assert o <= FSZ, (o, FSZ)
        return lambda tt: norm_tile(tt, xT, "xT", "ffng%d" % l, hT, "hT", sq, tmp, rstd)

    def ffn(l):
        G = 4
        o = 0
        NUP, NDN, NTR = 2, 3 * G, 4
        wu = [fsb("wu%d_%d" % (l, i), [128, KC, 256], BF16, o + i * 4096) for i in range(NUP)]; o += NUP * 4096
        wd = [fsb("wd%d_%d" % (l, i), [128, D], BF16, o + i * 2048) for i in range(NDN)]; o += NDN * 2048
        stg = [fsb("fstg%d_0" % l, [128, 2048], F32, o), fsb("fstg%d_1" % l, [128, 1024], F32, o + 8192)]; o += 8192 + 4096
        hal = [[fsb("hal%d_%d_%d" % (l, ab, i), [128, 8], F32, o + (ab * 2 + i) * 32) for i in range(2)] for ab in range(2)]
        o += 4 * 32
        tr = [[fsb("tr%d_%d_%d" % (l, ab, i), [128, TT], F32, o + (ab * NTR + i) * TT * 4) for i in range(NTR)] for ab in range(2)]
        o += 2 * NTR * TT * 4
        sa = [fsb("sa%d_%d" % (l, i), [128, TT], BF16, o + i * TT * 2) for i in range(2)]; o += 2 * TT * 2
        tb16 = [fsb("tb16_%d_%d" % (l, i), [128, TT], BF16, o + i * TT * 2) for i in range(NTR)]; o += NTR * TT * 2
        assert o <= FSZ, (o, FSZ)
        if dbg and l == 0:
            for ab in range(2):
                for i in range(NTR):
                    DBG["tr%d_%d" % (ab, i)] = (tr[ab][i], ("tr", ab, i), [128, TT], F32)
            for i in range(2):
                DBG["sa_%d" % i] = (sa[i], ("sa", i), [128, TT], BF16)
        actg = [nc.alloc_sbuf_tensor_at("actg%d_%d" % (l, i), [128, G, S], BF16,
                                        offset=BASE + KC * S * 4 + KC * S * 2 + i * G * S * 2) for i in range(2)]
        groups = [list(range(g, min(g + G, NPAIR))) for g in range(0, NPAIR, G)]
        dbank = [0]

        def dma_pair(i):
            DMA(stg[0][:, 0:2048], wup_d[l, i], (), [("stg", 0)])
            DMA(stg[1][:, 0:1024], wdn_d[l, i], (), [("stg", 1)])

        def cast_pair(i):
            COPY("act", wu[i % NUP][:, :, :].rearrange("p k n -> p (k n)"), stg[0][:, 0:2048], [("stg", 0)], [("wu", i % NUP)])
            COPY("act", wd[i % NDN][:, :], stg[1][:, 0:1024], [("stg", 1)], [("wd", i % NDN)])

        def down_mm(gi, d, tt):
            grp = groups[gi]
            ag = actg[gi % 2]
            c = slice(tt * TT, (tt + 1) * TT)
            b = 6 + dbank[0] % 2
            dbank[0] += 1
            for ii, i in enumerate(grp):
                MM(ps[b][:, :], wd[i % NDN][:, d * 128:(d + 1) * 128], ag[:, ii, c], ii == 0, ii == len(grp) - 1,
                   [("wd", i % NDN), ("actg", gi % 2, ii, tt)], [("ps", b)])
            return (b, d, tt)

        def down_add(b, d, tt):
            c = slice(tt * TT, (tt + 1) * TT)
            TTOP("dve", xT[:, d, c], ps[b][:, :], xT[:, d, c], ALU.add, [("ps", b), ("xT", d, tt)], [("xT", d, tt)])

        def stage_d(k, gi, ii, tt):
            c = slice(tt * TT, (tt + 1) * TT)
            ACT(sa[k % 2], tr[0][k % NTR], AF.Silu, [("tr", 0, k % NTR)], [("sa", k % 2)])
            TTOP("dve", actg[gi % 2][:, ii, c], sa[k % 2], tb16[k % NTR], ALU.mult,
                 [("sa", k % 2), ("tb16", k % NTR)], [("actg", gi % 2, ii, tt)])

        dma_pair(0)
        cast_pair(0)
        k = 0
        pend_d = []
        pend_down = []
        pend_add = []
        for gi, grp in enumerate(groups):
            niter = len(grp) * NT
            it = 0
            for ii, i in enumerate(grp):
                if i + 1 < NPAIR:
                    dma_pair(i + 1)
                wus = i % NUP
                for tt in range(NT):
                    c = slice(tt * TT, (tt + 1) * TT)
                    banks = [(k % 3) * 2, (k % 3) * 2 + 1]
                    while pend_add:
                        down_add(*pend_add.pop(0))
                    for ab in range(2):
                        b = banks[ab]
                        for kc in range(KC):
                            MM(ps[b][:, :], wu[wus][:, kc, ab * 128:(ab + 1) * 128], hT[:, kc, c], kc == 0, kc == KC - 1,
                               [("wu", wus), ("hT", kc, tt)], [("ps", b)])
                    for ab in range(2):
                        b = banks[ab]
                        col = ab * NPAIR + i
                        if tt + 1 < NT:
                            COPY("act", hal[ab][tt % 2][:, 0:2], ps[b][:, TT - 2:TT], [("ps", b)], [("hal", ab, tt % 2)])
                        ACT(tr[ab][k % NTR], ps[b][:, :], AF.Identity, [("ps", b), "pcol"], [("tr", ab, k % NTR)],
                            bias=pc("cb%d" % l, col), scale=pc("cw2_%d" % l, col))
                    for step in range(4):
                        for ab in range(2):
                            b = banks[ab]
                            col = ab * NPAIR + i
                            t = tr[ab][k % NTR]
                            tk = ("tr", ab, k % NTR)
                            if step == 0:
                                STT(t[:, 1:TT], ps[b][:, 0:TT - 1], pc("cw1_%d" % l, col), t[:, 1:TT], ALU.mult, ALU.add,
                                    [("ps", b), tk, "pcol"], [tk])
                            elif step == 1:
                                dst, dk = (t, tk) if ab == 0 else (tb16[k % NTR], ("tb16", k % NTR, "m"))
                                STT(dst[:, 2:TT], ps[b][:, 0:TT - 2], pc("cw0_%d" % l, col), t[:, 2:TT], ALU.mult, ALU.add,
                                    [("ps", b), tk, "pcol"], [dk])
                            elif tt > 0:
                                hp = hal[ab][(tt - 1) % 2]
                                hk = ("hal", ab, (tt - 1) % 2)
                                if step == 2:
                                    STT(t[:, 0:1], hp[:, 1:2], pc("cw1_%d" % l, col), t[:, 0:1], ALU.mult, ALU.add, [hk, tk, "pcol"], [tk])
                                else:
                                    dst, dk = (t, tk) if ab == 0 else (tb16[k % NTR], ("tb16", k % NTR, "h"))
                                    STT(dst[:, 0:2], hp[:, 0:2], pc("cw0_%d" % l, col), t[:, 0:2], ALU.mult, ALU.add, [hk, tk, "pcol"], [dk])
                            elif step == 3 and ab == 1:
                                COPY("dve", tb16[k % NTR][:, 0:2], t[:, 0:2], [tk], [("tb16", k % NTR, "h")])
                    if tt == min(1, NT - 1) and i + 1 < NPAIR:
                        cast_pair(i + 1)
                    pend_d.append((k, gi, ii, tt))
                    if len(pend_d) > 2:
                        stage_d(*pend_d.pop(0))
                    if pend_down:
                        n_left = niter - it
                        take = -(-len(pend_down) // max(1, n_left))
                        for _ in range(take):
                            if len(pend_add) >= 2:
                                down_add(*pend_add.pop(0))
                            pend_add.append(down_mm(*pend_down.pop(0)))
                    k += 1
                    it += 1
            while pend_down:
                if len(pend_add) >= 2:
                    down_add(*pend_add.pop(0))
                pend_add.append(down_mm(*pend_down.pop(0)))
            pend_down = [(gi, d, tt) for tt in range(NT) for d in range(KC)]
        while pend_d:
            stage_d(*pend_d.pop(0))
        while pend_down:
            if len(pend_add) >= 2:
                down_add(*pend_add.pop(0))
            pend_add.append(down_mm(*pend_down.pop(0)))
        while pend_add:
            down_add(*pend_add.pop(0))

    def hgrn():
        o = 0
        e3 = fsb("lb_e", [128, 18], F32, o); o += 128
        ssum = fsb("lb_s", [128, 6], F32, o); o += 64
        srec = fsb("lb_r", [128, 6], F32, o); o += 64
        o0, _ = pidx["lbl0"]
        ACT(e3[:, :], pcol[:, o0:o0 + 18], AF.Exp, ["pcol"], ["lb_e"])
        TTOP("dve", ssum[:, :], e3[:, 0:6], e3[:, 6:12], ALU.add, ["lb_e"], ["lb_s"])
        TTOP("dve", ssum[:, :], ssum[:, :], e3[:, 12:18], ALU.add, ["lb_e", "lb_s"], ["lb_s"])
        A("dve", lambda h: h.reciprocal(out=srec[:, :], in_=ssum[:, :]), ["lb_s"], ["lb_r"])
        TTOP("dve", lbc[:, 0:6], e3[:, 0:6], srec[:, :], ALU.mult, ["lb_e", "lb_r"], [("lbc", 0)])
        TS("dve", lbc[:, 6:12], lbc[:, 0:6], -1.0, 1.0, ALU.mult, ALU.add, [("lbc", 0)], [("lbc", 1)])
        o = 256
        wh2 = [fsb("wh%d" % i, [128, KC, 512], BF16, o + i * KC * 512 * 2) for i in range(2)]; o += 2 * KC * 512 * 2
        stg = [fsb("hstg%d" % i, [128, 2048], F32, o + i * 8192) for i in range(NSTG)]; o += NSTG * 8192
        QT = fsb("QT", [128, S], BF16, o); o += S * 2
        KT = fsb("KT", [128, S], BF16, o); o += S * 2
        Ktok = fsb("Ktok", [128, NB, 128], BF16, o); o += S * 2
        vtok = fsb("vtok", [128, NB, 128], BF16, o); o += S * 2
        gT = fsb("gT", [128, S], BF16, o); o += S * 2
        Bl = fsb("Bl", [128, NCH], F32, o); o += NCH * 4
        mscan = fsb("mscan", [128, TT], F32, o); o += TT * 4
        tA = [fsb("tA%d" % i, [128, TT], F32, o + i * TT * 4) for i in range(6)]; o += 6 * TT * 4
        fR = fsb("fR", [128, TT], F32, o); o += TT * 4
        fP = fsb("fP", [128, TT], F32, o); o += TT * 4
        Tst = [fsb("Tst%d" % i, [128, 128], F32, o + i * 512) for i in range(2)]; o += 1024
        stb = [fsb("stb%d" % i, [128, 128], BF16, o + i * 256) for i in range(3)]; o += 768
        aT = [fsb("aT%d" % i, [128, 64], BF16, o + i * 128) for i in range(2)]; o += 256
        sqo = fsb("sqo", [128, TT], BF16, o); o += TT * 2
        assert o <= FSZ, (o, FSZ)
        MEMSET(mscan[:, :], 1.0, ["mscan"])
        MEMSET(mscan[:, :].rearrange("p (n c) -> p n c", c=64)[:, :, 0:1], 0.0, ["mscan"])
        qs, sg, fg, lf, bb, eb = tA
        prr = [0]
        rcr = [0]

        def pbank():
            b = prr[0]
            prr[0] = (prr[0] + 1) % 3
            return b

        def rbank():
            b = (3, 6, 7)[rcr[0]]
            rcr[0] = (rcr[0] + 1) % 3
            return b

        def load_head(hh):
            load_w(w0_d[hh], 4096, wh2[hh % 2][:, :, :].rearrange("p k n -> p (k n)"), ("wh", hh % 2), stg)

        def pre_stages(hh, tt):
            wh = wh2[hh % 2]
            wk = ("wh", hh % 2)
            c = slice(tt * TT, (tt + 1) * TT)
            st = {}

            def P1():
                st["bf"] = pbank()
                for kc in range(KC):
                    MM(ps[st["bf"]][:, :], wh[:, kc, 128:256], hT[:, kc, c], kc == 0, kc == KC - 1, [wk, ("hT", kc, tt)], [("ps", st["bf"])])
                st["bq"] = pbank()
                for kc in range(KC):
                    MM(ps[st["bq"]][:, :], wh[:, kc, 0:128], hT[:, kc, c], kc == 0, kc == KC - 1, [wk, ("hT", kc, tt)], [("ps", st["bq"])])

            def P2():
                ACT(sg, ps[st["bf"]][:, :], AF.Sigmoid, [("ps", st["bf"])], ["sg"])
                st["bg"] = pbank()
                for kc in range(KC):
                    MM(ps[st["bg"]][:, :], wh[:, kc, 256:384], hT[:, kc, c], kc == 0, kc == KC - 1, [wk, ("hT", kc, tt)], [("ps", st["bg"])])

            def P3():
                TS("dve", fg, sg, lbc[:, 6 + hh:7 + hh], lbc[:, hh:hh + 1], ALU.mult, ALU.add, ["sg", "lbc"], ["fg"])

            def P4():
                ACT(lf, fg, AF.Ln, ["fg"], ["lf"])
                st["bv"] = pbank()
                for j in range(4):
                    tb = tt * 4 + j
                    for kc in range(KC):
                        MM(ps[st["bv"]][:, j * 128:(j + 1) * 128], hT[:, kc, tb * 128:(tb + 1) * 128], wh[:, kc, 384:512], kc == 0, kc == KC - 1,
                           [wk, ("hT", kc, tt)], [("ps", st["bv"])])

            def P5():
                A("dve", lambda h: h.tensor_tensor_scan(out=bb[:, :], data0=mscan[:, :], data1=lf[:, :], initial=0.0, op0=ALU.mult, op1=ALU.add),
                  ["mscan", "lf"], ["bb"])
                ACT(qs, ps[st["bq"]][:, :], AF.Silu, [("ps", st["bq"])], ["qs"])
                ACT(gT[:, c], ps[st["bg"]][:, :], AF.Silu, [("ps", st["bg"])], [("gT", tt)])

            def P6():
                ACT(eb, bb, AF.Exp, ["bb"], ["eb"])
                ACT(sg, bb, AF.Exp, ["bb"], ["sg"], scale=-1.0)
                COPY("act", vtok[:, tt * 4:(tt + 1) * 4, :].rearrange("p j n -> p (j n)"), ps[st["bv"]][:, :], [("ps", st["bv"])], [("vtok", tt)])

            def P7():
                STT(QT[:, c], qs, 128.0 ** -0.5, eb, ALU.mult, ALU.mult, ["qs", "eb"], [("QT", tt)])
                COPY("dve", Bl[:, tt * 8:(tt + 1) * 8], eb[:, :].rearrange("p (n c) -> p n c", c=64)[:, :, 63], ["eb"], [("Bl", tt)])
                TS("dve", lf, fg, -1.0, 1.0, ALU.mult, ALU.add, ["fg"], ["lf"])
                TTOP("dve", KT[:, c], lf, sg, ALU.mult, ["lf", "sg"], [("KT", tt)])

            def P8():
                b = pbank()
                pb16 = ps[b][:, :].bitcast(BF16)
                for j in range(4):
                    tb = tt * 4 + j
                    A("pe", lambda h, j=j, tb=tb, pb16=pb16: h.transpose(out=pb16[:, j * 128:(j + 1) * 128], in_=KT[:, tb * 128:(tb + 1) * 128],
                                                                           identity=ident_bf[:, :]),
                      [("KT", tt), "ident"], [("ps", b)])
                COPY("act", Ktok[:, tt * 4:(tt + 1) * 4, :].rearrange("p j n -> p (j n)"), pb16[:, 0:512], [("ps", b)], [("Ktok", tt)])

            return [P1, P2, P3, P4, P5, P6, P7, P8]

        BO = 4

        def rec_step(hh, n):
            if n < NCH:
                base = (n % 2) * 64
                blkI = n // 2
                tt = n // 8
                cs = slice(n * 64, (n + 1) * 64)
                bA = rbank()
                MM(ps[bA][base:base + 64, 0:64], KT[:, cs], QT[:, cs], True, True, [("KT", tt), ("QT", tt)], [("ps", bA)])
                a = aT[n % 2]
                TTOP("dve", a[base:base + 64, :], ps[bA][base:base + 64, 0:64], tri64[base:base + 64, :], ALU.mult,
                     [("ps", bA), "tri64"], [("aT", n % 2)])
                bU = rbank()
                MM(ps[bU][:, 0:128], Ktok[base:base + 64, blkI, :], vtok[base:base + 64, blkI, :], True, True,
                   [("Ktok", tt), ("vtok", tt)], [("ps", bU)])
                if n == 0:
                    COPY("dve", Tst[0][:, :], ps[bU][:, 0:128], [("ps", bU)], [("Tst", 0)])
                else:
                    STT(Tst[n % 2][:, :], Tst[(n - 1) % 2][:, :], Bl[:, n - 1:n], ps[bU][:, 0:128], ALU.mult, ALU.add,
                        [("Tst", (n - 1) % 2), ("Bl", (n - 1) // 8), ("ps", bU)], [("Tst", n % 2)])
                if n + 1 < NCH:
                    TS("dve", stb[(n + 1) % 3][:, :], Tst[n % 2][:, :], Bl[:, n:n + 1], None, ALU.mult, ALU.bypass,
                       [("Tst", n % 2), ("Bl", n // 8)], [("stb", (n + 1) % 3)])
            if n > 0:
                m = n - 1
                mbase = (m % 2) * 64
                mblk = m // 2
                mtt = m // 8
                mcs = slice(m * 64, (m + 1) * 64)
                bo = BO + (mtt % 2)
                oc = slice((m % 8) * 64, (m % 8 + 1) * 64)
                MM(ps[bo][:, oc], vtok[mbase:mbase + 64, mblk, :], aT[m % 2][mbase:mbase + 64, :], True, m == 0,
                   [("vtok", mtt), ("aT", m % 2)], [("ps", bo, m % 8)])
                if m > 0:
                    MM(ps[bo][:, oc], stb[m % 3][:, :], QT[:, mcs], False, True, [("stb", m % 3), ("QT", mtt)], [("ps", bo, m % 8)])
                if m % 8 == 7:
                    c = slice(mtt * TT, (mtt + 1) * TT)
                    ACT(sqo, ps[bo][:, :], AF.Square, [("ps", bo)], ["sqo"])
                    b2 = rbank()
                    MM(ps[b2][:, :], ones_bf[:, :], sqo, True, True, ["sqo", "ones"], [("ps", b2)])
                    rstd_from_ss(b2, TT, 1.0 / 128, fR, fR, "fR", "fR")
                    STT(fP, ps[bo][:, :], pc("hgo"), fR, ALU.mult, ALU.mult, [("ps", bo), "fR", "pcol"], ["fP"])
                    TTOP("dve", yT[:, hh, c], fP, gT[:, c], ALU.mult, ["fP", ("gT", mtt)], [("yT", hh, mtt)])

        load_head(0)
        tiles = [(hh, tt) for hh in range(6) for tt in range(NT)]
        for f_ in pre_stages(*tiles[0]):
            f_()
        for i, (hh, tt) in enumerate(tiles):
            if tt == 0 and hh + 1 < 6:
                load_head(hh + 1)
            steps = [(hh, n) for n in range(tt * 8, tt * 8 + 8)]
            if tt == NT - 1:
                steps.append((hh, NCH))
            stages = pre_stages(*tiles[i + 1]) if i + 1 < len(tiles) else []
            for j in range(max(len(steps), len(stages))):
                if j < len(steps):
                    rec_step(*steps[j])
                if j < len(stages):
                    stages[j]()

    def mla_layer():
        mixer_norm(1, 65536 if FSZ >= 65536 + 6144 else 0)
        if FSZ < 65536 + 6144:
            sch.barrier()
        mem_prep(1)
        sch.barrier()
        o = 0
        cqn = fsb("cqn", [128, 3, S], BF16, o); o += 3 * S * 2
        ckvn = fsb("ckvn", [128, 2, S], BF16, o); o += 2 * S * 2
        kpe = fsb("kpe", [128, S], BF16, o); o += S * 2
        kpes = fsb("kpes", [128, S], BF16, o); o += S * 2
        cosT = fsb("cosT", [128, S], BF16, o); o += S * 2
        sinT = fsb("sinT", [128, S], BF16, o); o += S * 2
        qw1 = fsb("qw1", [128, KC, 256], BF16, o); o += KC * 256 * 2
        P0 = o
        pi32 = fsb("pi32", [128, TT], I32, o); o += TT * 4
        ang = fsb("ang", [128, TT], F32, o); o += TT * 4
        kf = fsb("kf", [128, TT], F32, o); o += TT * 4
        rr_ = fsb("rr_", [128, TT], F32, o); o += TT * 4
        msk = fsb("msk", [128, TT], F32, o); o += TT * 4
        TWO_PI = 2.0 * np.pi
        C1 = 6.28125
        C2 = TWO_PI - C1
        for tt in range(NT):
            c = slice(tt * TT, (tt + 1) * TT)
            DMA(pi32[:, :], pos_d[:, c].partition_broadcast(128), (), ["pi32"])
            for which in range(2):
                COPY("dve", ang, pi32, ["pi32"], ["ang"])
                if which == 0:
                    TS("dve", ang, ang, pc("invf"), pc("halfpi"), ALU.mult, ALU.add, ["ang", "pcol"], ["ang"])
                else:
                    TS("dve", ang, ang, pc("invf"), None, ALU.mult, ALU.bypass, ["ang", "pcol"], ["ang"])
                TS("dve", kf, ang, 1.0 / TWO_PI, None, ALU.mult, ALU.bypass, ["ang"], ["kf"])
                COPY("dve", pi32, kf, ["kf"], ["pi32"])
                COPY("dve", kf, pi32, ["pi32"], ["kf"])
                STT(rr_, kf, -C1, ang, ALU.mult, ALU.add, ["kf", "ang"], ["rr_"])
                STT(rr_, kf, -C2, rr_, ALU.mult, ALU.add, ["kf", "rr_"], ["rr_"])
                TS("dve", msk, rr_, float(np.pi), -TWO_PI, ALU.is_gt, ALU.mult, ["rr_"], ["msk"])
                TTOP("dve", rr_, rr_, msk, ALU.add, ["rr_", "msk"], ["rr_"])
                TS("dve", msk, rr_, float(-np.pi), TWO_PI, ALU.is_lt, ALU.mult, ["rr_"], ["msk"])
                TTOP("dve", rr_, rr_, msk, ALU.add, ["rr_", "msk"], ["rr_"])
                TS("dve", rr_, rr_, float(np.pi), float(-np.pi), ALU.min, ALU.max, ["rr_"], ["rr_"])
                if which == 0:
                    ACT(cosT[:, c], rr_, AF.Sin, ["rr_"], [("cosT", tt)])
                else:
                    ACT(kf, rr_, AF.Sin, ["rr_"], ["kf"])
                    TS("dve", sinT[:, c], kf, pc("sgn"), None, ALU.mult, ALU.bypass, ["kf", "pcol"], [("sinT", tt)])
                if which == 0:
                    DMA(pi32[:, :], pos_d[:, c].partition_broadcast(128), (), ["pi32"])
        wg = fsb("w1g", [128, 3, KC, 128], BF16, o); o += 3 * KC * 128 * 2
        stg = [fsb("bstg%d" % i, [128, 1024], F32, o + i * 4096) for i in range(NSTG)]; o += NSTG * 4096
        sq = [fsb("bsq%d" % i, [128, TT], BF16, o + i * TT * 2) for i in range(2)]; o += 2 * TT * 2
        tmp = fsb("btmp", [128, TT], F32, o); o += TT * 4
        rstd = fsb("brstd", [128, TT], F32, o); o += TT * 4
        assert o <= FSZ, (o, FSZ)
        groups = [([0, 1, 2], cqn, "cqn", "qag", 384), ([3, 4], ckvn, "ckvn", "kvag", 256)]
        for tiles, dst, dkey, gname, width in groups:
            for gi, t in enumerate(tiles):
                load_w(w1in_d[t], 1024, wg[:, gi, :, :].rearrange("p k n -> p (k n)"), ("w1g", gi), stg)
            for tt in range(NT):
                c = slice(tt * TT, (tt + 1) * TT)
                bs = 6 + tt % 2
                for gi, t in enumerate(tiles):
                    b = (tt % 2) * 3 + gi
                    for kc in range(KC):
                        MM(ps[b][:, :], wg[:, gi, kc, :], hT[:, kc, c], kc == 0, kc == KC - 1, [("w1g", gi), ("hT", kc, tt)], [("ps", b)])
                    ACT(sq[gi % 2], ps[b][:, :], AF.Square, [("ps", b)], [("bsq", gi % 2)])
                    MM(ps[bs][:, :], ones_bf[:, :], sq[gi % 2], gi == 0, gi == len(tiles) - 1, [("bsq", gi % 2), "ones"], [("ps", bs)])
                rstd_from_ss(bs, TT, 1.0 / width, tmp, rstd, "btmp", "brstd")
                for gi, t in enumerate(tiles):
                    STT(dst[:, gi, c], ps[(tt % 2) * 3 + gi][:, :], pc(gname, gi), rstd, ALU.mult, ALU.mult,
                        [("ps", (tt % 2) * 3 + gi), "brstd", "pcol"], [(dkey, gi, tt)])
        for t, dst, dkey in ((5, kpe, "kpe"), (6, kpes, "kpes")):
            load_w(w1in_d[t], 1024, wg[:, 0, :, :].rearrange("p k n -> p (k n)"), ("w1g", 0), stg)
            for tt in range(NT):
                c = slice(tt * TT, (tt + 1) * TT)
                b = bank()
                for kc in range(KC):
                    MM(ps[b][:, :], wg[:, 0, kc, :], hT[:, kc, c], kc == 0, kc == KC - 1, [("w1g", 0), ("hT", kc, tt)], [("ps", b)])
                COPY("act", dst[:, c], ps[b][:, :], [("ps", b)], [(dkey, tt)])
        for j, t in enumerate((7, 8)):
            load_w(w1in_d[t], 1024, wg[:, 1, :, :].rearrange("p k n -> p (k n)"), ("w1g", 1), stg)
            for kc in range(KC):
                COPY("pool", qw1[:, kc, j * 128:(j + 1) * 128], wg[:, 1, kc, :], [("w1g", 1)], [("qw1", j)])
        H_ = slice(0, 64)
        for tt in range(NT):
            c = slice(tt * TT, (tt + 1) * TT)
            STT(tmp[H_, :], kpe[H_, c], pc("kg_r", 0, H_), cosT[H_, c], ALU.mult, ALU.mult, [("kpe", tt), ("cosT", tt), "pcol"], ["btmp"])
            STT(rstd[H_, :], kpes[H_, c], pc("kg_rs", 0, H_), sinT[H_, c], ALU.mult, ALU.mult, [("kpes", tt), ("sinT", tt), "pcol"], ["brstd"])
            ACT(kpes[H_, c], kpe[H_, c], AF.Square, [("kpe", tt)], [("kpes", tt)])
            TTOP("dve", kpe[H_, c], tmp[H_, :], rstd[H_, :], ALU.add, ["btmp", "brstd"], [("kpe", tt)])
        sch.barrier()
        mem_attn(1, qw1, "qw1", P0)
        sch.barrier()
        HB = BASE + KC * S * 4
        ho = [HB]

        def hsb(name, shape, dt):
            nb = int(np.prod(shape[1:])) * (4 if dt == F32 else 2)
            t = nc.alloc_sbuf_tensor_at(name, list(shape), dt, offset=ho[0])
            ho[0] += (nb + 63) // 64 * 64
            assert ho[0] <= HB + KC * S * 2, (name, ho[0] - HB)
            return t
        Kn = hsb("Kn", [128, S], BF16)
        Kr = hsb("Kr", [128, S], BF16)
        Vh = hsb("Vh", [128, NB, 128], BF16)
        Qn = hsb("Qn", [128, S], BF16)
        Qr = hsb("Qr", [128, S], BF16)
        wq2 = [hsb("wq%d" % i, [128, 3, 256], BF16) for i in range(2)]
        wkv2 = [hsb("wkv%d" % i, [128, 2, 256], BF16) for i in range(2)]
        o = P0
        pt = [fsb("apt%d" % i, [128, TT], BF16, o + i * TT * 2) for i in range(4)]; o += 4 * TT * 2
        stg = [fsb("cstg%d" % i, [128, 2048], F32, o + i * 8192) for i in range(NSTG)]; o += NSTG * 8192
        sq = [fsb("csq%d" % i, [128, TT], BF16, o + i * TT * 2) for i in range(2)]; o += 2 * TT * 2
        tmp = fsb("ctmp", [128, TT], F32, o); o += TT * 4
        rstd = fsb("crstd", [128, TT], F32, o); o += TT * 4
        r1 = fsb("cr1", [128, TT], F32, o); o += TT * 4
        r2 = fsb("cr2", [128, TT], F32, o); o += TT * 4
        rec = fsb("crec", [128, TT], F32, o); o += TT * 4
        assert o <= FSZ, (o, FSZ)
        H = slice(0, 64)
        SC = 192.0 ** -0.5
        pti = [0]

        def rope_part(raw, rawkeys, g_r, g_rs, dst, dkey, tt):
            c = slice(tt * TT, (tt + 1) * TT)
            STT(r1[H, :], raw[0], pc(g_r, 0, H), rstd[H, :], ALU.mult, ALU.mult, rawkeys + ["crstd", "pcol"], ["cr1"])
            STT(r2[H, :], raw[1], pc(g_rs, 0, H), rstd[H, :], ALU.mult, ALU.mult, rawkeys + ["crstd", "pcol"], ["cr2"])
            TTOP("dve", r1[H, :], r1[H, :], cosT[H, c], ALU.mult, ["cr1", ("cosT", tt)], ["cr1"])
            TTOP("dve", r2[H, :], r2[H, :], sinT[H, c], ALU.mult, ["cr2", ("sinT", tt)], ["cr2"])
            TTOP("dve", dst[H, c], r1[H, :], r2[H, :], ALU.add, ["cr1", "cr2"], [(dkey, tt)])

        def load_mla_head(hh):
            load_w(wuq_d[hh], 768, wq2[hh % 2][:, :, :].rearrange("p k n -> p (k n)"), ("wq", hh % 2), stg)
            load_w(wukv_d[hh], 512, wkv2[hh % 2][:, :, :].rearrange("p k n -> p (k n)"), ("wkv", hh % 2), stg)

        load_mla_head(0)
        for hh in range(6):
            wq, wkv = wq2[hh % 2], wkv2[hh % 2]
            wqk, wkvk = ("wq", hh % 2), ("wkv", hh % 2)
            if hh + 1 < 6:
                load_mla_head(hh + 1)
            for tt in range(NT):
                c = slice(tt * TT, (tt + 1) * TT)
                bK = bank8()
                for kc in range(2):
                    MM(ps[bK][:, :], wkv[:, kc, 0:128], ckvn[:, kc, c], kc == 0, kc == 1, [wkvk, ("ckvn", kc, tt)], [("ps", bK)])
                ACT(sq[0], ps[bK][:, :], AF.Square, [("ps", bK)], [("csq", 0)])
                bs = bank8()
                MM(ps[bs][:, :], ones_bf[:, :], sq[0], True, False, [("csq", 0), "ones"], [("ps", bs)])
                MM(ps[bs][:, :], ones_bf[H, :], kpes[H, c], False, True, [("kpes", tt), "ones"], [("ps", bs)])
                rstd_from_ss(bs, TT, 1.0 / 192, tmp, rstd, "ctmp", "crstd")
                STT(Kn[:, c], ps[bK][:, :], pc("kg_n"), rstd, ALU.mult, ALU.mult, [("ps", bK), "crstd", "pcol"], [("Kn", tt)])
                TTOP("dve", Kr[H, c], kpe[H, c], rstd[H, :], ALU.mult, [("kpe", tt), "crstd"], [("Kr", tt)])
                bV = bank8()
                for j in range(4):
                    tb = tt * 4 + j
                    for kc in range(2):
                        MM(ps[bV][:, j * 128:(j + 1) * 128], ckvn[:, kc, tb * 128:(tb + 1) * 128], wkv[:, kc, 128:256], kc == 0, kc == 1,
                           [wkvk, ("ckvn", kc, tt)], [("ps", bV)])
                COPY("act", Vh[:, tt * 4:(tt + 1) * 4, :].rearrange("p j n -> p (j n)"), ps[bV][:, :], [("ps", bV)], [("Vh", tt)])
                bQ = bank8()
                for kc in range(3):
                    MM(ps[bQ][:, :], wq[:, kc, 0:128], cqn[:, kc, c], kc == 0, kc == 2, [wqk, ("cqn", kc, tt)], [("ps", bQ)])
                bR = bank8()
                for kc in range(3):
                    MM(ps[bR][H, :], wq[:, kc, 128:192], cqn[:, kc, c], kc == 0, kc == 2, [wqk, ("cqn", kc, tt)], [("ps", bR)])
                bS2 = bank8()
                for kc in range(3):
                    MM(ps[bS2][H, :], wq[:, kc, 192:256], cqn[:, kc, c], kc == 0, kc == 2, [wqk, ("cqn", kc, tt)], [("ps", bS2)])
                ACT(sq[0], ps[bQ][:, :], AF.Square, [("ps", bQ)], [("csq", 0)])
                ACT(sq[1][H, :], ps[bR][H, :], AF.Square, [("ps", bR)], [("csq", 1)])
                bs = bank8()
                MM(ps[bs][:, :], ones_bf[:, :], sq[0], True, False, [("csq", 0), "ones"], [("ps", bs)])
                MM(ps[bs][:, :], ones_bf[H, :], sq[1][H, :], False, True, [("csq", 1), "ones"], [("ps", bs)])
                rstd_from_ss(bs, TT, 1.0 / 192, tmp, rstd, "ctmp", "crstd")
                STT(Qn[:, c], ps[bQ][:, :], pc("qg_n"), rstd, ALU.mult, ALU.mult, [("ps", bQ), "crstd", "pcol"], [("Qn", tt)])
                rope_part((ps[bR][H, :], ps[bS2][H, :]), [("ps", bR), ("ps", bS2)], "qg_r", "qg_rs", Qr, "Qr", tt)
            for qt in range(NT):
                c0 = qt * TT
                nkt = 4 * (qt + 1)
                par = (hh * NT + qt) % 2
                bo, bd = 4 + par, 6 + par
                pend_pv = []

                def pv(kt, pi, q0, ktt, bo=bo, bd=bd, nkt=nkt):
                    MM(ps[bo][:, q0:TT], Vh[:, kt, :], pt[pi][:, q0:TT], kt == 0, kt == nkt - 1, [("Vh", ktt), ("apt", pi)], [("ps", bo)])
                    MM(ps[bd][:, q0:TT], ones_bf[:, :], pt[pi][:, q0:TT], kt == 0, kt == nkt - 1, ["ones", ("apt", pi)], [("ps", bd)])

                for kt in range(nkt):
                    j = kt - 4 * qt
                    q0 = max(0, j) * 128
                    ktt = kt // 4
                    bSc = bank()
                    MM(ps[bSc][:, q0:TT], Kn[:, kt * 128:(kt + 1) * 128], Qn[:, c0 + q0:c0 + TT], True, False,
                       [("Kn", ktt), ("Qn", qt)], [("ps", bSc)])
                    MM(ps[bSc][:, q0:TT], Kr[H, kt * 128:(kt + 1) * 128], Qr[H, c0 + q0:c0 + TT], False, True,
                       [("Kr", ktt), ("Qr", qt)], [("ps", bSc)])
                    pi = pti[0] % 4
                    pti[0] += 1
                    ACT(pt[pi][:, q0:TT], ps[bSc][:, q0:TT], AF.Exp, [("ps", bSc)], [("apt", pi)], scale=SC)
                    if j >= 0:
                        A("pool", lambda h, pi=pi, q0=q0: h.affine_select(out=pt[pi][:, q0:q0 + 128], in_=pt[pi][:, q0:q0 + 128], pattern=[[1, 128]],
                                                                           compare_op=ALU.is_ge, fill=0.0, base=0, channel_multiplier=-1),
                          [("apt", pi)], [("apt", pi)])
                    pend_pv.append((kt, pi, q0, ktt))
                    if len(pend_pv) > 2:
                        pv(*pend_pv.pop(0))
                while pend_pv:
                    pv(*pend_pv.pop(0))
                A("dve", lambda h, bd=bd: h.reciprocal(out=rec[:, :], in_=ps[bd][:, :]), [("ps", bd)], ["crec"])
                TTOP("dve", yT[:, hh, c0:c0 + TT], ps[bo][:, :], rec[:, :], ALU.mult, [("ps", bo), "crec"], [("yT", hh, qt)])
        sch.barrier()
        out_proj(1, 0, ffn_norm_cb(1, KC * D * 2 + NSTG * 8192))
        sch.barrier()
        ffn(1)
        sch.barrier()

    def mixer_norm(l, o=0):
        sq = [fsb("nsq%d_%d" % (l, i), [128, TT], BF16, o + i * TT * 2) for i in range(2)]; o += 2 * TT * 2
        tmp = fsb("ntmp%d" % l, [128, TT], F32, o); o += TT * 4
        rstd = fsb("nrstd%d" % l, [128, TT], F32, o); o += TT * 4
        norm_full(xT, "xT", "mixg%d" % l, hT, "hT", sq, tmp, rstd)

    mem_prep(0)
    for tt in range(NT):
        for kc in range(KC):
            DMA(xT[:, kc, tt * TT:(tt + 1) * TT], xT_d[kc * 128:(kc + 1) * 128, tt * TT:(tt + 1) * TT], (), [("xT", kc, tt)])
    mixer_norm(0, 65536 if FSZ >= 65536 + 6144 else 0)
    sch.barrier()
    hgrn()
    sch.barrier()
    qw = fsb("qw0", [128, KC, 256], BF16, 0)
    _st = fsb("qstg0", [128, 2048], F32, 4096)
    stg0 = [_st] * NSTG
    load_w(w0m_d, 2048, qw[:, :, :].rearrange("p k n -> p (k n)"), "qw", stg0)
    o_end = mem_attn(0, qw, "qw", 4096 + 8192)
    if o_end + KC * D * 2 + NSTG * 8192 + 6144 <= FSZ:
        out_proj(0, o_end, ffn_norm_cb(0, o_end + KC * D * 2 + NSTG * 8192))
    else:
        sch.barrier()
        out_proj(0, 0, ffn_norm_cb(0, KC * D * 2 + NSTG * 8192))
    sch.barrier()
    ffn(0)
    sch.barrier()
    if nlayers > 1:
        mla_layer()
    outs = []
    for name, (t, key, shape, dt) in DBG.items():
        outs.append(dump(name, t, key, shape, dt))
    for kc in range(KC):
        outs.append(DMA(out_d[kc * 128:(kc + 1) * 128, :], xT[:, kc, :], [("xT", kc)], [("outd", kc)]))
    sch.emit(nc, final_wait_ops=outs)
    return nc


def make_in_maps(inputs, S, ncores):
    shared, pidx = pack_shared(inputs)
    x = np.asarray(inputs["x"], np.float32)
    mem = np.asarray(inputs["mem"], np.float32)
    pos = np.asarray(inputs["positions"]).astype(np.int32)
    in_maps = []
    for b in range(ncores):
        m = dict(shared)
        m["xT"] = np.ascontiguousarray(x[b].T)
        m["memT"] = np.ascontiguousarray(mem[b].T)
        m["pos"] = np.ascontiguousarray(pos[b][None, :])
        in_maps.append(m)
    return in_maps, pidx, shared["pcol"].shape[1]


def kernel(**inputs):
    x = np.asarray(inputs["x"])
    B, S, _ = x.shape
    in_maps, pidx, ncol = make_in_maps(inputs, S, B)
    nc = build(S, pidx, ncol)
    res = run_bass_kernel_spmd(nc, in_maps, core_ids=list(range(B)))
    out = np.stack([np.ascontiguousarray(res.results[b]["outT"].T) for b in range(B)])
    return out.astype(np.float32)
```
